# Optimizing a Trainium2 kernel written in Bass

```python
import math
import jax, jax.numpy as jnp
from jax import lax
import numpy as np

D_MODEL = 4096
BATCH = 4
SEQ = 4096
DEPTH = 1

NORM_EPS = 1e-6
GLA_HEADS = 16
GLA_DK = 64
GLA_DV = 128
GLA_KEY_W = GLA_HEADS * GLA_DK
GLA_VAL_W = GLA_HEADS * GLA_DV
GLA_GATE_RANK = 16
GLA_TAU = 16.0
GLA_CHUNK = 64
NSA_HEADS = 16
NSA_KV_HEADS = 4
NSA_GROUP = NSA_HEADS // NSA_KV_HEADS
NSA_DIM = 128
NSA_Q_W = NSA_HEADS * NSA_DIM
NSA_KV_W = NSA_KV_HEADS * NSA_DIM
CMP_BLOCK = 32
CMP_STRIDE = 16
PHI_HIDDEN = 128
SEL_BLOCK = 64
SEL_TOPK = 16
SEL_LOCAL = 2
SEL_QUERY_BLOCK = 32
WINDOW = 512
WIN_QUERY_BLOCK = 128
REL_BUCKETS = 32
REL_MAX_DIST = 128
D_FF = 4 * D_MODEL

NEG_INF = -1e30
FORCE_SCORE = 1e4

IN_SPLITS = (GLA_KEY_W, GLA_KEY_W, GLA_VAL_W, GLA_GATE_RANK, GLA_VAL_W,
             NSA_Q_W, NSA_KV_W, NSA_KV_W, NSA_KV_W, NSA_KV_W, NSA_KV_W, NSA_KV_W, 3 * NSA_HEADS,
             D_MODEL, D_MODEL)
D_IN_PROJ = sum(IN_SPLITS)
SPLIT_POINTS = [int(v) for v in np.cumsum(IN_SPLITS)[:-1]]

kernel_name = 'hybrid_gla_nsa_gated_block'


def _rmsnorm(x, g):
    xf = x.astype(jnp.float32)
    y = xf * lax.rsqrt(jnp.mean(xf * xf, axis=-1, keepdims=True) + NORM_EPS)
    return (y * g.astype(jnp.float32)).astype(x.dtype)


def _rel_bucket(dist):
    n = jnp.maximum(dist, 0)
    max_exact = REL_BUCKETS // 2
    nf = jnp.maximum(n, max_exact).astype(jnp.float32)
    large = max_exact + (jnp.log(nf / max_exact) / math.log(REL_MAX_DIST / max_exact)
                         * (REL_BUCKETS - max_exact)).astype(jnp.int32)
    large = jnp.minimum(large, REL_BUCKETS - 1)
    return jnp.where(n < max_exact, n, large)


def _masked_softmax(logits, valid):
    p = jax.nn.softmax(jnp.where(valid, logits, NEG_INF), axis=-1)
    return jnp.where(valid, p, 0.0)


def _cmp_to_sel_matrix(n_c, n_s):
    m_mat = np.zeros((n_c, n_s), np.float32)
    j = np.arange(n_s)
    for m in range(SEL_BLOCK // CMP_STRIDE):
        for n in range(CMP_BLOCK // CMP_STRIDE):
            c = (SEL_BLOCK // CMP_STRIDE) * j + m - n
            ok = (c >= 0) & (c < n_c)
            np.add.at(m_mat, (c[ok], j[ok]), 1.0)
    return jnp.asarray(m_mat)


def _gla_mixer(q, k, v, a_lr, r, w_alpha2, b_alpha, norm_g):
    B, S, _ = q.shape
    N, C, H = S // GLA_CHUNK, GLA_CHUNK, GLA_HEADS

    def chunks(t, d):
        return t.reshape(B, N, C, H, d).transpose(0, 3, 1, 2, 4)

    log_a = jax.nn.log_sigmoid((a_lr @ w_alpha2 + b_alpha).astype(jnp.float32)) / GLA_TAU
    bcum = jnp.cumsum(chunks(log_a, GLA_DK), axis=3)
    qc = chunks(q, GLA_DK).astype(jnp.float32) * (GLA_DK ** -0.5)
    kc = chunks(k, GLA_DK).astype(jnp.float32)
    vc = chunks(v, GLA_DV).astype(jnp.float32)
    qe = qc * jnp.exp(bcum)
    ke = kc * jnp.exp(-bcum)
    kd = kc * jnp.exp(bcum[:, :, :, -1:, :] - bcum)
    causal = jnp.asarray(np.tril(np.ones((C, C), dtype=bool)))
    attn = jnp.where(causal, jnp.einsum('bhncd,bhnsd->bhncs', qe, ke), 0.0)
    o_intra = jnp.einsum('bhncs,bhnse->bhnce', attn, vc)
    kv = jnp.einsum('bhnsd,bhnse->bhnde', kd, vc)
    decay = jnp.exp(bcum[:, :, :, -1, :])

    def step(state, inp):
        dec, kv_n = inp
        return dec[..., None] * state + kv_n, state

    init = jnp.zeros((B, H, GLA_DK, GLA_DV), jnp.float32)
    _, prev = lax.scan(step, init, (jnp.moveaxis(decay, 2, 0), jnp.moveaxis(kv, 2, 0)))
    o_inter = jnp.einsum('bhncd,nbhde->bhnce', qe, prev)
    o = (o_intra + o_inter).transpose(0, 2, 3, 1, 4).reshape(B, S, H, GLA_DV)
    o = _rmsnorm(o, norm_g)
    return (o.reshape(B, S, GLA_VAL_W) * jax.nn.silu(r.astype(jnp.float32))).astype(q.dtype)


def _nsa_mixer(q, k_cmp, v_cmp, k_sel, v_sel, k_win, v_win, gate_logits,
               pos_k, pos_v, phi_k_w1, phi_k_w2, phi_v_w1, phi_v_w2, rel_table):
    B, S, _ = q.shape
    G, R, d = NSA_KV_HEADS, NSA_GROUP, NSA_DIM
    scale = d ** -0.5
    q5 = q.reshape(B, S, G, R, d)
    k_cmp, v_cmp, k_sel, v_sel, k_win, v_win = [t.reshape(B, S, G, d) for t in
                                                 (k_cmp, v_cmp, k_sel, v_sel, k_win, v_win)]
    table_gr = rel_table.astype(jnp.float32).reshape(REL_BUCKETS, G, R)
    pos_t = jnp.arange(S, dtype=jnp.int32)

    n_c = (S - CMP_BLOCK) // CMP_STRIDE + 1
    blk_idx = np.arange(n_c)[:, None] * CMP_STRIDE + np.arange(CMP_BLOCK)[None, :]

    def compress(t, pos, w1, w2):
        blocks = t[:, blk_idx] + pos[None, None, :, None, :]
        flat = blocks.transpose(0, 1, 3, 2, 4).reshape(B, n_c, G, CMP_BLOCK * d)
        return jax.nn.relu(flat @ w1) @ w2

    kc = compress(k_cmp, pos_k, phi_k_w1, phi_k_w2)
    vc = compress(v_cmp, pos_v, phi_v_w1, phi_v_w2)
    cmp_end = jnp.arange(n_c, dtype=jnp.int32) * CMP_STRIDE + (CMP_BLOCK - 1)
    dist_c = pos_t[:, None] - cmp_end[None, :]
    bias_c = table_gr[_rel_bucket(dist_c)].transpose(2, 3, 0, 1)
    logits_c = jnp.einsum('bsgrd,bcgd->bgrsc', q5, kc).astype(jnp.float32) * scale + bias_c
    p_c = _masked_softmax(logits_c, dist_c >= 0)
    o_cmp = jnp.einsum('bgrsc,bcgd->bsgrd', p_c.astype(vc.dtype), vc)

    n_s = S // SEL_BLOCK
    top_n = min(SEL_TOPK, n_s)
    imp = jnp.einsum('bgrsc,cj->bgsj', p_c, _cmp_to_sel_matrix(n_c, n_s))
    blk = jnp.arange(n_s, dtype=jnp.int32)
    cur = pos_t // SEL_BLOCK
    forced = (blk[None, :] == 0) | ((blk[None, :] <= cur[:, None]) &
                                     (blk[None, :] > cur[:, None] - SEL_LOCAL))
    reachable = blk[None, :] * SEL_BLOCK <= pos_t[:, None]
    score = jnp.where(forced, FORCE_SCORE, jnp.where(reachable, imp, -FORCE_SCORE))
    _, sel_idx = lax.top_k(score, top_n)
    ks_blk = k_sel.reshape(B, n_s, SEL_BLOCK, G, d).transpose(0, 3, 1, 2, 4)
    vs_blk = v_sel.reshape(B, n_s, SEL_BLOCK, G, d).transpose(0, 3, 1, 2, 4)
    b_ix = jnp.arange(B)[:, None, None, None]
    g_ix = jnp.arange(G)[None, :, None, None]
    n_key = top_n * SEL_BLOCK
    QS = SEL_QUERY_BLOCK

    def sel_chunk(i):
        q0 = i * QS
        qb = lax.dynamic_slice_in_dim(q5, q0, QS, axis=1)
        idx = lax.dynamic_slice_in_dim(sel_idx, q0, QS, axis=2)
        kg = ks_blk[b_ix, g_ix, idx].reshape(B, G, QS, n_key, d)
        vg = vs_blk[b_ix, g_ix, idx].reshape(B, G, QS, n_key, d)
        kpos = (idx[..., None] * SEL_BLOCK + jnp.arange(SEL_BLOCK, dtype=jnp.int32)).reshape(B, G, QS, n_key)
        dist = (q0 + jnp.arange(QS, dtype=jnp.int32))[None, None, :, None] - kpos
        bias = table_gr[_rel_bucket(dist), g_ix].transpose(0, 1, 4, 2, 3)
        logits = jnp.einsum('bqgrd,bgqkd->bgrqk', qb, kg).astype(jnp.float32) * scale + bias
        p = _masked_softmax(logits, (dist >= 0)[:, :, None])
        return jnp.einsum('bgrqk,bgqkd->bqgrd', p.astype(vg.dtype), vg)

    o_sel = lax.map(sel_chunk, jnp.arange(S // QS, dtype=jnp.int32))
    o_sel = jnp.moveaxis(o_sel, 0, 1).reshape(B, S, G, R, d)

    QB = WIN_QUERY_BLOCK
    span = QB + WINDOW
    pad = ((0, 0), (WINDOW, 0), (0, 0), (0, 0))
    kw_pad = jnp.pad(k_win, pad)
    vw_pad = jnp.pad(v_win, pad)

    def win_block(i):
        s0 = i * QB
        qb = lax.dynamic_slice_in_dim(q5, s0, QB, axis=1)
        kb = lax.dynamic_slice_in_dim(kw_pad, s0, span, axis=1)
        vb = lax.dynamic_slice_in_dim(vw_pad, s0, span, axis=1)
        tq = s0 + jnp.arange(QB, dtype=jnp.int32)
        kpos = s0 - WINDOW + jnp.arange(span, dtype=jnp.int32)
        dist = tq[:, None] - kpos[None, :]
        valid = (dist >= 0) & (dist < WINDOW) & (kpos[None, :] >= 0)
        bias = table_gr[_rel_bucket(dist)].transpose(2, 3, 0, 1)
        logits = jnp.einsum('bqgrd,bkgd->bgrqk', qb, kb).astype(jnp.float32) * scale + bias
        p = _masked_softmax(logits, valid)
        return jnp.einsum('bgrqk,bkgd->bqgrd', p.astype(vb.dtype), vb)

    o_win = lax.map(win_block, jnp.arange(S // QB, dtype=jnp.int32))
    o_win = jnp.moveaxis(o_win, 0, 1).reshape(B, S, G, R, d)

    gates = jax.nn.sigmoid(gate_logits.astype(jnp.float32)).reshape(B, S, G, R, 3)
    o = gates[..., 0:1] * o_cmp + gates[..., 1:2] * o_sel + gates[..., 2:3] * o_win
    return o.reshape(B, S, NSA_Q_W).astype(q.dtype)


def setup_inputs(seed: int = 0) -> dict:
    key = jax.random.key(seed)
    ks = jax.random.split(key, 22)
    f32 = jnp.float32
    L = DEPTH

    def nrm(k, shape, scale):
        return jax.random.normal(k, shape, f32) * scale

    def gain(k, shape):
        return 1.0 + nrm(k, shape, 0.02)

    return {
        'x': nrm(ks[0], (BATCH, SEQ, D_MODEL), 1.0),
        'g_mix_norm': gain(ks[1], (L, D_MODEL)),
        'w_in': nrm(ks[2], (L, D_MODEL, D_IN_PROJ), D_MODEL ** -0.5),
        'w_alpha2': nrm(ks[3], (L, GLA_GATE_RANK, GLA_KEY_W), GLA_GATE_RANK ** -0.5),
        'b_alpha': nrm(ks[4], (L, GLA_KEY_W), 0.1),
        'gla_norm_g': gain(ks[5], (L, GLA_HEADS, GLA_DV)),
        'cmp_pos_k': nrm(ks[6], (L, CMP_BLOCK, NSA_DIM), 0.02),
        'cmp_pos_v': nrm(ks[7], (L, CMP_BLOCK, NSA_DIM), 0.02),
        'phi_k_w1': nrm(ks[8], (L, CMP_BLOCK * NSA_DIM, PHI_HIDDEN), (CMP_BLOCK * NSA_DIM) ** -0.5),
        'phi_k_w2': nrm(ks[9], (L, PHI_HIDDEN, NSA_DIM), PHI_HIDDEN ** -0.5),
        'phi_v_w1': nrm(ks[10], (L, CMP_BLOCK * NSA_DIM, PHI_HIDDEN), (CMP_BLOCK * NSA_DIM) ** -0.5),
        'phi_v_w2': nrm(ks[11], (L, PHI_HIDDEN, NSA_DIM), PHI_HIDDEN ** -0.5),
        'rel_bias_table': nrm(ks[12], (REL_BUCKETS, NSA_HEADS), 0.5),
        'w_gla_proj': nrm(ks[13], (L, GLA_VAL_W, D_MODEL), GLA_VAL_W ** -0.5),
        'w_nsa_proj': nrm(ks[14], (L, NSA_Q_W, D_MODEL), NSA_Q_W ** -0.5),
        'w_out': nrm(ks[15], (L, D_MODEL, D_MODEL), D_MODEL ** -0.5),
        'g_mlp_norm': gain(ks[16], (L, D_MODEL)),
        'w_up': nrm(ks[17], (L, D_MODEL, D_FF), D_MODEL ** -0.5),
        'w_down': nrm(ks[18], (L, D_FF, D_MODEL), D_FF ** -0.5),
        'g_final_norm': gain(ks[19], (D_MODEL,)),
    }


def reference(x, g_mix_norm, w_in, w_alpha2, b_alpha, gla_norm_g, cmp_pos_k, cmp_pos_v,
              phi_k_w1, phi_k_w2, phi_v_w1, phi_v_w2, rel_bias_table, w_gla_proj, w_nsa_proj,
              w_out, g_mlp_norm, w_up, w_down, g_final_norm):
    for l in range(DEPTH):
        h = _rmsnorm(x, g_mix_norm[l])
        u = h @ w_in[l]
        (gq, gk, gv, ga, gr, nq, kc, vc, ksl, vsl, kw, vw, ng, m_gla, m_nsa) = jnp.split(u, SPLIT_POINTS, axis=-1)
        o_gla = _gla_mixer(gq, gk, gv, ga, gr, w_alpha2[l], b_alpha[l], gla_norm_g[l])
        o_nsa = _nsa_mixer(nq, kc, vc, ksl, vsl, kw, vw, ng, cmp_pos_k[l], cmp_pos_v[l],
                           phi_k_w1[l], phi_k_w2[l], phi_v_w1[l], phi_v_w2[l], rel_bias_table)
        mix = (jax.nn.sigmoid(m_gla) * (o_gla @ w_gla_proj[l])
               + jax.nn.sigmoid(m_nsa) * (o_nsa @ w_nsa_proj[l]))
        x = x + mix @ w_out[l]
        h = _rmsnorm(x, g_mlp_norm[l])
        x = x + jnp.square(jax.nn.relu(h @ w_up[l])) @ w_down[l]
    return _rmsnorm(x, g_final_norm)
```

```python
import concourse.bass as bass
import concourse.mybir as mybir

F32 = mybir.dt.float32
BF16 = mybir.dt.bfloat16
AF = mybir.ActivationFunctionType
ALU = mybir.AluOpType

ENGS = ("tensor", "vector", "scalar", "gpsimd", "sync")


class Buf:
    __slots__ = ("name", "w", "r", "dma_sem", "dma_cum", "last_dma")

    def __init__(self, name):
        self.name = name
        self.w = []
        self.r = []
        self.dma_sem = None
        self.dma_cum = 0
        self.last_dma = None


class Rec:
    __slots__ = ("eng", "fn", "deps", "flag", "count", "dma_sem", "dma_cum", "is_dma")

    def __init__(self, eng, fn):
        self.eng = eng
        self.fn = fn
        self.deps = []
        self.flag = False
        self.count = 0
        self.dma_sem = None
        self.dma_cum = 0
        self.is_dma = False


class KB:
    def __init__(self, nc, stack):
        self.nc = nc
        self.stack = stack
        self.ops = {e: [] for e in ENGS}
        self.esem = {}
        for e in ("tensor", "vector", "scalar", "gpsimd"):
            self.esem[e] = stack.enter_context(nc.semaphore("es_" + e))
        self.dma_sems = []
        self.free_sems = []
        self.sem_final = {}
        self.sem_kind = {}
        self.pending = {e: [] for e in ENGS}
        self.all_bufs = []
        self.nsem = 4

    def buf(self, name):
        b = Buf(name)
        self.all_bufs.append(b)
        return b

    def _dep(self, rec, ev):
        if ev[0] == "op":
            r = ev[1]
            if r is rec:
                return
            if r.eng == "tensor" and rec.eng == "tensor" and not rec.is_dma:
                return
            r.flag = True
        rec.deps.append(ev)

    def _track(self, rec, reads, writes):
        for b in reads:
            for ev in b.w:
                self._dep(rec, ev)
        for b in writes:
            for ev in b.w:
                self._dep(rec, ev)
            for ev in b.r:
                self._dep(rec, ev)
        for ev in self.pending[rec.eng]:
            self._dep(rec, ev)
        self.pending[rec.eng] = []

    def _publish(self, ev, reads, writes):
        for b in reads:
            b.r.append(ev)
        for b in writes:
            b.w = [ev]
            b.r = []

    def op(self, eng, fn, reads=(), writes=()):
        rec = Rec(eng, fn)
        self._track(rec, reads, writes)
        self._publish(("op", rec), reads, writes)
        self.ops[eng].append(rec)
        return rec

    def dma(self, queue, fns, sb, reads=(), writes=()):
        rec = Rec(queue, fns)
        rec.is_dma = True
        if sb.dma_sem is None:
            kind = "sw" if queue == "gpsimd" else "hw"
            pool = [i for i, f in enumerate(self.free_sems) if f[2] == kind]
            if pool:
                sb.dma_sem, sb.dma_cum, _k = self.free_sems.pop(pool[-1])
            else:
                sb.dma_sem = self.stack.enter_context(self.nc.semaphore("ds%d" % self.nsem))
                self.nsem += 1
                sb.dma_cum = 0
            sb.last_dma = None
            self.dma_sems.append(sb)
            self.sem_kind[id(sb.dma_sem)] = "sw" if queue == "gpsimd" else "hw"
        self._track(rec, reads, writes)
        if sb.last_dma is not None:
            rec.deps.append(sb.last_dma)
        sb.dma_cum += 16 * len(fns)
        rec.dma_sem = sb.dma_sem
        rec.dma_cum = sb.dma_cum
        ev = ("dma", sb.dma_sem, sb.dma_cum)
        sb.last_dma = ev
        self.sem_final[id(sb.dma_sem)] = (sb.dma_sem, sb.dma_cum)
        self._publish(ev, reads, writes)
        self.ops[queue].append(rec)
        return rec

    def barrier(self):
        evs = []
        for e in ("tensor", "vector", "scalar", "gpsimd"):
            for rec in reversed(self.ops[e]):
                if not rec.is_dma:
                    rec.flag = True
                    evs.append(("op", rec))
                    break
        for sb in self.dma_sems:
            if sb.last_dma is not None:
                evs.append(sb.last_dma)
        for e in ENGS:
            self.pending[e] = list(self.pending[e]) + evs
        for b in self.all_bufs:
            b.w = []
            b.r = []
        for sb in self.dma_sems:
            self.free_sems.append((sb.dma_sem, sb.dma_cum, self.sem_kind[id(sb.dma_sem)]))
            sb.dma_sem = None
            sb.last_dma = None
        self.dma_sems = []

    def emit(self, block):
        for e in ("tensor", "vector", "scalar", "gpsimd"):
            c = 0
            for rec in self.ops[e]:
                if rec.flag and not rec.is_dma:
                    c += 1
                    rec.count = c
            print("engine", e, "n_ops", len(self.ops[e]), "flagged", c)
        print("sync n_ops", len(self.ops["sync"]), "n sems", self.nsem,
              "max dma cum", max([v[1] for v in self.sem_final.values()] + [0]))
        kb = self

        def run(eng_name, eng):
            waited = {}
            for rec in kb.ops[eng_name]:
                need = {}
                for ev in rec.deps:
                    if ev[0] == "op":
                        r = ev[1]
                        sem = kb.esem[r.eng]
                        cnt = r.count
                        assert cnt > 0
                    else:
                        sem, cnt = ev[1], ev[2]
                    key = id(sem)
                    if key not in need or need[key][1] < cnt:
                        need[key] = (sem, cnt)
                for key, (sem, cnt) in need.items():
                    if waited.get(key, 0) >= cnt:
                        continue
                    eng.wait_ge(sem, cnt)
                    waited[key] = cnt
                if rec.is_dma:
                    for f in rec.fn:
                        f(eng).then_inc(rec.dma_sem, 16)
                else:
                    ins = rec.fn(eng)
                    if rec.flag:
                        ins.then_inc(kb.esem[eng_name], 1)
            if eng_name == "sync":
                for key, (sem, cum) in kb.sem_final.items():
                    if cum > waited.get(key, 0):
                        eng.wait_ge(sem, cum)

        block.tensor(lambda t: run("tensor", t))
        block.vector(lambda v: run("vector", v))
        block.scalar(lambda s: run("scalar", s))
        block.gpsimd(lambda g: run("gpsimd", g))
        block.sync(lambda s: run("sync", s))


class Arena:
    def __init__(self, ap, ncols):
        self.ap = ap
        self.n = ncols
        self.off = 0

    def reset(self, off=0):
        self.off = off

    def alloc(self, cols, dtype=BF16):
        w = cols * (2 if dtype == F32 else 1)
        w = (w + 31) // 32 * 32
        off = self.off
        assert off + w <= self.n, ("arena overflow", off, w, self.n)
        self.off += w
        ap = self.ap[:, off:off + (cols * (2 if dtype == F32 else 1))]
        if dtype == F32:
            ap = ap.bitcast(F32)
        return ap

import numpy as np
import ml_dtypes
from contextlib import ExitStack
from concourse.bass_utils import run_bass_kernel_spmd

D = 4096
NOWN = 2048
NV = 4096
DIN = 19520
EPS = 1e-6
NEG = -32768.0

FAMS = [
    ("gq", 0, 1024, "F", "own", "copy"),
    ("gk", 1024, 1024, "F", "all", "copy"),
    ("gv", 2048, 2048, "T", "all", "copy"),
    ("ga", 4096, 16, "F", "all", "copy"),
    ("gr", 4112, 2048, "F", "own", "silu"),
    ("nq", 6160, 2048, "F", "own", "scale"),
    ("kc", 8208, 512, "F", "all", "copy"),
    ("vc", 8720, 512, "F", "all", "copy"),
    ("ksl", 9232, 512, "F", "all", "copy"),
    ("vsl", 9744, 512, "T", "all", "copy"),
    ("kw", 10256, 512, "F", "win", "copy"),
    ("vw", 10768, 512, "T", "win", "copy"),
    ("ng", 11280, 48, "F", "own", "sigmoid"),
    ("mg", 11328, 4096, "F", "own", "sigmoid"),
    ("mn", 15424, 4096, "F", "own", "sigmoid"),
]


class Prog:
    pass


def build(debug=(), phases=("A", "G", "N", "P", "O", "M", "F"), ext_in=()):
    nc = bass.Bass("TRN2", target_bir_lowering=False)
    P = Prog()
    P.nc = nc

    def din(name, shape, dt=F32):
        return nc.dram_tensor(name, list(shape), dt, kind="ExternalInput").ap()

    def scr(name, shape, dt=BF16):
        kind = "ExternalOutput" if name in debug else ("ExternalInput" if name in ext_in else "Internal")
        return nc.dram_tensor(name, list(shape), dt, kind=kind).ap()

    I = {}
    I["x_own"] = din("x_own", [NOWN, D])
    I["x_pre"] = din("x_pre", [NOWN, D])
    I["g_mix"] = din("g_mix", [D])
    I["w_in"] = din("w_in", [D, DIN])
    I["ident"] = din("ident", [128, 128])
    I["w_alpha2"] = din("w_alpha2", [16, 1024])
    I["b_alpha"] = din("b_alpha", [1024])
    I["gla_norm_g"] = din("gla_norm_g", [16, 128])
    I["m2"] = din("m2", [128, 128])
    for nm in ("phi_k_w1", "phi_v_w1"):
        I[nm] = din(nm, [4096, 128])
    for nm in ("phi_k_w2", "phi_v_w2"):
        I[nm] = din(nm, [128, 128])
    for nm in ("cmp_pos_k", "cmp_pos_v"):
        I[nm] = din(nm, [32, 128])
    I["rel_table"] = din("rel_table", [32, 16])
    I["bfull"] = din("bfull", [16, 128, 376])
    I["w0"] = din("w0", [16, 128, 128])
    I["w1"] = din("w1", [16, 128, 128])
    I["w4"] = din("w4", [16, 128, 128])
    I["ekt"] = din("ekt", [64, 32 * 128])
    I["reach01"] = din("reach01", [128, 16, 64])
    I["addc"] = din("addc", [128, 16, 64])
    I["killc"] = din("killc", [1, 256])
    I["killcol"] = din("killcol", [128, 1])
    I["w_gla_proj"] = din("w_gla_proj", [2048, D])
    I["w_nsa_proj"] = din("w_nsa_proj", [2048, D])
    I["w_out"] = din("w_out", [D, D])
    I["g_mlp"] = din("g_mlp", [D])
    I["w_up"] = din("w_up", [D, 4 * D])
    I["w_down"] = din("w_down", [4 * D, D])
    I["g_final"] = din("g_final", [D])
    S = {}
    S["gq"] = scr("s_gq", [1024, NOWN])
    S["gk"] = scr("s_gk", [1024, NV])
    S["gv"] = scr("s_gv", [NV, 2048])
    S["ga"] = scr("s_ga", [16, NV])
    S["gr"] = scr("s_gr", [2048, NOWN])
    S["nq"] = scr("s_nq", [2048, NOWN])
    S["kc"] = scr("s_kc", [512, NV])
    S["vc"] = scr("s_vc", [512, NV])
    S["ksl"] = scr("s_ksl", [512, NV])
    S["vsl"] = scr("s_vsl", [NV, 512])
    S["kw"] = scr("s_kw", [512, 2560])
    S["vw"] = scr("s_vw", [2560, 512])
    S["ng"] = scr("s_ng", [48, NOWN])
    S["mg"] = scr("s_mg", [D, NOWN])
    S["mn"] = scr("s_mn", [D, NOWN])
    S["oall"] = scr("s_oall", [D, NOWN])
    S["mix"] = scr("s_mix", [D, NOWN])
    S["x2"] = scr("s_x2", [NOWN, D], F32)
    S["hid"] = scr("s_hid", [16, 128, 128, 128])
    S["x3"] = scr("s_x3", [NOWN, D], F32)
    P.out = nc.dram_tensor("out", [NOWN, D], F32, kind="ExternalOutput").ap()
    P.I, P.S = I, S

    stack = ExitStack()
    with stack:
        kb = KB(nc, stack)
        ARN = 100 * 1024
        arena_t = stack.enter_context(nc.sbuf_tensor("arena", [128, ARN], BF16))
        pers_t = stack.enter_context(nc.sbuf_tensor("pers", [128, 1024], BF16))
        ar = Arena(arena_t[:, :], ARN)
        pers = Arena(pers_t[:, :], 1024)
        psb = [stack.enter_context(nc.psum_tensor("ps%d" % i, [128, 512], F32)) for i in range(8)]
        PS = [p[:, :] for p in psb]
        PSB = [kb.buf("psum%d" % i) for i in range(8)]
        ident = pers.alloc(128, BF16)
        ident_f = pers.alloc(128, F32)
        b_ident = kb.buf("ident")
        kb.dma("sync", [lambda e: e.dma_start(out=ident_f, in_=I["ident"])], b_ident, writes=[b_ident])
        kb.op("vector", lambda e: e.tensor_copy(out=ident, in_=ident_f), reads=[b_ident], writes=[b_ident])
        P.kb, P.ar, P.pers, P.PS, P.PSB, P.ident, P.b_ident = kb, ar, pers, PS, PSB, ident, b_ident
        P.ident_f = ident_f

        if "A" in phases:
            phase_inproj(P)
        if "G" in phases:
            phase_gla(P)
        if "N" in phases:
            phase_nsa(P)
        if "P" in phases:
            phase_proj(P)
        if "O" in phases:
            phase_out(P)
        if "M" in phases:
            phase_mlp(P)
        if "F" in phases:
            phase_final(P)
        block = stack.enter_context(nc.Block())
        kb.emit(block)
    return nc


def rmsnorm_transpose(P, x_ap, g_ap, hT, b_hT, ntile, region_off):
    kb, ar, PS, PSB = P.kb, P.ar, P.PS, P.PSB
    ar.reset(region_off)
    xs = [ar.alloc(D, F32) for _ in range(2)]
    b_xs = [kb.buf("xs%d" % i) for i in range(2)]
    hb = [ar.alloc(D, BF16) for _ in range(2)]
    b_hb = [kb.buf("hb%d" % i) for i in range(2)]
    gb = ar.alloc(D, F32)
    b_gb = kb.buf("gb")
    st = ar.alloc(64, F32)
    b_st = [kb.buf("st%d" % i) for i in range(2)]
    kb.dma("sync", [lambda e: e.dma_start(out=gb, in_=g_ap.partition_broadcast(128))], b_gb, writes=[b_gb])
    for t in range(ntile):
        s = t % 2
        xt, ht = xs[s], hb[s]
        ss = st[:, 2 * s:2 * s + 1]
        rs = st[:, 2 * s + 1:2 * s + 2]
        kb.dma("sync", [lambda e, xt=xt, t=t: e.dma_start(out=xt, in_=x_ap[t * 128:(t + 1) * 128, :])],
               b_xs[s], writes=[b_xs[s]])
        kb.op("scalar", lambda e, xt=xt, ht=ht, ss=ss: e.activation(out=ht, in_=xt, func=AF.Square, accum_out=ss),
              reads=[b_xs[s]], writes=[b_hb[s], b_st[s]])
        kb.op("vector", lambda e, ss=ss, rs=rs: e.tensor_scalar(out=rs, in0=ss, scalar1=1.0 / D, scalar2=EPS,
                                                              op0=ALU.mult, op1=ALU.add),
              reads=[b_st[s]], writes=[b_st[s]])
        kb.op("scalar", lambda e, rs=rs: e.sqrt(out=rs, in_=rs), reads=[b_st[s]], writes=[b_st[s]])
        kb.op("vector", lambda e, rs=rs: e.reciprocal(out=rs, in_=rs), reads=[b_st[s]], writes=[b_st[s]])
        kb.op("vector", lambda e, xt=xt, ht=ht, rs=rs: e.scalar_tensor_tensor(out=ht, in0=xt, scalar=rs, in1=gb,
                                                                              op0=ALU.mult, op1=ALU.mult),
              reads=[b_xs[s], b_st[s], b_gb], writes=[b_hb[s]])
        for q in range(4):
            bank = q % 2
            pT = PS[bank].bitcast(BF16)

            def tfn(e, ht=ht, q=q, pT=pT):
                ins = None
                for j in range(8):
                    kt = q * 8 + j
                    ins = e.transpose(pT[:, j * 128:(j + 1) * 128], ht[:, kt * 128:(kt + 1) * 128], P.ident)
                return ins
            kb.op("tensor", tfn, reads=[b_hb[s], P.b_ident], writes=[PSB[bank]])
            dst = hT[:, q * 8:(q + 1) * 8, t * 128:(t + 1) * 128]
            src = pT.rearrange("p (j c) -> p j c", j=8)
            if q % 2 == 0:
                kb.op("scalar", lambda e, dst=dst, src=src: e.copy(out=dst, in_=src), reads=[PSB[bank]], writes=[kb.buf('hTq')])
            else:
                kb.op("vector", lambda e, dst=dst, src=src: e.tensor_copy(out=dst, in_=src), reads=[PSB[bank]], writes=[kb.buf('hTq')])


def phase_inproj(P):
    kb, ar, PS, PSB, I, S = P.kb, P.ar, P.PS, P.PSB, P.I, P.S
    nc = P.nc
    ar.reset()
    hT = ar.alloc(32 * NOWN, BF16).rearrange("p (k t) -> p k t", k=32)
    b_hT = kb.buf("hT")
    R0 = ar.off
    for ps_i, (xname, tok0) in enumerate((("x_own", NOWN), ("x_pre", 0))):
        own = xname == "x_own"
        rmsnorm_transpose(P, I[xname], I["g_mix"], hT, b_hT, 16, R0)
        kb.barrier()
        ar.reset(R0)
        WS = [ar.alloc(32 * 512, BF16).rearrange("p (k c) -> p k c", k=32) for _ in range(2)]
        b_ws = [kb.buf("ws%d" % i) for i in range(2)]
        OS = [ar.alloc(2048, BF16) for _ in range(2)]
        OSF = [o.bitcast(F32) for o in OS]
        b_os = [kb.buf("os%d" % i) for i in range(2)]
        wcnt = 0
        ocnt = 0
        for (name, c0, ncols, layout, scope, act) in FAMS:
            if not own and scope == "own":
                continue
            if scope == "win" and not own:
                chunks = [3]
            else:
                chunks = [0, 1, 2, 3]
            dst = S[name]
            for w0 in range(0, ncols, 512):
                wn = min(512, ncols - w0)
                ws = WS[wcnt % 2]
                bw = b_ws[wcnt % 2]
                wcnt += 1
                fns = []
                for i in range(4):
                    fns.append(lambda e, ws=ws, i=i, c0=c0, w0=w0, wn=wn: e.dma_start(
                        out=ws[:, i * 8:(i + 1) * 8, 0:wn],
                        in_=I["w_in"][i * 1024:(i + 1) * 1024, c0 + w0:c0 + w0 + wn].rearrange("(k p) c -> p k c", p=128)))
                kb.dma("gpsimd", fns, bw, writes=[bw])
                if layout == "F":
                    for cb in range(0, wn, 128):
                        cn = min(128, wn - cb)
                        half = (cb // 128) % 2
                        banks = [half * 4 + i for i in range(len(chunks))]

                        def mm(e, ws=ws, cb=cb, cn=cn, banks=banks, chunks=chunks):
                            ins = None
                            for kt in range(32):
                                for bi, ch in zip(banks, chunks):
                                    ins = e.matmul(PS[bi][0:cn, :], ws[:, kt, cb:cb + cn], hT[:, kt, ch * 512:(ch + 1) * 512],
                                                   start=(kt == 0), stop=(kt == 31))
                            return ins
                        kb.op("tensor", mm, reads=[bw], writes=[PSB[b] for b in banks])
                        osl = ocnt % 2
                        ocnt += 1
                        f32out = False
                        for li, (bi, ch) in enumerate(zip(banks, chunks)):
                            if f32out:
                                o_ap = OSF[osl][0:cn, (li % 2) * 512:(li % 2 + 1) * 512]
                            else:
                                o_ap = OS[osl][0:cn, li * 512:(li + 1) * 512]
                            src = PS[bi][0:cn, :]
                            if act == "copy":
                                kb.op("vector", lambda e, o_ap=o_ap, src=src: e.tensor_copy(out=o_ap, in_=src),
                                      reads=[PSB[bi]], writes=[b_os[osl]])
                            elif act == "scale":
                                kb.op("vector", lambda e, o_ap=o_ap, src=src: e.tensor_scalar(
                                    out=o_ap, in0=src, scalar1=128.0 ** -0.5, scalar2=None, op0=ALU.mult),
                                    reads=[PSB[bi]], writes=[b_os[osl]])
                            elif act == "silu":
                                kb.op("scalar", lambda e, o_ap=o_ap, src=src: e.activation(out=o_ap, in_=src, func=AF.Silu),
                                      reads=[PSB[bi]], writes=[b_os[osl]])
                            elif act == "sigmoid":
                                kb.op("scalar", lambda e, o_ap=o_ap, src=src: e.activation(out=o_ap, in_=src, func=AF.Sigmoid),
                                      reads=[PSB[bi]], writes=[b_os[osl]])
                            if f32out and (li % 2 == 1):
                                t0 = (chunks[li - 1]) * 512
                                kb.dma("sync", [lambda e, osl=osl, cn=cn, t0=t0, w0=w0, cb=cb, dst=dst: e.dma_start(
                                    out=dst[w0 + cb:w0 + cb + cn, t0:t0 + 1024], in_=OSF[osl][0:cn, 0:1024])],
                                    b_os[osl], reads=[b_os[osl]])
                                if li != len(chunks) - 1:
                                    osl = ocnt % 2
                                    ocnt += 1
                        if not f32out:
                            if scope == "own":
                                tcol0 = 0
                            elif scope == "all":
                                tcol0 = (NOWN if own else 0)
                            else:
                                tcol0 = (512 if own else -1536)
                            tA = tcol0 + chunks[0] * 512
                            nt = 512 * len(chunks)
                            kb.dma("sync", [lambda e, osl=osl, cn=cn, tA=tA, nt=nt, w0=w0, cb=cb, dst=dst: e.dma_start(
                                out=dst[w0 + cb:w0 + cb + cn, tA:tA + nt], in_=OS[osl][0:cn, 0:nt])],
                                b_os[osl], reads=[b_os[osl]])
                else:
                    ttiles = [ch * 4 + i for ch in chunks for i in range(4)]
                    for g0 in range(0, len(ttiles), 4):
                        grp = ttiles[g0:g0 + 4]
                        half = (g0 // 4) % 2
                        banks = [half * 4 + i for i in range(4)]
                        for bi, tt in zip(banks, grp):
                            def mm(e, ws=ws, wn=wn, bi=bi, tt=tt):
                                ins = None
                                for kt in range(32):
                                    ins = e.matmul(PS[bi][:, 0:wn], hT[:, kt, tt * 128:(tt + 1) * 128], ws[:, kt, 0:wn],
                                                   start=(kt == 0), stop=(kt == 31))
                                return ins
                            kb.op("tensor", mm, reads=[bw], writes=[PSB[bi]])
                        osl = ocnt % 2
                        ocnt += 1
                        for li, (bi, tt) in enumerate(zip(banks, grp)):
                            o_ap = OS[osl][:, li * 512:li * 512 + wn]
                            src = PS[bi][:, 0:wn]
                            if li % 2 == 0:
                                kb.op("vector", lambda e, o_ap=o_ap, src=src: e.tensor_copy(out=o_ap, in_=src),
                                      reads=[PSB[bi]], writes=[b_os[osl]])
                            else:
                                kb.op("scalar", lambda e, o_ap=o_ap, src=src: e.copy(out=o_ap, in_=src),
                                      reads=[PSB[bi]], writes=[b_os[osl]])
                        if scope == "all":
                            r0 = (NOWN if own else 0)
                        else:
                            r0 = (512 if own else -1536)
                        rA = r0 + grp[0] * 128
                        kb.dma("sync", [lambda e, osl=osl, rA=rA, w0=w0, wn=wn, dst=dst: e.dma_start(
                            out=dst[rA:rA + 512, w0:w0 + wn].rearrange("(j p) c -> p j c", p=128),
                            in_=OS[osl][:, :].rearrange("p (j c) -> p j c", j=4)[:, :, 0:wn])],
                            b_os[osl], reads=[b_os[osl]])
        kb.barrier()


def phase_gla(P):
    kb, ar, PS, PSB, I, S = P.kb, P.ar, P.PS, P.PSB, P.I, P.S
    ar.reset()
    wa2 = ar.alloc(1024, BF16)
    aT = ar.alloc(NV, BF16)
    nb = ar.alloc(8, F32)
    gcol = ar.alloc(16, F32)
    m2 = ar.alloc(128, F32)
    ones = ar.alloc(128, BF16)
    msk = ar.alloc(NV, F32)
    b_c = kb.buf("gconst")
    b_c2 = kb.buf("gconst2")
    b_c3 = kb.buf("gconst3")
    b_c4 = kb.buf("gconst4")
    b_c5 = kb.buf("gconst5")
    kb.dma("gpsimd", [lambda e: e.dma_start(out=wa2[0:16, :], in_=I["w_alpha2"])], b_c, writes=[b_c])
    kb.dma("sync", [lambda e: e.dma_start(out=aT[0:16, :], in_=S["ga"])], b_c2, writes=[b_c2])
    b8 = ar.alloc(128, F32)
    g16 = ar.alloc(128, F32)
    b_b8 = kb.buf("b8")
    b_g16 = kb.buf("g16")
    kb.dma("sync", [lambda e: e.dma_start(out=b8[0:8, :], in_=I["b_alpha"].rearrange("(f p) -> f p", p=128))], b_b8, writes=[b_b8])
    kb.dma("sync", [lambda e: e.dma_start(out=g16[0:16, :], in_=I["gla_norm_g"])], b_g16, writes=[b_g16])
    kb.op("tensor", lambda e: e.matmul(PS[7][:, 0:8], b8[0:8, :], P.ident_f[0:8, 0:8], start=True, stop=True),
          reads=[b_b8, P.b_ident], writes=[PSB[7]])
    kb.op("vector", lambda e: e.tensor_scalar(out=nb, in0=PS[7][:, 0:8], scalar1=-1.0, scalar2=None, op0=ALU.mult),
          reads=[PSB[7]], writes=[b_c3])
    kb.op("tensor", lambda e: e.matmul(PS[7][:, 0:16], g16[0:16, :], P.ident_f[0:16, 0:16], start=True, stop=True),
          reads=[b_g16, P.b_ident], writes=[PSB[7]])
    kb.op("vector", lambda e: e.tensor_copy(out=gcol, in_=PS[7][:, 0:16]), reads=[PSB[7]], writes=[b_c4])
    kb.dma("sync", [lambda e: e.dma_start(out=m2, in_=I["m2"])], b_c5, writes=[b_c5])
    b_ones = kb.buf("ones")
    kb.op("vector", lambda e: e.memset(ones, 1.0), writes=[b_ones])
    b_msk = kb.buf("msk")
    kb.op("gpsimd", lambda e: e.memset(msk, 1.0), writes=[b_msk])
    kb.op("gpsimd", lambda e: e.memset(msk.rearrange("p (n c) -> p n c", c=64)[:, :, 0:1], 0.0), reads=[b_msk], writes=[b_msk])

    sp = ar.alloc(NV, F32)
    Lp = ar.alloc(NOWN, F32)
    onesf = ar.alloc(NOWN, F32)
    bc = ar.alloc(1, F32)
    b_Lp, b_onesf, b_bc = kb.buf('Lp'), kb.buf('onesf'), kb.buf('bc')
    kb.op('gpsimd', lambda e: e.memset(onesf, 1.0), writes=[b_onesf])
    Eq = ar.alloc(NV, F32)
    Ek = ar.alloc(NV, F32)
    qT = ar.alloc(NOWN, BF16)
    kT = ar.alloc(NV, BF16)
    qeT = ar.alloc(NOWN, BF16)
    keT = ar.alloc(NV, BF16)
    ke_tm = ar.alloc(32 * 128, BF16).rearrange("p (t c) -> p t c", t=32)
    v_tm = ar.alloc(32 * 256, BF16).rearrange("p (t c) -> p t c", t=32)
    Tst = [ar.alloc(256, F32) for _ in range(2)]
    S32 = ar.alloc(256, F32)
    b_S32 = kb.buf('S32')
    SbA = ar.alloc(32 * 256, BF16).rearrange('p (n c) -> p n c', n=32)
    atA = ar.alloc(32 * 128, BF16).rearrange('p (n c) -> p n c', n=32)
    kvS = ar.alloc(32 * 256, BF16).rearrange('p (n c) -> p n c', n=32)
    o_sb = [ar.alloc(512, F32) for _ in range(2)]
    sq = [ar.alloc(512, BF16) for _ in range(2)]
    rs = [ar.alloc(512, F32) for _ in range(2)]
    rT = [ar.alloc(512, BF16) for _ in range(2)]
    ob = [ar.alloc(512, BF16) for _ in range(2)]
    b_sp, b_Eq, b_Ek, b_qT, b_kT, b_qeT, b_keT, b_ketm, b_vtm = [kb.buf(n) for n in
        ("sp", "Eq", "Ek", "qT", "kT", "qeT", "keT", "ketm", "vtm")]
    b_T = [kb.buf("T0"), kb.buf("T1")]
    b_SbA = [kb.buf("SbA%d" % i) for i in range(32)]
    b_atA = [kb.buf("atA%d" % i) for i in range(32)]
    b_kvS = [kb.buf("kvS%d" % i) for i in range(32)]
    b_osb = [kb.buf("osb%d" % i) for i in range(2)]
    b_sq = [kb.buf("sq%d" % i) for i in range(2)]
    b_rs = [kb.buf("rs%d" % i) for i in range(2)]
    b_rT = [kb.buf("rT%d" % i) for i in range(2)]
    b_ob = [kb.buf("ob%d" % i) for i in range(2)]
    b_po = [[kb.buf("po%d%d" % (h, i)) for i in range(2)] for h in range(2)]
    b_pa = [PSB[4]] * 4
    b_pk = [PSB[5], PSB[6]]
    b_p6 = PSB[7]
    b_p7 = PSB[7]
    ecnt = 0
    print("G arena cols used", ar.off)
    for fb in range(8):
        kb.dma("sync", [lambda e, fb=fb: e.dma_start(out=qT, in_=S["gq"][fb * 128:(fb + 1) * 128, :])], b_qT, writes=[b_qT])
        kb.dma("sync", [lambda e, fb=fb: e.dma_start(out=kT, in_=S["gk"][fb * 128:(fb + 1) * 128, :])], b_kT, writes=[b_kT])
        kb.dma("sync", [lambda e, fb=fb: e.dma_start(
            out=v_tm, in_=S["gv"][:, fb * 256:(fb + 1) * 256].rearrange("(t p) c -> p t c", p=128))], b_vtm, writes=[b_vtm])
        for c in range(8):
            pb_, bb = (PS[7], b_p7) if c % 2 == 0 else (PS[4], PSB[4])
            kb.op("tensor", lambda e, pb_=pb_, c=c, fb=fb: e.matmul(pb_[:, :], wa2[0:16, fb * 128:(fb + 1) * 128],
                                                                    aT[0:16, c * 512:(c + 1) * 512], start=True, stop=True),
                  reads=[b_c, b_c2], writes=[bb])
            kb.op("scalar", lambda e, pb_=pb_, c=c, fb=fb: e.activation(out=sp[:, c * 512:(c + 1) * 512], in_=pb_[:, :], func=AF.Exp,
                                                                        bias=nb[:, fb:fb + 1], scale=-1.0),
                  reads=[bb, b_c3], writes=[b_sp])
        kb.op("scalar", lambda e: e.activation(out=sp, in_=sp, func=AF.Ln, bias=1.0, scale=1.0), reads=[b_sp], writes=[b_sp])
        kb.op("vector", lambda e: e.tensor_tensor_scan(out=Lp, data0=onesf, data1=sp[:, 0:NOWN], initial=0.0, op0=ALU.mult, op1=ALU.add),
              reads=[b_sp, b_onesf], writes=[b_Lp])
        kb.op("vector", lambda e: e.tensor_scalar(out=bc, in0=Lp[:, NOWN - 1:NOWN], scalar1=-1.0 / 16.0, scalar2=None, op0=ALU.mult),
              reads=[b_Lp], writes=[b_bc])
        kb.op("scalar", lambda e: e.activation(out=Ek[:, 0:NOWN], in_=Lp, func=AF.Exp, bias=bc, scale=1.0 / 16.0), reads=[b_Lp, b_bc], writes=[b_Ek])
        kb.op("vector", lambda e: e.tensor_tensor_scan(out=sp[:, NOWN:NV], data0=msk[:, 0:NOWN], data1=sp[:, NOWN:NV], initial=0.0, op0=ALU.mult, op1=ALU.add),
              reads=[b_sp, b_msk], writes=[b_sp])
        kb.op("scalar", lambda e: e.activation(out=Eq[:, NOWN:NV], in_=sp[:, NOWN:NV], func=AF.Exp, scale=-1.0 / 16.0), reads=[b_sp], writes=[b_Eq])
        kb.op("scalar", lambda e: e.activation(out=Ek[:, NOWN:NV], in_=sp[:, NOWN:NV], func=AF.Exp, scale=1.0 / 16.0), reads=[b_sp], writes=[b_Ek])
        kb.op("vector", lambda e: e.scalar_tensor_tensor(out=qeT, in0=qT, scalar=0.125, in1=Eq[:, NOWN:NV], op0=ALU.mult, op1=ALU.mult),
              reads=[b_qT, b_Eq], writes=[b_qeT])
        kb.op("vector", lambda e: e.tensor_tensor(out=keT, in0=kT, in1=Ek, op=ALU.mult), reads=[b_kT, b_Ek], writes=[b_keT])
        for q in range(4):
            pb_, bb = (PS[7], b_p7)
            pT = pb_.bitcast(BF16)

            def tfn(e, q=q, pT=pT):
                ins = None
                for j in range(8):
                    t = q * 8 + j
                    ins = e.transpose(pT[:, j * 128:(j + 1) * 128], keT[:, t * 128:(t + 1) * 128], P.ident)
                return ins
            kb.op("tensor", tfn, reads=[b_keT, P.b_ident], writes=[bb])
            dst = ke_tm[:, q * 8:(q + 1) * 8, :]
            src = pT.rearrange("p (j c) -> p j c", j=8)
            if q % 2 == 0:
                kb.op("scalar", lambda e, dst=dst, src=src: e.copy(out=dst, in_=src), reads=[bb], writes=[b_ketm])
            else:
                kb.op("vector", lambda e, dst=dst, src=src: e.tensor_copy(out=dst, in_=src), reads=[bb], writes=[b_ketm])
        def pfx(e):
            ins = None
            for t in range(16):
                ins = e.matmul(PS[5][:, 0:256], ke_tm[:, t, :], v_tm[:, t, :], start=(t == 0), stop=(t == 15))
            return ins
        kb.op("tensor", pfx, reads=[b_ketm, b_vtm], writes=[PSB[5]])
        kb.op("vector", lambda e: e.tensor_copy(out=S32, in_=PS[5][:, 0:256]), reads=[PSB[5]], writes=[b_S32])
        kb.op("scalar", lambda e: e.copy(out=SbA[:, 0, :], in_=S32), reads=[b_S32], writes=[b_SbA[0]])
        for n in range(32, 64):
            t = n // 2
            pb0 = 64 * (n % 2)
            kslot = n % 2
            pk = PS[5 + kslot][:, 0:256]
            kb.op("tensor", lambda e, pk=pk, t=t, pb0=pb0: e.matmul(pk, ke_tm[pb0:pb0 + 64, t, :], v_tm[pb0:pb0 + 64, t, :],
                                                                    start=True, stop=True),
                  reads=[b_ketm, b_vtm], writes=[b_pk[kslot]])
            kb.op("scalar", lambda e, pk=pk, n=n: e.copy(out=kvS[:, n - 32, :], in_=pk), reads=[b_pk[kslot]], writes=[b_kvS[n - 32]])
        masks = []
        for tq in range(16):
            t = 16 + tq
            for hd in range(2):
                hb = 64 * hd
                ai = tq * 2 + hd
                pa = PS[4][:, 0:128]
                kb.op("tensor", lambda e, pa=pa, hb=hb, t=t, tq=tq: e.matmul(
                    pa, keT[hb:hb + 64, t * 128:(t + 1) * 128], qeT[hb:hb + 64, tq * 128:(tq + 1) * 128], start=True, stop=True),
                    reads=[b_keT, b_qeT], writes=[PSB[4]])
                kb.op("vector", lambda e, pa=pa, ai=ai: e.tensor_tensor(out=atA[:, ai, :], in0=pa, in1=m2, op=ALU.mult),
                      reads=[PSB[4], b_c5], writes=[b_atA[ai]])
        for n in range(32, 64):
            Tn, bTn = Tst[n % 2], b_T[n % 2]
            Tp, bTp = Tst[(n - 1) % 2], b_T[(n - 1) % 2]
            kv_ = kvS[:, n - 32, :]
            if n == 32:
                kb.op("vector", lambda e, kv_=kv_, Tn=Tn: e.tensor_tensor(out=Tn, in0=S32, in1=kv_, op=ALU.add),
                      reads=[b_kvS[n - 32], b_S32], writes=[bTn])
            else:
                dcol = Eq[:, 64 * (n - 1) + 63:64 * (n - 1) + 64]
                kb.op("vector", lambda e, kv_=kv_, dcol=dcol, Tn=Tn, Tp=Tp: e.scalar_tensor_tensor(out=Tn, in0=Tp, scalar=dcol, in1=kv_,
                                                                                                  op0=ALU.mult, op1=ALU.add),
                      reads=[b_kvS[n - 32], bTp, b_Eq], writes=[bTn])
            if n < 63:
                dcol = Eq[:, 64 * n + 63:64 * n + 64]
                kb.op("vector", lambda e, n=n, dcol=dcol, Tn=Tn: e.tensor_scalar(out=SbA[:, n + 1 - 32, :], in0=Tn, scalar1=dcol, scalar2=None, op0=ALU.mult),
                      reads=[bTn, b_Eq], writes=[b_SbA[n + 1 - 32]])
        for tq in range(16):
            t = 16 + tq
            n0, n1 = 2 * t, 2 * t + 1
            g4 = tq % 4
            for hd in range(2):
                hb = 64 * hd
                ec = 128 * hd
                h = fb * 2 + hd
                ai = tq * 2 + hd
                dbl = (tq // 4) % 2
                po = PS[hd * 2 + dbl][:, g4 * 128:(g4 + 1) * 128]
                s0, s1 = n0 - 32, n1 - 32

                def ofn(e, po=po, t=t, ec=ec, hb=hb, ai=ai, s0=s0, s1=s1, tq=tq):
                    e.matmul(po[:, 0:64], v_tm[:, t, ec:ec + 128], atA[:, ai, 0:64], start=True, stop=False)
                    e.matmul(po[:, 0:64], SbA[hb:hb + 64, s0, ec:ec + 128], qeT[hb:hb + 64, tq * 128:tq * 128 + 64],
                             start=False, stop=True)
                    e.matmul(po[:, 64:128], v_tm[:, t, ec:ec + 128], atA[:, ai, 64:128], start=True, stop=False)
                    return e.matmul(po[:, 64:128], SbA[hb:hb + 64, s1, ec:ec + 128], qeT[hb:hb + 64, tq * 128 + 64:tq * 128 + 128],
                                    start=False, stop=True)
                kb.op("tensor", ofn, reads=[b_vtm, b_atA[ai], b_SbA[s0], b_SbA[s1], b_qeT], writes=[b_po[hd][dbl]])
                if g4 == 3:
                    c0 = (tq - 3) * 128
                    sl = hd
                    pob = PS[hd * 2 + dbl]
                    kb.dma("sync", [lambda e, sl=sl, h=h, c0=c0: e.dma_start(out=rT[sl], in_=S["gr"][h * 128:(h + 1) * 128, c0:c0 + 512])],
                           b_rT[sl], writes=[b_rT[sl]])
                    kb.op("scalar", lambda e, sl=sl, pob=pob: e.copy(out=o_sb[sl], in_=pob[:, :]), reads=[b_po[hd][dbl]], writes=[b_osb[sl]])
                    kb.op("scalar", lambda e, sl=sl, pob=pob: e.activation(out=sq[sl], in_=pob[:, :], func=AF.Square),
                          reads=[b_po[hd][dbl]], writes=[b_sq[sl]])
                    pss, bss = (PS[7], b_p7)
                    kb.op("tensor", lambda e, sl=sl, pss=pss: e.matmul(pss[:, :], ones, sq[sl], start=True, stop=True),
                          reads=[b_sq[sl], b_ones], writes=[bss])
                    kb.op("vector", lambda e, sl=sl, pss=pss: e.tensor_scalar(out=rs[sl], in0=pss[:, :], scalar1=1.0 / 128.0, scalar2=EPS,
                                                                              op0=ALU.mult, op1=ALU.add),
                          reads=[bss], writes=[b_rs[sl]])
                    kb.op("scalar", lambda e, sl=sl: e.activation(out=rs[sl], in_=rs[sl], func=AF.Ln), reads=[b_rs[sl]], writes=[b_rs[sl]])
                    kb.op("scalar", lambda e, sl=sl: e.activation(out=rs[sl], in_=rs[sl], func=AF.Exp, scale=-0.5), reads=[b_rs[sl]], writes=[b_rs[sl]])
                    kb.op("vector", lambda e, sl=sl, h=h: e.scalar_tensor_tensor(out=o_sb[sl], in0=o_sb[sl], scalar=gcol[:, h:h + 1], in1=rs[sl],
                                                                                 op0=ALU.mult, op1=ALU.mult),
                          reads=[b_osb[sl], b_rs[sl], b_c4], writes=[b_osb[sl]])
                    kb.op("vector", lambda e, sl=sl: e.tensor_tensor(out=ob[sl], in0=o_sb[sl], in1=rT[sl], op=ALU.mult),
                          reads=[b_osb[sl], b_rT[sl]], writes=[b_ob[sl]])
                    kb.dma("sync", [lambda e, sl=sl, h=h, c0=c0: e.dma_start(out=S["oall"][h * 128:(h + 1) * 128, c0:c0 + 512], in_=ob[sl])],
                           b_ob[sl], reads=[b_ob[sl]])
    kb.barrier()


def phase_nsa(P):
    kb, ar, PS, PSB, I, S = P.kb, P.ar, P.PS, P.PSB, P.I, P.S
    ar.reset()
    A = ar.alloc
    nbuf = kb.buf
    kcmpT = A(4 * 256, BF16).rearrange("p (g c) -> p g c", g=4)
    vcmp = A(2 * 4 * 128, BF16).rearrange("p (t g d) -> p t g d", t=2, g=4)
    reach = A(16 * 64, F32).rearrange("p (q j) -> p q j", q=16)
    addc = A(16 * 64, F32).rearrange("p (q j) -> p q j", q=16)
    ekt = A(32 * 128, BF16).rearrange("p (k c) -> p k c", k=32)
    ones = A(128, BF16)
    killc = A(256, BF16)
    killcol = A(1, F32)
    zerocol = A(1, F32)
    ctab = A(16, F32)
    TEMP0 = ar.off
    ekt_f = A(32 * 128, F32)
    killc_f = A(256, F32)
    w1 = [A(32 * 128, BF16).rearrange("p (j h) -> p j h", j=32) for _ in range(2)]
    w2 = [A(128, BF16) for _ in range(2)]
    posf = [A(128, F32) for _ in range(2)]
    posT = [A(32, BF16) for _ in range(2)]
    c0 = [A(1, F32) for _ in range(2)]
    b_w1 = [nbuf("w1%d" % i) for i in range(2)]
    b_w2 = [nbuf("w2%d" % i) for i in range(2)]
    b_pos = [nbuf("pos%d" % i) for i in range(2)]
    b_posT = [nbuf("posT%d" % i) for i in range(2)]
    b_c0 = [nbuf("c0%d" % i) for i in range(2)]
    b_kcmp, b_vcmp, b_reach, b_addc, b_ektf, b_ekt, b_ones, b_killcf, b_killc, b_killcol, b_zero, b_ctab = [nbuf(n) for n in (
        "kcmp", "vcmp", "reach", "addc", "ektf", "ekt", "ones", "killcf", "killc", "killcol", "zero", "ctab")]
    for i, (wn1, wn2, pn) in enumerate((("phi_k_w1", "phi_k_w2", "cmp_pos_k"), ("phi_v_w1", "phi_v_w2", "cmp_pos_v"))):
        kb.dma("gpsimd", [lambda e, i=i, wn1=wn1: e.dma_start(out=w1[i], in_=I[wn1].rearrange("(j d) h -> d j h", d=128))],
               b_w1[i], writes=[b_w1[i]])
        kb.dma("gpsimd", [lambda e, i=i, wn2=wn2: e.dma_start(out=w2[i], in_=I[wn2])], b_w2[i], writes=[b_w2[i]])
        kb.dma("sync", [lambda e, i=i, pn=pn: e.dma_start(out=posf[i][0:32, :], in_=I[pn])], b_pos[i], writes=[b_pos[i]])
    kb.dma("sync", [lambda e: e.dma_start(out=reach, in_=I["reach01"])], b_reach, writes=[b_reach])
    kb.dma("sync", [lambda e: e.dma_start(out=addc, in_=I["addc"])], b_addc, writes=[b_addc])
    kb.dma("sync", [lambda e: e.dma_start(out=ekt_f[0:64, :], in_=I["ekt"])], b_ektf, writes=[b_ektf])
    kb.op("vector", lambda e: e.tensor_copy(out=ekt[0:64, :, :], in_=ekt_f[0:64, :].rearrange("p (k c) -> p k c", k=32)),
          reads=[b_ektf], writes=[b_ekt])
    kb.op("vector", lambda e: e.memset(ones, 1.0), writes=[b_ones])
    kb.dma("sync", [lambda e: e.dma_start(out=killc_f[0:1, :], in_=I["killc"])], b_killcf, writes=[b_killcf])
    kb.op("vector", lambda e: e.tensor_copy(out=killc[0:1, :], in_=killc_f[0:1, :]), reads=[b_killcf], writes=[b_killc])
    kb.dma("sync", [lambda e: e.dma_start(out=killcol, in_=I["killcol"])], b_killcol, writes=[b_killcol])
    kb.op("vector", lambda e: e.memset(zerocol, 0.0), writes=[b_zero])
    kb.dma("sync", [lambda e: e.dma_start(out=ctab, in_=I["rel_table"][31, :].partition_broadcast(128))], b_ctab, writes=[b_ctab])

    srcT = [A(NV, BF16) for _ in range(2)]
    b_src = [nbuf("csrc%d" % i) for i in range(2)]
    hid = [A(256, BF16) for _ in range(2)]
    b_hid = [nbuf("hid%d" % i) for i in range(2)]
    for i in range(2):
        kb.op("vector", lambda e, i=i: e.memset(hid[i], 0.0), writes=[b_hid[i]])
        kb.op("tensor", lambda e, i=i: e.matmul(PS[7][:, 0:32], posf[i][0:32, :], P.ident_f[0:32, 0:32], start=True, stop=True),
              reads=[b_pos[i], P.b_ident], writes=[PSB[7]])
        kb.op("vector", lambda e, i=i: e.tensor_copy(out=posT[i], in_=PS[7][:, 0:32]), reads=[PSB[7]], writes=[b_posT[i]])

        def c0fn(e, i=i):
            ins = None
            for j in range(32):
                ins = e.matmul(PS[7][:, 0:1], w1[i][:, j, :], posT[i][:, j:j + 1], start=(j == 0), stop=(j == 31))
            return ins
        kb.op("tensor", c0fn, reads=[b_w1[i], b_posT[i]], writes=[PSB[7]])
        kb.op("vector", lambda e, i=i: e.tensor_copy(out=c0[i], in_=PS[7][:, 0:1]), reads=[PSB[7]], writes=[b_c0[i]])
    cnt = 0
    for g in range(4):
        for i, sname in enumerate(("kc", "vc")):
            sl = cnt % 2
            cnt += 1
            kb.dma("sync", [lambda e, sl=sl, sname=sname, g=g: e.dma_start(out=srcT[sl], in_=S[sname][g * 128:(g + 1) * 128, :])],
                   b_src[sl], writes=[b_src[sl]])
            sv = srcT[sl].rearrange("p (c s) -> p c s", s=16)
            pb = 6 + sl

            def hfn(e, i=i, sv=sv, pb=pb):
                ins = None
                for j in range(32):
                    a, b = j // 16, j % 16
                    ins = e.matmul(PS[pb][:, 0:255], w1[i][:, j, :], sv[:, a:a + 255, b], start=(j == 0), stop=(j == 31))
                return ins
            kb.op("tensor", hfn, reads=[b_w1[i], b_src[sl]], writes=[PSB[pb]])
            kb.op("scalar", lambda e, i=i, pb=pb: e.activation(out=hid[i][:, 0:255], in_=PS[pb][:, 0:255], func=AF.Relu, bias=c0[i], scale=1.0),
                  reads=[PSB[pb], b_c0[i]], writes=[b_hid[i]])
            if i == 0:
                kb.op("tensor", lambda e, pb=pb: e.matmul(PS[pb][:, 0:256], w2[0], hid[0], start=True, stop=True),
                      reads=[b_w2[0], b_hid[0]], writes=[PSB[pb]])
                kb.op("vector", lambda e, g=g, pb=pb: e.tensor_copy(out=kcmpT[:, g, :], in_=PS[pb][:, 0:256]), reads=[PSB[pb]], writes=[b_kcmp])
            else:
                def vfn(e, pb=pb):
                    e.matmul(PS[pb][:, 0:128], hid[1][:, 0:128], w2[1], start=True, stop=True)
                    return e.matmul(PS[pb][:, 128:256], hid[1][:, 128:256], w2[1], start=True, stop=True)
                kb.op("tensor", vfn, reads=[b_w2[1], b_hid[1]], writes=[PSB[pb]])
                kb.op("vector", lambda e, g=g, pb=pb: e.tensor_copy(out=vcmp[:, :, g, :], in_=PS[pb][:, 0:256].rearrange("p (t d) -> p t d", t=2)),
                      reads=[PSB[pb]], writes=[b_vcmp])

    kb.barrier()
    ar.reset(TEMP0)
    kslT = [A(NV, BF16) for _ in range(2)]
    vsl = [A(32 * 128, BF16).rearrange("p (k d) -> p k d", k=32) for _ in range(2)]
    kwT = [A(2560, BF16)] * 2
    vw = [A(20 * 128, BF16).rearrange("p (k d) -> p k d", k=20)] * 2
    nqT = [A(4 * NOWN, BF16).rearrange("p (r t) -> p r t", r=4) for _ in range(2)]
    bfull = [A(4 * 376, F32).rearrange("p (r c) -> p r c", r=4) for _ in range(2)]
    bfullK = [A(4 * 376, F32).rearrange("p (r c) -> p r c", r=4) for _ in range(2)]
    b_bfullK = [nbuf("bfullK%d" % i) for i in range(2)]
    Wm = [[A(512, F32).rearrange("p (r q) -> p r q", r=4) for _ in range(5)]] * 2
    b_ksl, b_vsl, b_nq, b_bfull = [[nbuf(n + str(i)) for i in range(2)] for n in ("kslg", "vslg", "nqg", "bfullg")]
    b_kw = [nbuf("kwg")] * 2
    b_vw = [nbuf("vwg")] * 2
    b_Wm = [[nbuf("Wm%d" % i) for i in range(5)]] * 2
    lgs = A(1024, F32).rearrange("p (r c) -> p r c", r=4)
    Ec = A(1024, F32).rearrange("p (r c) -> p r c", r=4)
    rsum = A(4, F32)
    Pn = A(1024, F32).rearrange("p (r c) -> p r c", r=4)
    Pnb = A(1024, BF16).rearrange("p (r c) -> p r c", r=4)
    PnT = A(1024, BF16).rearrange("p (k q) -> p k q", k=8)
    s01 = A(256, F32)
    s23 = A(256, F32)
    Pp = A(260, F32)
    ia = A(64, F32)
    ib = A(64, F32)
    score = A(64, F32)
    sc2 = A(64, F32)
    mx = A(16, F32)
    sel = A(64, F32)
    negm4 = A(256, BF16).rearrange("p (r j) -> p r j", r=4)
    negmT = [A(512, BF16).rearrange("p (r q) -> p r q", r=4) for _ in range(2)]
    b_lgs, b_Ec, b_rsum, b_Pn, b_Pnb, b_PnT, b_s01, b_s23, b_Pp, b_ia, b_ib, b_score, b_sc2, b_mx, b_sel, b_negm4 = [
        nbuf(n) for n in ("lgs", "Ec", "rsum", "Pn", "Pnb", "PnT", "s01", "s23", "Pp", "ia", "ib", "score", "sc2", "mx", "sel", "negm4")]
    b_negmT = [nbuf("negmT%d" % i) for i in range(2)]
    kb.op("vector", lambda e: e.memset(Pp, 0.0), writes=[b_Pp])
    lgT = [A(512, F32).rearrange("p (r q) -> p r q", r=4) for _ in range(2)]
    b_lgT = [nbuf("lgT%d" % i) for i in range(2)]
    PT = [A(512, BF16) for _ in range(4)]
    b_PT = [nbuf("PT%d" % i) for i in range(4)]
    gt = [A(12 * 128, BF16).rearrange("p (k q) -> p k q", k=12) for _ in range(3)]
    b_gt = [nbuf("gt%d" % i) for i in range(3)]
    acc = [A(512, F32).rearrange("p (r q) -> p r q", r=4) for _ in range(3)]
    b_acc = [nbuf("acc%d" % i) for i in range(3)]
    rinv = [A(512, F32).rearrange("p (r q) -> p r q", r=4) for _ in range(2)]
    b_rinv = [nbuf("rinv%d" % i) for i in range(2)]
    tmp = A(512, F32).rearrange("p (r q) -> p r q", r=4)
    b_tmp = nbuf("tmp")
    obf = [A(512, BF16).rearrange("p (r q) -> p r q", r=4) for _ in range(3)]
    b_obf = [nbuf("obf%d" % i) for i in range(3)]
    print("NSA arena cols used", ar.off)
    SB = [0, 1, 4]
    smb = [A(512, F32).rearrange("p (r q) -> p r q", r=4) for _ in range(2)]
    ocb = [A(512, F32).rearrange("p (r q) -> p r q", r=4) for _ in range(2)]
    b_smb = [nbuf("smb%d" % i) for i in range(2)]
    b_ocb = [nbuf("ocb%d" % i) for i in range(2)]
    st = {"l": 0, "p": 0, "s": 0}
    deferred = []

    def load_group(g):
        s_ = g % 2
        kb.dma("sync", [lambda e: e.dma_start(out=nqT[s_], in_=S["nq"][g * 512:(g + 1) * 512, :].rearrange("(r d) t -> d r t", d=128))],
               b_nq[s_], writes=[b_nq[s_]])
        kb.dma("sync", [lambda e: e.dma_start(out=bfull[s_], in_=I["bfull"][g * 4:(g + 1) * 4].rearrange("r i c -> i r c"))],
               b_bfull[s_], writes=[b_bfull[s_]])
        kb.op("vector", lambda e: e.tensor_scalar(out=bfullK[s_], in0=bfull[s_], scalar1=killcol, scalar2=None, op0=ALU.add),
              reads=[b_bfull[s_], b_killcol], writes=[b_bfullK[s_]])
        kb.dma("sync", [lambda e: e.dma_start(out=kslT[s_], in_=S["ksl"][g * 128:(g + 1) * 128, :])], b_ksl[s_], writes=[b_ksl[s_]])
        kb.dma("sync", [lambda e: e.dma_start(out=vsl[s_], in_=S["vsl"][:, g * 128:(g + 1) * 128].rearrange("(k p) d -> p k d", p=128))],
               b_vsl[s_], writes=[b_vsl[s_]])

    def load_group_late(g):
        s_ = g % 2
        kb.dma("sync", [lambda e: e.dma_start(out=kwT[s_], in_=S["kw"][g * 128:(g + 1) * 128, :])], b_kw[s_], writes=[b_kw[s_]])
        kb.dma("sync", [lambda e: e.dma_start(out=vw[s_], in_=S["vw"][:, g * 128:(g + 1) * 128].rearrange("(k p) d -> p k d", p=128))],
               b_vw[s_], writes=[b_vw[s_]])
        for wi, wname in enumerate(("w0", "w1", "w4")):
            kb.dma("sync", [lambda e, wi=wi, wname=wname: e.dma_start(out=Wm[s_][wi], in_=I[wname][g * 4:(g + 1) * 4].rearrange("r k q -> k r q"))],
                   b_Wm[s_][wi], writes=[b_Wm[s_][wi]])
        cb4 = ctab[:, 4 * g:4 * g + 4].unsqueeze(2).to_broadcast([128, 4, 128])
        for wi in range(2):
            kb.op("vector", lambda e, wi=wi: e.tensor_tensor(out=Wm[s_][3 + wi], in0=Wm[s_][wi], in1=cb4, op=ALU.subtract),
                  reads=[b_Wm[s_][wi], b_ctab], writes=[b_Wm[s_][3 + wi]])

    def stage1(it):
        g, qi = it // 16, it % 16
        s_ = g % 2
        a_ = it % 2
        a3 = it % 3
        qv = 16 + qi
        q0 = qi * 128
        gsl = gt[a3]
        nq_, bf_ = nqT[s_], bfull[s_]
        nmT = negmT[a_]
        kb.dma("sync", [lambda e: e.dma_start(out=gsl, in_=S["ng"][g * 12:(g + 1) * 12, q0:q0 + 128].partition_broadcast(128))],
               b_gt[a3], writes=[b_gt[a3]])
        gview = gsl.rearrange("p (r i) q -> p r i q", i=3)
        for half in range(2):
            pb = 7

            def lfn(e, pb=pb, half=half):
                ins = None
                for rr in range(2):
                    r = half * 2 + rr
                    ins = e.matmul(PS[pb][:, rr * 256:(rr + 1) * 256], nq_[:, r, q0:q0 + 128], kcmpT[:, g, :], start=True, stop=True)
                return ins
            kb.op("tensor", lfn, reads=[b_nq[s_], b_kcmp], writes=[PSB[pb]])
            cc0 = 248 - 8 * qv
            bk_ = bfullK[s_]
            kb.op("vector", lambda e, pb=pb, half=half, cc0=cc0: e.tensor_tensor(
                out=lgs[:, half * 2:half * 2 + 2, 0:128], in0=PS[pb][:, :].rearrange("p (r c) -> p r c", r=2)[:, :, 0:128],
                in1=bk_[:, half * 2:half * 2 + 2, cc0:cc0 + 128], op=ALU.add),
                reads=[PSB[pb], b_bfullK[s_]], writes=[b_lgs])
            kb.op("vector", lambda e, pb=pb, half=half, cc0=cc0: e.tensor_tensor(
                out=lgs[:, half * 2:half * 2 + 2, 128:256], in0=PS[pb][:, :].rearrange("p (r c) -> p r c", r=2)[:, :, 128:256],
                in1=bf_[:, half * 2:half * 2 + 2, cc0 + 128:cc0 + 256], op=ALU.add),
                reads=[PSB[pb], b_bfull[s_]], writes=[b_lgs])
            yield
        for r in range(4):
            kb.op("scalar", lambda e, r=r: e.activation(out=Ec[:, r, :], in_=lgs[:, r, :], func=AF.Exp, accum_out=rsum[:, r:r + 1]),
                  reads=[b_lgs], writes=[b_Ec, b_rsum])
            yield
        kb.op("vector", lambda e: e.tensor_scalar(out=rsum, in0=rsum, scalar1=1e-30, scalar2=None, op0=ALU.max), reads=[b_rsum], writes=[b_rsum])
        kb.op("vector", lambda e: e.reciprocal(out=rsum, in_=rsum), reads=[b_rsum], writes=[b_rsum])
        kb.op("vector", lambda e: e.tensor_tensor(out=Pn, in0=Ec, in1=rsum.unsqueeze(2).to_broadcast([128, 4, 256]), op=ALU.mult),
              reads=[b_Ec, b_rsum], writes=[b_Pn])
        yield
        kb.op("gpsimd", lambda e: e.tensor_copy(out=Pnb, in_=Pn), reads=[b_Pn], writes=[b_Pnb])
        kb.op("vector", lambda e: e.tensor_tensor(out=s01, in0=Pn[:, 0, :], in1=Pn[:, 1, :], op=ALU.add), reads=[b_Pn], writes=[b_s01])
        yield
        kb.op("vector", lambda e: e.tensor_tensor(out=s23, in0=Pn[:, 2, :], in1=Pn[:, 3, :], op=ALU.add), reads=[b_Pn], writes=[b_s23])
        yield
        kb.op("vector", lambda e: e.tensor_tensor(out=Pp[:, 1:257], in0=s01, in1=s23, op=ALU.add), reads=[b_s01, b_s23], writes=[b_Pp])
        p6b = PS[7].bitcast(BF16)
        p7b = PS[7].bitcast(BF16)

        def t6(e):
            ins = None
            for r in range(4):
                for ct in range(2):
                    k = r * 2 + ct
                    ins = e.transpose(p6b[:, k * 128:(k + 1) * 128], Pnb[:, r, ct * 128:(ct + 1) * 128], P.ident)
            return ins
        yield
        kb.op("tensor", t6, reads=[b_Pnb, P.b_ident], writes=[PSB[7]])
        yield
        kb.op("scalar", lambda e: e.copy(out=PnT, in_=p6b.rearrange("p (k q) -> p k q", k=8)), reads=[PSB[7]], writes=[b_PnT])
        v0 = Pp[:, 0:256].rearrange("p (j m) -> p j m", m=4)
        v1 = Pp[:, 4:260].rearrange("p (j m) -> p j m", m=4)
        kb.op("vector", lambda e: e.tensor_tensor(out=ia, in0=v0[:, :, 1], in1=v0[:, :, 2], op=ALU.add), reads=[b_Pp], writes=[b_ia])
        kb.op("vector", lambda e: e.tensor_tensor(out=ib, in0=v0[:, :, 0], in1=v1[:, :, 0], op=ALU.add), reads=[b_Pp], writes=[b_ib])
        yield
        kb.op("vector", lambda e: e.tensor_tensor(out=ia, in0=ia, in1=v0[:, :, 3], op=ALU.add), reads=[b_Pp, b_ia], writes=[b_ia])

        def ocfn(e):
            ins = None
            for r in range(4):
                for ct in range(2):
                    ins = e.matmul(PS[7][:, r * 128:(r + 1) * 128], vcmp[:, ct, g, :], PnT[:, r * 2 + ct, :], start=(ct == 0), stop=(ct == 1))
            return ins
        yield
        kb.op("tensor", ocfn, reads=[b_vcmp, b_PnT], writes=[PSB[7]])
        yield
        kb.op("vector", lambda e: e.scalar_tensor_tensor(out=score, in0=ia, scalar=2.0, in1=ib, op0=ALU.mult, op1=ALU.add),
              reads=[b_ia, b_ib], writes=[b_score])
        ac = acc[a3]
        kb.op("vector", lambda e: e.tensor_tensor(out=ac, in0=PS[7][:, :].rearrange("p (r q) -> p r q", r=4), in1=gview[:, :, 0, :], op=ALU.mult),
              reads=[PSB[7], b_gt[a3]], writes=[b_acc[a3]])
        yield
        kb.op("vector", lambda e: e.tensor_tensor(out=score, in0=score, in1=reach[:, qi, :], op=ALU.mult), reads=[b_score, b_reach], writes=[b_score])
        kb.op("vector", lambda e: e.tensor_tensor(out=score, in0=score, in1=addc[:, qi, :], op=ALU.add), reads=[b_score, b_addc], writes=[b_score])
        yield
        kb.op("vector", lambda e: e.max(out=mx[:, 0:8], in_=score), reads=[b_score], writes=[b_mx])
        yield
        kb.op("vector", lambda e: e.match_replace(out=sc2, in_to_replace=mx[:, 0:8], in_values=score, imm_value=-3e4),
              reads=[b_score, b_mx], writes=[b_sc2])
        yield
        kb.op("vector", lambda e: e.max(out=mx[:, 8:16], in_=sc2), reads=[b_sc2], writes=[b_mx])
        yield
        kb.op("vector", lambda e: e.tensor_scalar(out=sel, in0=score, scalar1=mx[:, 15:16], scalar2=None, op0=ALU.is_ge),
              reads=[b_score, b_mx], writes=[b_sel])
        kb.op("vector", lambda e: e.tensor_scalar(out=sel, in0=sel, scalar1=-1.0, scalar2=32768.0, op0=ALU.add, op1=ALU.mult),
              reads=[b_sel], writes=[b_sel])
        yield
        for r in range(4):
            kb.op("vector", lambda e, r=r: e.tensor_scalar(out=negm4[:, r, :], in0=sel, scalar1=ctab[:, 4 * g + r:4 * g + r + 1], scalar2=None, op0=ALU.add),
                  reads=[b_sel, b_ctab], writes=[b_negm4])
        yield

        def t7(e):
            ins = None
            for r in range(4):
                ins = e.transpose(p7b[0:64, r * 128:(r + 1) * 128], negm4[:, r, :], P.ident)
            return ins
        yield
        kb.op("tensor", t7, reads=[b_negm4, P.b_ident], writes=[PSB[7]])
        yield
        kb.op("vector", lambda e: e.tensor_copy(out=nmT[0:64, :, :], in_=p7b[0:64, 0:512].rearrange("p (r q) -> p r q", r=4)),
              reads=[PSB[7]], writes=[b_negmT[a_]])
        yield

    def stage2(it, tick):
        g, qi = it // 16, it % 16
        s_ = g % 2
        a_ = it % 2
        a3 = it % 3
        qv = 16 + qi
        q0 = qi * 128
        gview = gt[a3].rearrange("p (r i) q -> p r i q", i=3)
        ac = acc[a3]
        nq_ = nqT[s_]
        nmT = negmT[a_]
        cb4 = ctab[:, 4 * g:4 * g + 4].unsqueeze(2).to_broadcast([128, 4, 128])
        W_ = Wm[s_]
        bW_ = b_Wm[s_]
        for br in range(2):
            po, psm = (2, 3) if (2 * it + br) % 2 == 0 else (5, 6)
            kts = list(range(0, qv + 1)) if br == 0 else list(range(qv - 4, qv + 1))
            sbs = {}

            def qk(idx):
                kt = kts[idx]
                sb = SB[st["s"] % 3]
                st["s"] += 1
                sbs[idx] = sb
                if br == 0:
                    def sfn(e):
                        e.matmul(PS[sb][:, :], kslT[s_][:, kt * 128:(kt + 1) * 128], nq_[:, :, q0:q0 + 128], start=True, stop=False)
                        return e.matmul(PS[sb][:, :], ekt[0:64, kt, :], nmT[0:64, :, :], start=False, stop=True)
                    kb.op("tensor", sfn, reads=[b_ksl[s_], b_nq[s_], b_ekt, b_negmT[a_]], writes=[PSB[sb]])
                else:
                    lt = kt - 12
                    kb.op("tensor", lambda e: e.matmul(PS[sb][:, :], kwT[s_][:, lt * 128:(lt + 1) * 128], nq_[:, :, q0:q0 + 128], start=True, stop=True),
                          reads=[b_kw[s_], b_nq[s_]], writes=[PSB[sb]])
            LA = 2
            for idx in range(min(LA, len(kts))):
                qk(idx)
            for idx, kt in enumerate(kts):
                if idx + LA < len(kts):
                    qk(idx + LA)
                off = qv - kt
                sb = sbs[idx]
                first, last = (idx == 0), (idx == len(kts) - 1)
                bcol, b_bcol = (killcol, b_killcol) if kt < 16 else (zerocol, b_zero)
                pt = PT[st["p"] % 4]
                b_pt = b_PT[st["p"] % 4]
                st["p"] += 1
                addm = None
                if br == 0 and off <= 1:
                    addm = (W_[3 + off], bW_[3 + off])
                elif br == 1:
                    if off == 0:
                        addm = (W_[0], bW_[0])
                    elif off == 1:
                        addm = (W_[1], bW_[1])
                    elif off == 4:
                        addm = (W_[2], bW_[2])
                    else:
                        addm = (cb4, b_ctab)
                if addm is not None:
                    lg = lgT[st["l"] % 2]
                    b_lg = b_lgT[st["l"] % 2]
                    st["l"] += 1
                    kb.op("vector", lambda e, lg=lg, sb=sb, addm=addm: e.tensor_tensor(out=lg, in0=PS[sb][:, :].rearrange("p (r q) -> p r q", r=4), in1=addm[0], op=ALU.add),
                          reads=[PSB[sb], addm[1]], writes=[b_lg])
                    kb.op("scalar", lambda e, pt=pt, lg=lg, bcol=bcol: e.activation(out=pt, in_=lg.rearrange("p r q -> p (r q)"), func=AF.Exp, bias=bcol, scale=1.0),
                          reads=[b_lg, b_bcol], writes=[b_pt])
                else:
                    kb.op("scalar", lambda e, pt=pt, sb=sb, bcol=bcol: e.activation(out=pt, in_=PS[sb][:, :], func=AF.Exp, bias=bcol, scale=1.0),
                          reads=[PSB[sb], b_bcol], writes=[b_pt])
                if br == 0:
                    vv, b_vv = vsl[s_][:, kt, :], b_vsl[s_]
                else:
                    vv, b_vv = vw[s_][:, kt - 12, :], b_vw[s_]

                def pvfn(e, vv=vv, pt=pt, first=first, last=last, po=po, psm=psm):
                    e.matmul(PS[po][:, :], vv, pt, start=first, stop=last)
                    return e.matmul(PS[psm][:, :], ones, pt, start=first, stop=last)
                kb.op("tensor", pvfn, reads=[b_vv, b_pt, b_ones], writes=[PSB[po], PSB[psm]])
                if idx == (3 if br == 0 else len(kts) - 1):
                    while deferred:
                        deferred.pop(0)()
                tick()
            sm, oc = smb[br], ocb[br]
            kb.op("scalar", lambda e, sm=sm, psm=psm: e.copy(out=sm, in_=PS[psm][:, :].rearrange("p (r q) -> p r q", r=4)),
                  reads=[PSB[psm]], writes=[b_smb[br]])
            kb.op("scalar", lambda e, oc=oc, po=po: e.copy(out=oc, in_=PS[po][:, :].rearrange("p (r q) -> p r q", r=4)),
                  reads=[PSB[po]], writes=[b_ocb[br]])

            def fin(br=br, sm=sm, oc=oc):
                kb.op("vector", lambda e: e.reciprocal(out=sm, in_=sm), reads=[b_smb[br]], writes=[b_smb[br]])
                kb.op("gpsimd", lambda e: e.tensor_tensor(out=oc, in0=oc, in1=gview[:, :, 1 + br, :], op=ALU.mult),
                      reads=[b_ocb[br], b_gt[a3]], writes=[b_ocb[br]])
                kb.op("gpsimd", lambda e: e.tensor_tensor(out=oc, in0=oc, in1=sm, op=ALU.mult),
                      reads=[b_ocb[br], b_smb[br]], writes=[b_ocb[br]])
                if br == 0:
                    kb.op("gpsimd", lambda e: e.tensor_tensor(out=ac, in0=ac, in1=oc, op=ALU.add), reads=[b_ocb[br], b_acc[a3]], writes=[b_acc[a3]])
                else:
                    ob_ = obf[a3]
                    kb.op("gpsimd", lambda e: e.tensor_tensor(out=ob_, in0=ac, in1=oc, op=ALU.add), reads=[b_ocb[br], b_acc[a3]], writes=[b_obf[a3]])
                    kb.dma("sync", [lambda e: e.dma_start(
                        out=S["oall"][2048 + g * 512:2048 + (g + 1) * 512, q0:q0 + 128].rearrange("(r d) q -> d r q", d=128), in_=ob_)],
                        b_obf[a3], reads=[b_obf[a3]])
            deferred.append(fin)

    load_group(0)
    for _ in stage1(0):
        pass
    for it in range(64):
        if it % 16 == 0 and it // 16 + 1 < 4:
            load_group(it // 16 + 1)
        gen = stage1(it + 1) if it + 1 < 64 else None
        state = {"gen": gen}

        def tick():
            if state["gen"] is not None:
                try:
                    next(state["gen"])
                except StopIteration:
                    state["gen"] = None
        if it % 16 == 0:
            load_group_late(it // 16)
        stage2(it, tick)
        while state["gen"] is not None:
            tick()
    while deferred:
        deferred.pop(0)()
    kb.barrier()


def phase_proj(P):
    kb, ar, PS, PSB, I, S = P.kb, P.ar, P.PS, P.PSB, P.I, P.S
    ar.reset()
    oT = ar.alloc(32 * NOWN, BF16).rearrange("p (k t) -> p k t", k=32)
    b_oT = kb.buf("oT")
    kb.dma("sync", [lambda e, i=i: e.dma_start(out=oT[:, i * 8:(i + 1) * 8, :],
                                               in_=S["oall"][i * 1024:(i + 1) * 1024, :].rearrange("(k p) t -> p k t", p=128)) for i in range(4)],
           b_oT, writes=[b_oT])
    WS = [ar.alloc(32 * 256, BF16).rearrange("p (k c) -> p k c", k=32) for _ in range(2)]
    b_ws = [kb.buf("pws%d" % i) for i in range(2)]
    SG = [ar.alloc(NOWN, BF16) for _ in range(2)]
    SN = [ar.alloc(NOWN, BF16) for _ in range(2)]
    b_sg = [kb.buf("sg%d" % i) for i in range(2)]
    b_sn = [kb.buf("sn%d" % i) for i in range(2)]
    OS = [ar.alloc(NOWN, BF16) for _ in range(2)]
    b_os = [kb.buf("pos%d" % i) for i in range(2)]
    T1 = [ar.alloc(512, F32) for _ in range(2)]
    T2 = [ar.alloc(512, F32) for _ in range(2)]
    b_t1 = [kb.buf("t1%d" % i) for i in range(2)]
    b_t2 = [kb.buf("t2%d" % i) for i in range(2)]
    wc = 0
    fc = 0
    pc = 0
    tc = 0
    for w0 in range(0, D, 256):
        ws, bw = WS[wc % 2], b_ws[wc % 2]
        wc += 1
        fns = []
        for i in range(2):
            fns.append(lambda e, ws=ws, i=i, w0=w0: e.dma_start(
                out=ws[:, i * 8:(i + 1) * 8, :], in_=I["w_gla_proj"][i * 1024:(i + 1) * 1024, w0:w0 + 256].rearrange("(k p) c -> p k c", p=128)))
        for i in range(2):
            fns.append(lambda e, ws=ws, i=i, w0=w0: e.dma_start(
                out=ws[:, 16 + i * 8:16 + (i + 1) * 8, :], in_=I["w_nsa_proj"][i * 1024:(i + 1) * 1024, w0:w0 + 256].rearrange("(k p) c -> p k c", p=128)))
        kb.dma("gpsimd", fns, bw, writes=[bw])
        for cb in (0, 128):
            f0 = w0 + cb
            fs = fc % 2
            fc += 1
            kb.dma("sync", [lambda e, fs=fs, f0=f0: e.dma_start(out=SG[fs], in_=S["mg"][f0:f0 + 128, :])], b_sg[fs], writes=[b_sg[fs]])
            kb.dma("sync", [lambda e, fs=fs, f0=f0: e.dma_start(out=SN[fs], in_=S["mn"][f0:f0 + 128, :])], b_sn[fs], writes=[b_sn[fs]])
            for pair in range(2):
                base = 4 * (pc % 2)
                pc += 1
                chs = (2 * pair, 2 * pair + 1)

                def mm(e, ws=ws, cb=cb, base=base, chs=chs):
                    ins = None
                    for kt in range(32):
                        for ci, ch in enumerate(chs):
                            bi = base + ci + (0 if kt < 16 else 2)
                            ins = e.matmul(PS[bi][:, :], ws[:, kt, cb:cb + 128], oT[:, kt, ch * 512:(ch + 1) * 512],
                                           start=(kt % 16 == 0), stop=(kt % 16 == 15))
                    return ins
                kb.op("tensor", mm, reads=[bw, b_oT], writes=[PSB[base + i] for i in range(4)])
                for ci, ch in enumerate(chs):
                    ts = tc % 2
                    tc += 1
                    csl = slice(ch * 512, (ch + 1) * 512)
                    kb.op("vector", lambda e, ts=ts, base=base, ci=ci, fs=fs, csl=csl: e.tensor_tensor(
                        out=T1[ts], in0=PS[base + ci][:, :], in1=SG[fs][:, csl], op=ALU.mult),
                        reads=[PSB[base + ci], b_sg[fs]], writes=[b_t1[ts]])
                    kb.op("vector", lambda e, ts=ts, base=base, ci=ci, fs=fs, csl=csl: e.tensor_tensor(
                        out=T2[ts], in0=PS[base + 2 + ci][:, :], in1=SN[fs][:, csl], op=ALU.mult),
                        reads=[PSB[base + 2 + ci], b_sn[fs]], writes=[b_t2[ts]])
                    kb.op("vector", lambda e, ts=ts, fs=fs, csl=csl: e.tensor_tensor(out=OS[fs][:, csl], in0=T1[ts], in1=T2[ts], op=ALU.add),
                          reads=[b_t1[ts], b_t2[ts]], writes=[b_os[fs]])
            kb.dma("scalar", [lambda e, fs=fs, f0=f0: e.dma_start(out=S["mix"][f0:f0 + 128, :], in_=OS[fs])], b_os[fs], reads=[b_os[fs]])
    kb.barrier()


def phase_out(P):
    kb, ar, PS, PSB, I, S = P.kb, P.ar, P.PS, P.PSB, P.I, P.S
    ar.reset()
    mT = ar.alloc(32 * NOWN, BF16).rearrange("p (k t) -> p k t", k=32)
    b_mT = kb.buf("mT")
    kb.dma("sync", [lambda e, i=i: e.dma_start(out=mT[:, i * 8:(i + 1) * 8, :],
                                               in_=S["mix"][i * 1024:(i + 1) * 1024, :].rearrange("(k p) t -> p k t", p=128)) for i in range(4)],
           b_mT, writes=[b_mT])
    WS = [ar.alloc(32 * 512, BF16).rearrange("p (k c) -> p k c", k=32) for _ in range(2)]
    b_ws = [kb.buf("ows%d" % i) for i in range(2)]
    XS = [ar.alloc(512, F32) for _ in range(2)]
    b_xs = [kb.buf("oxs%d" % i) for i in range(2)]
    OS = [ar.alloc(512, F32) for _ in range(2)]
    b_os = [kb.buf("oos%d" % i) for i in range(2)]
    cnt = 0
    for cbk in range(8):
        c0 = cbk * 512
        ws, bw = WS[cbk % 2], b_ws[cbk % 2]
        kb.dma("gpsimd", [lambda e, ws=ws, i=i, c0=c0: e.dma_start(
            out=ws[:, i * 8:(i + 1) * 8, :], in_=I["w_out"][i * 1024:(i + 1) * 1024, c0:c0 + 512].rearrange("(k p) c -> p k c", p=128)) for i in range(4)],
            bw, writes=[bw])
        for tt in range(16):
            bi = cnt % 8
            sl = cnt % 2
            cnt += 1

            def mm(e, ws=ws, tt=tt, bi=bi):
                ins = None
                for kt in range(32):
                    ins = e.matmul(PS[bi][:, :], mT[:, kt, tt * 128:(tt + 1) * 128], ws[:, kt, :], start=(kt == 0), stop=(kt == 31))
                return ins
            kb.op("tensor", mm, reads=[bw, b_mT], writes=[PSB[bi]])
            kb.dma("sync", [lambda e, sl=sl, tt=tt, c0=c0: e.dma_start(out=XS[sl], in_=I["x_own"][tt * 128:(tt + 1) * 128, c0:c0 + 512])],
                   b_xs[sl], writes=[b_xs[sl]])
            kb.op("vector", lambda e, sl=sl, bi=bi: e.tensor_tensor(out=OS[sl], in0=PS[bi][:, :], in1=XS[sl], op=ALU.add),
                  reads=[PSB[bi], b_xs[sl]], writes=[b_os[sl]])
            kb.dma("scalar", [lambda e, sl=sl, tt=tt, c0=c0: e.dma_start(out=S["x2"][tt * 128:(tt + 1) * 128, c0:c0 + 512], in_=OS[sl])],
                   b_os[sl], reads=[b_os[sl]])
    kb.barrier()


def phase_mlp(P):
    kb, ar, PS, PSB, I, S = P.kb, P.ar, P.PS, P.PSB, P.I, P.S
    ar.reset()
    hT = ar.alloc(32 * NOWN, BF16).rearrange("p (k t) -> p k t", k=32)
    R0 = ar.off
    rmsnorm_transpose(P, S["x2"], I["g_mlp"], hT, None, 16, R0)
    kb.barrier()
    ar.reset(R0)
    WS = [ar.alloc(32 * 512, BF16).rearrange("p (k c) -> p k c", k=32) for _ in range(2)]
    b_ws = [kb.buf("mws%d" % i) for i in range(2)]
    OS = [ar.alloc(NOWN, BF16) for _ in range(2)]
    b_os = [kb.buf("mos%d" % i) for i in range(2)]
    hid5 = S["hid"]
    oc = 0
    rc = 0
    RL = None
    for w0 in range(0, 4 * D, 512):
        ws, bw = WS[(w0 // 512) % 2], b_ws[(w0 // 512) % 2]
        kb.dma("gpsimd", [lambda e, ws=ws, i=i, w0=w0: e.dma_start(
            out=ws[:, i * 8:(i + 1) * 8, :], in_=I["w_up"][i * 1024:(i + 1) * 1024, w0:w0 + 512].rearrange("(k p) c -> p k c", p=128)) for i in range(4)],
            bw, writes=[bw])
        for cb in range(0, 512, 128):
            half = (cb // 128) % 2
            banks = [half * 4 + i for i in range(4)]

            def mm(e, ws=ws, cb=cb, banks=banks):
                ins = None
                for kt in range(32):
                    for ch, bi in enumerate(banks):
                        ins = e.matmul(PS[bi][:, :], ws[:, kt, cb:cb + 128], hT[:, kt, ch * 512:(ch + 1) * 512], start=(kt == 0), stop=(kt == 31))
                return ins
            kb.op("tensor", mm, reads=[bw], writes=[PSB[b] for b in banks])
            osl = oc % 2
            oc += 1
            for ch, bi in enumerate(banks):
                o_ap = OS[osl][:, ch * 512:(ch + 1) * 512]
                kb.op("scalar", lambda e, o_ap=o_ap, bi=bi: e.activation(out=o_ap, in_=PS[bi][:, :], func=AF.Relu),
                      reads=[PSB[bi]], writes=[b_os[osl]])
                kb.op("vector", lambda e, o_ap=o_ap: e.tensor_tensor(out=o_ap, in0=o_ap, in1=o_ap, op=ALU.mult),
                      reads=[b_os[osl]], writes=[b_os[osl]])
            kff = (w0 + cb) // 128
            kb.dma("sync", [lambda e, osl=osl, kff=kff: e.dma_start(
                out=hid5[:, :, kff, :].rearrange("t p q -> p t q"), in_=OS[osl].rearrange("p (t q) -> p t q", t=16))],
                b_os[osl], reads=[b_os[osl]])
    kb.barrier()
    ar.reset()
    WD = [ar.alloc(64 * 512, BF16).rearrange("p (k c) -> p k c", k=64) for _ in range(2)]
    b_wd = [kb.buf("wd%d" % i) for i in range(2)]
    HS = [ar.alloc(64 * 128, BF16).rearrange("p (k q) -> p k q", k=64) for _ in range(2)]
    b_hs = [kb.buf("hs%d" % i) for i in range(2)]
    ACC = ar.alloc(16 * 512, F32).rearrange("p (t c) -> p t c", t=16)
    b_acc = [kb.buf("dacc%d" % i) for i in range(16)]
    XS = [ar.alloc(512, F32) for _ in range(2)]
    b_xs = [kb.buf("dxs%d" % i) for i in range(2)]
    OS2 = [ar.alloc(512, F32) for _ in range(2)]
    b_os2 = [kb.buf("dos%d" % i) for i in range(2)]
    print("down arena cols used", ar.off)
    units = [(cbk, kh) for cbk in range(8) for kh in range(2)]

    def load_w(u):
        cbk, kh = units[u]
        c0 = cbk * 512
        sl = u % 2
        kb.dma("gpsimd", [lambda e, i=i: e.dma_start(
            out=WD[sl][:, i * 8:(i + 1) * 8, :],
            in_=I["w_down"][kh * 8192 + i * 1024:kh * 8192 + (i + 1) * 1024, c0:c0 + 512].rearrange("(k p) c -> p k c", p=128)) for i in range(8)],
            b_wd[sl], writes=[b_wd[sl]])
    jobs = [(u, tt) for u in range(16) for tt in range(16)]

    def load_h(j):
        u, tt = jobs[j]
        kh = units[u][1]
        sl = j % 2
        kb.dma("sync", [lambda e: e.dma_start(out=HS[sl], in_=hid5[tt][:, kh * 64:(kh + 1) * 64, :])], b_hs[sl], writes=[b_hs[sl]])
        if kh == 0:
            cbk = units[u][0]
            kb.dma("sync", [lambda e: e.dma_start(out=XS[sl], in_=S["x2"][tt * 128:(tt + 1) * 128, cbk * 512:(cbk + 1) * 512])],
                   b_xs[sl], writes=[b_xs[sl]])
    load_w(0)
    load_h(0)
    for j, (u, tt) in enumerate(jobs):
        cbk, kh = units[u]
        c0 = cbk * 512
        if tt == 0 and u + 1 < 16:
            load_w(u + 1)
        if j + 1 < len(jobs):
            load_h(j + 1)
        bi = j % 8
        sl = j % 2
        wsl = u % 2

        def mm(e, sl=sl, bi=bi, wsl=wsl):
            ins = None
            for kt in range(64):
                ins = e.matmul(PS[bi][:, :], HS[sl][:, kt, :], WD[wsl][:, kt, :], start=(kt == 0), stop=(kt == 63))
            return ins
        kb.op("tensor", mm, reads=[b_wd[wsl], b_hs[sl]], writes=[PSB[bi]])
        if kh == 0:
            kb.op("vector", lambda e, sl=sl, bi=bi, tt=tt: e.tensor_tensor(out=ACC[:, tt, :], in0=PS[bi][:, :], in1=XS[sl], op=ALU.add),
                  reads=[PSB[bi], b_xs[sl]], writes=[b_acc[tt]])
        else:
            kb.op("vector", lambda e, sl=sl, bi=bi, tt=tt: e.tensor_tensor(out=OS2[sl], in0=PS[bi][:, :], in1=ACC[:, tt, :], op=ALU.add),
                  reads=[PSB[bi], b_acc[tt]], writes=[b_os2[sl]])
            kb.dma("scalar", [lambda e, sl=sl, tt=tt, c0=c0: e.dma_start(out=S["x3"][tt * 128:(tt + 1) * 128, c0:c0 + 512], in_=OS2[sl])],
                   b_os2[sl], reads=[b_os2[sl]])
    kb.barrier()


def phase_final(P):
    kb, ar, PS, PSB, I, S = P.kb, P.ar, P.PS, P.PSB, P.I, P.S
    ar.reset()
    xs = [ar.alloc(D, F32) for _ in range(2)]
    b_xs = [kb.buf("fxs%d" % i) for i in range(2)]
    ys = [ar.alloc(D, F32) for _ in range(2)]
    b_ys = [kb.buf("fys%d" % i) for i in range(2)]
    gb = ar.alloc(D, F32)
    b_gb = kb.buf("fgb")
    st = ar.alloc(64, F32)
    b_st = [kb.buf("fst%d" % i) for i in range(2)]
    kb.dma("sync", [lambda e: e.dma_start(out=gb, in_=I["g_final"].partition_broadcast(128))], b_gb, writes=[b_gb])
    for t in range(16):
        s = t % 2
        xt, yt = xs[s], ys[s]
        ss = st[:, 2 * s:2 * s + 1]
        rs = st[:, 2 * s + 1:2 * s + 2]
        kb.dma("sync", [lambda e, xt=xt, t=t: e.dma_start(out=xt, in_=S["x3"][t * 128:(t + 1) * 128, :])], b_xs[s], writes=[b_xs[s]])
        kb.op("scalar", lambda e, xt=xt, yt=yt, ss=ss: e.activation(out=yt, in_=xt, func=AF.Square, accum_out=ss),
              reads=[b_xs[s]], writes=[b_ys[s], b_st[s]])
        kb.op("vector", lambda e, ss=ss, rs=rs: e.tensor_scalar(out=rs, in0=ss, scalar1=1.0 / D, scalar2=EPS, op0=ALU.mult, op1=ALU.add),
              reads=[b_st[s]], writes=[b_st[s]])
        kb.op("scalar", lambda e, rs=rs: e.sqrt(out=rs, in_=rs), reads=[b_st[s]], writes=[b_st[s]])
        kb.op("vector", lambda e, rs=rs: e.reciprocal(out=rs, in_=rs), reads=[b_st[s]], writes=[b_st[s]])
        kb.op("vector", lambda e, xt=xt, yt=yt, rs=rs: e.scalar_tensor_tensor(out=yt, in0=xt, scalar=rs, in1=gb, op0=ALU.mult, op1=ALU.mult),
              reads=[b_xs[s], b_st[s], b_gb], writes=[b_ys[s]])
        kb.dma("sync", [lambda e, yt=yt, t=t: e.dma_start(out=P.out[t * 128:(t + 1) * 128, :], in_=yt)], b_ys[s], reads=[b_ys[s]])
    kb.barrier()


import math


def _rel_bucket_np(dist):
    n = np.maximum(dist, 0)
    nf = np.maximum(n, 16).astype(np.float32)
    large = 16 + (np.log(nf / np.float32(16)) / np.float32(math.log(128 / 16)) * np.float32(16)).astype(np.int32)
    large = np.minimum(large, 31)
    return np.where(n < 16, n, large).astype(np.int64)


def nsa_consts(rel_table, half):
    tab_ext = np.concatenate([np.asarray(rel_table, np.float32), np.full((1, 16), NEG, np.float32)], 0)
    i = np.arange(128)[:, None]
    cc = np.arange(376)[None, :]
    dist = i - 16 * (cc - 248) - 31
    idx = np.where(dist >= 0, _rel_bucket_np(dist), 32)
    bfull = np.ascontiguousarray(tab_ext[idx].transpose(2, 0, 1))
    key = np.arange(128)[:, None]
    q = np.arange(128)[None, :]
    d0 = q - key
    w0 = np.ascontiguousarray(tab_ext[np.where(d0 >= 0, _rel_bucket_np(d0), 32)].transpose(2, 0, 1))
    w1 = np.ascontiguousarray(tab_ext[_rel_bucket_np(128 + q - key)].transpose(2, 0, 1))
    w4 = np.ascontiguousarray(tab_ext[np.where(key > q, 31, 32)].transpose(2, 0, 1))
    ekt = np.zeros((64, 32, 128), np.float32)
    for kt in range(32):
        ekt[2 * kt, kt, 0:64] = 1.0
        ekt[2 * kt + 1, kt, 64:128] = 1.0
    ii = np.arange(128)[:, None, None]
    qi = np.arange(16)[None, :, None]
    jv = np.arange(64)[None, None, :]
    t = half * 2048 + 128 * qi + ii
    j = jv - 32 * (1 - half)
    cur = t // 64
    exists = j >= 0
    forced = exists & ((j == 0) | ((j <= cur) & (j > cur - 2)))
    reachable = exists & (j * 64 <= t)
    reach01 = (reachable & ~forced).astype(np.float32)
    addc = np.where(forced, 1e4, np.where(reachable, 0.0, -1e4)).astype(np.float32)
    killc = np.zeros((1, 256), np.float32)
    killcol = np.zeros((128, 1), np.float32)
    if half == 0:
        killc[0, :128] = NEG
        killcol[:] = NEG
    return {"bfull": bfull, "w0": w0, "w1": w1, "w4": w4, "ekt": ekt.reshape(64, 32 * 128), "reach01": np.ascontiguousarray(reach01),
            "addc": np.ascontiguousarray(addc), "killc": killc, "killcol": killcol}


def _m2_const():
    s = np.arange(128)[:, None]
    c = np.arange(128)[None, :]
    return ((s // 64 == c // 64) & (s <= c)).astype(np.float32)


def _core_inputs(inputs, x, b, half, shared):
    m = dict(shared)
    xo = x[b, half * NOWN:(half + 1) * NOWN]
    m["x_own"] = np.ascontiguousarray(xo)
    m["x_pre"] = np.ascontiguousarray(x[b, 0:NOWN]) if half == 1 else np.zeros((NOWN, D), np.float32)
    m.update(shared["_nsa"][half])
    del m["_nsa"]
    return m


def make_in_maps(inputs, cores):
    f = lambda k: np.ascontiguousarray(np.asarray(inputs[k], np.float32))
    x = f("x")
    shared = {
        "g_mix": f("g_mix_norm")[0], "w_in": f("w_in")[0], "ident": np.eye(128, dtype=np.float32),
        "w_alpha2": f("w_alpha2")[0], "b_alpha": f("b_alpha")[0], "gla_norm_g": f("gla_norm_g")[0], "m2": _m2_const(),
        "phi_k_w1": f("phi_k_w1")[0], "phi_v_w1": f("phi_v_w1")[0], "phi_k_w2": f("phi_k_w2")[0], "phi_v_w2": f("phi_v_w2")[0],
        "cmp_pos_k": f("cmp_pos_k")[0], "cmp_pos_v": f("cmp_pos_v")[0], "rel_table": f("rel_bias_table"),
        "w_gla_proj": f("w_gla_proj")[0], "w_nsa_proj": f("w_nsa_proj")[0], "w_out": f("w_out")[0], "g_mlp": f("g_mlp_norm")[0],
        "w_up": f("w_up")[0], "w_down": f("w_down")[0], "g_final": f("g_final_norm"),
    }
    shared["_nsa"] = [nsa_consts(shared["rel_table"], h) for h in range(2)]
    return [_core_inputs(inputs, x, c // 2, c % 2, shared) for c in cores]


def kernel(**inputs):
    nc = build()
    cores = list(range(8))
    in_maps = make_in_maps(inputs, cores)
    res = run_bass_kernel_spmd(nc, in_maps, core_ids=cores)
    out = np.empty((4, 4096, D), np.float32)
    for c in cores:
        out[c // 2, (c % 2) * NOWN:(c % 2 + 1) * NOWN] = np.asarray(res.results[c]["out"], np.float32)
    return out
```

```python
import concourse.bass as bass
import concourse.mybir as mybir

F32 = mybir.dt.float32
BF16 = mybir.dt.bfloat16
AF = mybir.ActivationFunctionType
ALU = mybir.AluOpType

ENGS = ("tensor", "vector", "scalar", "gpsimd", "sync")


class Buf:
    __slots__ = ("name", "w", "r", "dma_sem", "dma_cum", "last_dma")

    def __init__(self, name):
        self.name = name
        self.w = []
        self.r = []
        self.dma_sem = None
        self.dma_cum = 0
        self.last_dma = None


class Rec:
    __slots__ = ("eng", "fn", "deps", "flag", "count", "dma_sem", "dma_cum", "is_dma")

    def __init__(self, eng, fn):
        self.eng = eng
        self.fn = fn
        self.deps = []
        self.flag = False
        self.count = 0
        self.dma_sem = None
        self.dma_cum = 0
        self.is_dma = False


class KB:
    def __init__(self, nc, stack):
        self.nc = nc
        self.stack = stack
        self.ops = {e: [] for e in ENGS}
        self.esem = {}
        for e in ("tensor", "vector", "scalar", "gpsimd"):
            self.esem[e] = stack.enter_context(nc.semaphore("es_" + e))
        self.dma_sems = []
        self.free_sems = []
        self.sem_final = {}
        self.sem_kind = {}
        self.pending = {e: [] for e in ENGS}
        self.all_bufs = []
        self.nsem = 4

    def buf(self, name):
        b = Buf(name)
        self.all_bufs.append(b)
        return b

    def _dep(self, rec, ev):
        if ev[0] == "op":
            r = ev[1]
            if r is rec:
                return
            if r.eng == "tensor" and rec.eng == "tensor" and not rec.is_dma:
                return
            r.flag = True
        rec.deps.append(ev)

    def _track(self, rec, reads, writes):
        for b in reads:
            for ev in b.w:
                self._dep(rec, ev)
        for b in writes:
            for ev in b.w:
                self._dep(rec, ev)
            for ev in b.r:
                self._dep(rec, ev)
        for ev in self.pending[rec.eng]:
            self._dep(rec, ev)
        self.pending[rec.eng] = []

    def _publish(self, ev, reads, writes):
        for b in reads:
            b.r.append(ev)
        for b in writes:
            b.w = [ev]
            b.r = []

    def op(self, eng, fn, reads=(), writes=()):
        rec = Rec(eng, fn)
        self._track(rec, reads, writes)
        self._publish(("op", rec), reads, writes)
        self.ops[eng].append(rec)
        return rec

    def dma(self, queue, fns, sb, reads=(), writes=()):
        rec = Rec(queue, fns)
        rec.is_dma = True
        if sb.dma_sem is None:
            kind = "sw" if queue == "gpsimd" else "hw"
            pool = [i for i, f in enumerate(self.free_sems) if f[2] == kind]
            if pool:
                sb.dma_sem, sb.dma_cum, _k = self.free_sems.pop(pool[-1])
            else:
                sb.dma_sem = self.stack.enter_context(self.nc.semaphore("ds%d" % self.nsem))
                self.nsem += 1
                sb.dma_cum = 0
            sb.last_dma = None
            self.dma_sems.append(sb)
            self.sem_kind[id(sb.dma_sem)] = "sw" if queue == "gpsimd" else "hw"
        self._track(rec, reads, writes)
        if sb.last_dma is not None:
            rec.deps.append(sb.last_dma)
        sb.dma_cum += 16 * len(fns)
        rec.dma_sem = sb.dma_sem
        rec.dma_cum = sb.dma_cum
        ev = ("dma", sb.dma_sem, sb.dma_cum)
        sb.last_dma = ev
        self.sem_final[id(sb.dma_sem)] = (sb.dma_sem, sb.dma_cum)
        self._publish(ev, reads, writes)
        self.ops[queue].append(rec)
        return rec

    def barrier(self):
        evs = []
        for e in ("tensor", "vector", "scalar", "gpsimd"):
            for rec in reversed(self.ops[e]):
                if not rec.is_dma:
                    rec.flag = True
                    evs.append(("op", rec))
                    break
        for sb in self.dma_sems:
            if sb.last_dma is not None:
                evs.append(sb.last_dma)
        for e in ENGS:
            self.pending[e] = list(self.pending[e]) + evs
        for b in self.all_bufs:
            b.w = []
            b.r = []
        for sb in self.dma_sems:
            self.free_sems.append((sb.dma_sem, sb.dma_cum, self.sem_kind[id(sb.dma_sem)]))
            sb.dma_sem = None
            sb.last_dma = None
        self.dma_sems = []

    def emit(self, block):
        for e in ("tensor", "vector", "scalar", "gpsimd"):
            c = 0
            for rec in self.ops[e]:
                if rec.flag and not rec.is_dma:
                    c += 1
                    rec.count = c
            print("engine", e, "n_ops", len(self.ops[e]), "flagged", c)
        print("sync n_ops", len(self.ops["sync"]), "n sems", self.nsem,
              "max dma cum", max([v[1] for v in self.sem_final.values()] + [0]))
        kb = self

        def run(eng_name, eng):
            waited = {}
            for rec in kb.ops[eng_name]:
                need = {}
                for ev in rec.deps:
                    if ev[0] == "op":
                        r = ev[1]
                        sem = kb.esem[r.eng]
                        cnt = r.count
                        assert cnt > 0
                    else:
                        sem, cnt = ev[1], ev[2]
                    key = id(sem)
                    if key not in need or need[key][1] < cnt:
                        need[key] = (sem, cnt)
                for key, (sem, cnt) in need.items():
                    if waited.get(key, 0) >= cnt:
                        continue
                    eng.wait_ge(sem, cnt)
                    waited[key] = cnt
                if rec.is_dma:
                    for f in rec.fn:
                        f(eng).then_inc(rec.dma_sem, 16)
                else:
                    ins = rec.fn(eng)
                    if rec.flag:
                        ins.then_inc(kb.esem[eng_name], 1)
            if eng_name == "sync":
                for key, (sem, cum) in kb.sem_final.items():
                    if cum > waited.get(key, 0):
                        eng.wait_ge(sem, cum)

        block.tensor(lambda t: run("tensor", t))
        block.vector(lambda v: run("vector", v))
        block.scalar(lambda s: run("scalar", s))
        block.gpsimd(lambda g: run("gpsimd", g))
        block.sync(lambda s: run("sync", s))


class Arena:
    def __init__(self, ap, ncols):
        self.ap = ap
        self.n = ncols
        self.off = 0

    def reset(self, off=0):
        self.off = off

    def alloc(self, cols, dtype=BF16):
        w = cols * (2 if dtype == F32 else 1)
        w = (w + 31) // 32 * 32
        off = self.off
        assert off + w <= self.n, ("arena overflow", off, w, self.n)
        self.off += w
        ap = self.ap[:, off:off + (cols * (2 if dtype == F32 else 1))]
        if dtype == F32:
            ap = ap.bitcast(F32)
        return ap

import numpy as np
import ml_dtypes
from contextlib import ExitStack
from concourse.bass_utils import run_bass_kernel_spmd

D = 4096
NOWN = 2048
NV = 4096
DIN = 19520
EPS = 1e-6
NEG = -32768.0

FAMS = [
    ("gq", 0, 1024, "F", "own", "copy"),
    ("gk", 1024, 1024, "F", "all", "copy"),
    ("gv", 2048, 2048, "T", "all", "copy"),
    ("ga", 4096, 16, "F", "all", "copy"),
    ("gr", 4112, 2048, "F", "own", "silu"),
    ("nq", 6160, 2048, "F", "own", "scale"),
    ("kc", 8208, 512, "F", "all", "copy"),
    ("vc", 8720, 512, "F", "all", "copy"),
    ("ksl", 9232, 512, "F", "all", "copy"),
    ("vsl", 9744, 512, "T", "all", "copy"),
    ("kw", 10256, 512, "F", "win", "copy"),
    ("vw", 10768, 512, "T", "win", "copy"),
    ("ng", 11280, 48, "F", "own", "sigmoid"),
    ("mg", 11328, 4096, "F", "own", "sigmoid"),
    ("mn", 15424, 4096, "F", "own", "sigmoid"),
]


class Prog:
    pass


def build(debug=(), phases=("A", "G", "N", "P", "O", "M", "F"), ext_in=()):
    nc = bass.Bass("TRN2", target_bir_lowering=False)
    P = Prog()
    P.nc = nc

    def din(name, shape, dt=F32):
        return nc.dram_tensor(name, list(shape), dt, kind="ExternalInput").ap()

    def scr(name, shape, dt=BF16):
        kind = "ExternalOutput" if name in debug else ("ExternalInput" if name in ext_in else "Internal")
        return nc.dram_tensor(name, list(shape), dt, kind=kind).ap()

    I = {}
    I["x_own"] = din("x_own", [NOWN, D])
    I["x_pre"] = din("x_pre", [NOWN, D])
    I["g_mix"] = din("g_mix", [D])
    I["w_in"] = din("w_in", [D, DIN])
    I["ident"] = din("ident", [128, 128])
    I["w_alpha2"] = din("w_alpha2", [16, 1024])
    I["b_alpha"] = din("b_alpha", [1024])
    I["gla_norm_g"] = din("gla_norm_g", [16, 128])
    I["m2"] = din("m2", [128, 128])
    for nm in ("phi_k_w1", "phi_v_w1"):
        I[nm] = din(nm, [4096, 128])
    for nm in ("phi_k_w2", "phi_v_w2"):
        I[nm] = din(nm, [128, 128])
    for nm in ("cmp_pos_k", "cmp_pos_v"):
        I[nm] = din(nm, [32, 128])
    I["rel_table"] = din("rel_table", [32, 16])
    I["bfull"] = din("bfull", [16, 128, 376])
    I["w0"] = din("w0", [16, 128, 128])
    I["w1"] = din("w1", [16, 128, 128])
    I["w4"] = din("w4", [16, 128, 128])
    I["ekt"] = din("ekt", [64, 32 * 128])
    I["reach01"] = din("reach01", [128, 16, 64])
    I["addc"] = din("addc", [128, 16, 64])
    I["killc"] = din("killc", [1, 256])
    I["killcol"] = din("killcol", [128, 1])
    I["w_gla_proj"] = din("w_gla_proj", [2048, D])
    I["w_nsa_proj"] = din("w_nsa_proj", [2048, D])
    I["w_out"] = din("w_out", [D, D])
    I["g_mlp"] = din("g_mlp", [D])
    I["w_up"] = din("w_up", [D, 4 * D])
    I["w_down"] = din("w_down", [4 * D, D])
    I["g_final"] = din("g_final", [D])
    S = {}
    S["gq"] = scr("s_gq", [1024, NOWN])
    S["gk"] = scr("s_gk", [1024, NV])
    S["gv"] = scr("s_gv", [NV, 2048])
    S["ga"] = scr("s_ga", [16, NV])
    S["gr"] = scr("s_gr", [2048, NOWN])
    S["nq"] = scr("s_nq", [2048, NOWN])
    S["kc"] = scr("s_kc", [512, NV])
    S["vc"] = scr("s_vc", [512, NV])
    S["ksl"] = scr("s_ksl", [512, NV])
    S["vsl"] = scr("s_vsl", [NV, 512])
    S["kw"] = scr("s_kw", [512, 2560])
    S["vw"] = scr("s_vw", [2560, 512])
    S["ng"] = scr("s_ng", [48, NOWN])
    S["mg"] = scr("s_mg", [D, NOWN])
    S["mn"] = scr("s_mn", [D, NOWN])
    S["oall"] = scr("s_oall", [D, NOWN])
    S["mix"] = scr("s_mix", [D, NOWN])
    S["x2"] = scr("s_x2", [NOWN, D], F32)
    S["hid"] = scr("s_hid", [16, 128, 128, 128])
    S["x3"] = scr("s_x3", [NOWN, D], F32)
    P.out = nc.dram_tensor("out", [NOWN, D], F32, kind="ExternalOutput").ap()
    P.I, P.S = I, S

    stack = ExitStack()
    with stack:
        kb = KB(nc, stack)
        ARN = 100 * 1024
        arena_t = stack.enter_context(nc.sbuf_tensor("arena", [128, ARN], BF16))
        pers_t = stack.enter_context(nc.sbuf_tensor("pers", [128, 1024], BF16))
        ar = Arena(arena_t[:, :], ARN)
        pers = Arena(pers_t[:, :], 1024)
        psb = [stack.enter_context(nc.psum_tensor("ps%d" % i, [128, 512], F32)) for i in range(8)]
        PS = [p[:, :] for p in psb]
        PSB = [kb.buf("psum%d" % i) for i in range(8)]
        ident = pers.alloc(128, BF16)
        ident_f = pers.alloc(128, F32)
        b_ident = kb.buf("ident")
        kb.dma("sync", [lambda e: e.dma_start(out=ident_f, in_=I["ident"])], b_ident, writes=[b_ident])
        kb.op("vector", lambda e: e.tensor_copy(out=ident, in_=ident_f), reads=[b_ident], writes=[b_ident])
        P.kb, P.ar, P.pers, P.PS, P.PSB, P.ident, P.b_ident = kb, ar, pers, PS, PSB, ident, b_ident
        P.ident_f = ident_f

        if "A" in phases:
            phase_inproj(P)
        if "G" in phases:
            phase_gla(P)
        if "N" in phases:
            phase_nsa(P)
        if "P" in phases:
            phase_proj(P)
        if "O" in phases:
            phase_out(P)
        if "M" in phases:
            phase_mlp(P)
        if "F" in phases:
            phase_final(P)
        block = stack.enter_context(nc.Block())
        kb.emit(block)
    return nc


def rmsnorm_transpose(P, x_ap, g_ap, hT, b_hT, ntile, region_off):
    kb, ar, PS, PSB = P.kb, P.ar, P.PS, P.PSB
    ar.reset(region_off)
    xs = [ar.alloc(D, F32) for _ in range(2)]
    b_xs = [kb.buf("xs%d" % i) for i in range(2)]
    hb = [ar.alloc(D, BF16) for _ in range(2)]
    b_hb = [kb.buf("hb%d" % i) for i in range(2)]
    gb = ar.alloc(D, F32)
    b_gb = kb.buf("gb")
    st = ar.alloc(64, F32)
    b_st = [kb.buf("st%d" % i) for i in range(2)]
    kb.dma("sync", [lambda e: e.dma_start(out=gb, in_=g_ap.partition_broadcast(128))], b_gb, writes=[b_gb])
    for t in range(ntile):
        s = t % 2
        xt, ht = xs[s], hb[s]
        ss = st[:, 2 * s:2 * s + 1]
        rs = st[:, 2 * s + 1:2 * s + 2]
        kb.dma("sync", [lambda e, xt=xt, t=t: e.dma_start(out=xt, in_=x_ap[t * 128:(t + 1) * 128, :])],
               b_xs[s], writes=[b_xs[s]])
        kb.op("scalar", lambda e, xt=xt, ht=ht, ss=ss: e.activation(out=ht, in_=xt, func=AF.Square, accum_out=ss),
              reads=[b_xs[s]], writes=[b_hb[s], b_st[s]])
        kb.op("vector", lambda e, ss=ss, rs=rs: e.tensor_scalar(out=rs, in0=ss, scalar1=1.0 / D, scalar2=EPS,
                                                              op0=ALU.mult, op1=ALU.add),
              reads=[b_st[s]], writes=[b_st[s]])
        kb.op("scalar", lambda e, rs=rs: e.sqrt(out=rs, in_=rs), reads=[b_st[s]], writes=[b_st[s]])
        kb.op("vector", lambda e, rs=rs: e.reciprocal(out=rs, in_=rs), reads=[b_st[s]], writes=[b_st[s]])
        kb.op("vector", lambda e, xt=xt, ht=ht, rs=rs: e.scalar_tensor_tensor(out=ht, in0=xt, scalar=rs, in1=gb,
                                                                              op0=ALU.mult, op1=ALU.mult),
              reads=[b_xs[s], b_st[s], b_gb], writes=[b_hb[s]])
        for q in range(4):
            bank = q % 2
            pT = PS[bank].bitcast(BF16)

            def tfn(e, ht=ht, q=q, pT=pT):
                ins = None
                for j in range(8):
                    kt = q * 8 + j
                    ins = e.transpose(pT[:, j * 128:(j + 1) * 128], ht[:, kt * 128:(kt + 1) * 128], P.ident)
                return ins
            kb.op("tensor", tfn, reads=[b_hb[s], P.b_ident], writes=[PSB[bank]])
            dst = hT[:, q * 8:(q + 1) * 8, t * 128:(t + 1) * 128]
            src = pT.rearrange("p (j c) -> p j c", j=8)
            if q % 2 == 0:
                kb.op("scalar", lambda e, dst=dst, src=src: e.copy(out=dst, in_=src), reads=[PSB[bank]], writes=[kb.buf('hTq')])
            else:
                kb.op("vector", lambda e, dst=dst, src=src: e.tensor_copy(out=dst, in_=src), reads=[PSB[bank]], writes=[kb.buf('hTq')])


def phase_inproj(P):
    kb, ar, PS, PSB, I, S = P.kb, P.ar, P.PS, P.PSB, P.I, P.S
    nc = P.nc
    ar.reset()
    hT = ar.alloc(32 * NOWN, BF16).rearrange("p (k t) -> p k t", k=32)
    b_hT = kb.buf("hT")
    R0 = ar.off
    for ps_i, (xname, tok0) in enumerate((("x_own", NOWN), ("x_pre", 0))):
        own = xname == "x_own"
        rmsnorm_transpose(P, I[xname], I["g_mix"], hT, b_hT, 16, R0)
        kb.barrier()
        ar.reset(R0)
        WS = [ar.alloc(32 * 512, BF16).rearrange("p (k c) -> p k c", k=32) for _ in range(2)]
        b_ws = [kb.buf("ws%d" % i) for i in range(2)]
        OS = [ar.alloc(2048, BF16) for _ in range(2)]
        OSF = [o.bitcast(F32) for o in OS]
        b_os = [kb.buf("os%d" % i) for i in range(2)]
        wcnt = 0
        ocnt = 0
        for (name, c0, ncols, layout, scope, act) in FAMS:
            if not own and scope == "own":
                continue
            if scope == "win" and not own:
                chunks = [3]
            else:
                chunks = [0, 1, 2, 3]
            dst = S[name]
            for w0 in range(0, ncols, 512):
                wn = min(512, ncols - w0)
                ws = WS[wcnt % 2]
                bw = b_ws[wcnt % 2]
                wcnt += 1
                fns = []
                for i in range(4):
                    fns.append(lambda e, ws=ws, i=i, c0=c0, w0=w0, wn=wn: e.dma_start(
                        out=ws[:, i * 8:(i + 1) * 8, 0:wn],
                        in_=I["w_in"][i * 1024:(i + 1) * 1024, c0 + w0:c0 + w0 + wn].rearrange("(k p) c -> p k c", p=128)))
                kb.dma("gpsimd", fns, bw, writes=[bw])
                if layout == "F":
                    for cb in range(0, wn, 128):
                        cn = min(128, wn - cb)
                        half = (cb // 128) % 2
                        banks = [half * 4 + i for i in range(len(chunks))]

                        def mm(e, ws=ws, cb=cb, cn=cn, banks=banks, chunks=chunks):
                            ins = None
                            for kt in range(32):
                                for bi, ch in zip(banks, chunks):
                                    ins = e.matmul(PS[bi][0:cn, :], ws[:, kt, cb:cb + cn], hT[:, kt, ch * 512:(ch + 1) * 512],
                                                   start=(kt == 0), stop=(kt == 31))
                            return ins
                        kb.op("tensor", mm, reads=[bw], writes=[PSB[b] for b in banks])
                        osl = ocnt % 2
                        ocnt += 1
                        f32out = False
                        for li, (bi, ch) in enumerate(zip(banks, chunks)):
                            if f32out:
                                o_ap = OSF[osl][0:cn, (li % 2) * 512:(li % 2 + 1) * 512]
                            else:
                                o_ap = OS[osl][0:cn, li * 512:(li + 1) * 512]
                            src = PS[bi][0:cn, :]
                            if act == "copy":
                                kb.op("vector", lambda e, o_ap=o_ap, src=src: e.tensor_copy(out=o_ap, in_=src),
                                      reads=[PSB[bi]], writes=[b_os[osl]])
                            elif act == "scale":
                                kb.op("vector", lambda e, o_ap=o_ap, src=src: e.tensor_scalar(
                                    out=o_ap, in0=src, scalar1=128.0 ** -0.5, scalar2=None, op0=ALU.mult),
                                    reads=[PSB[bi]], writes=[b_os[osl]])
                            elif act == "silu":
                                kb.op("scalar", lambda e, o_ap=o_ap, src=src: e.activation(out=o_ap, in_=src, func=AF.Silu),
                                      reads=[PSB[bi]], writes=[b_os[osl]])
                            elif act == "sigmoid":
                                kb.op("scalar", lambda e, o_ap=o_ap, src=src: e.activation(out=o_ap, in_=src, func=AF.Sigmoid),
                                      reads=[PSB[bi]], writes=[b_os[osl]])
                            if f32out and (li % 2 == 1):
                                t0 = (chunks[li - 1]) * 512
                                kb.dma("sync", [lambda e, osl=osl, cn=cn, t0=t0, w0=w0, cb=cb, dst=dst: e.dma_start(
                                    out=dst[w0 + cb:w0 + cb + cn, t0:t0 + 1024], in_=OSF[osl][0:cn, 0:1024])],
                                    b_os[osl], reads=[b_os[osl]])
                                if li != len(chunks) - 1:
                                    osl = ocnt % 2
                                    ocnt += 1
                        if not f32out:
                            if scope == "own":
                                tcol0 = 0
                            elif scope == "all":
                                tcol0 = (NOWN if own else 0)
                            else:
                                tcol0 = (512 if own else -1536)
                            tA = tcol0 + chunks[0] * 512
                            nt = 512 * len(chunks)
                            kb.dma("sync", [lambda e, osl=osl, cn=cn, tA=tA, nt=nt, w0=w0, cb=cb, dst=dst: e.dma_start(
                                out=dst[w0 + cb:w0 + cb + cn, tA:tA + nt], in_=OS[osl][0:cn, 0:nt])],
                                b_os[osl], reads=[b_os[osl]])
                else:
                    ttiles = [ch * 4 + i for ch in chunks for i in range(4)]
                    for g0 in range(0, len(ttiles), 4):
                        grp = ttiles[g0:g0 + 4]
                        half = (g0 // 4) % 2
                        banks = [half * 4 + i for i in range(4)]
                        for bi, tt in zip(banks, grp):
                            def mm(e, ws=ws, wn=wn, bi=bi, tt=tt):
                                ins = None
                                for kt in range(32):
                                    ins = e.matmul(PS[bi][:, 0:wn], hT[:, kt, tt * 128:(tt + 1) * 128], ws[:, kt, 0:wn],
                                                   start=(kt == 0), stop=(kt == 31))
                                return ins
                            kb.op("tensor", mm, reads=[bw], writes=[PSB[bi]])
                        osl = ocnt % 2
                        ocnt += 1
                        for li, (bi, tt) in enumerate(zip(banks, grp)):
                            o_ap = OS[osl][:, li * 512:li * 512 + wn]
                            src = PS[bi][:, 0:wn]
                            if li % 2 == 0:
                                kb.op("vector", lambda e, o_ap=o_ap, src=src: e.tensor_copy(out=o_ap, in_=src),
                                      reads=[PSB[bi]], writes=[b_os[osl]])
                            else:
                                kb.op("scalar", lambda e, o_ap=o_ap, src=src: e.copy(out=o_ap, in_=src),
                                      reads=[PSB[bi]], writes=[b_os[osl]])
                        if scope == "all":
                            r0 = (NOWN if own else 0)
                        else:
                            r0 = (512 if own else -1536)
                        rA = r0 + grp[0] * 128
                        kb.dma("sync", [lambda e, osl=osl, rA=rA, w0=w0, wn=wn, dst=dst: e.dma_start(
                            out=dst[rA:rA + 512, w0:w0 + wn].rearrange("(j p) c -> p j c", p=128),
                            in_=OS[osl][:, :].rearrange("p (j c) -> p j c", j=4)[:, :, 0:wn])],
                            b_os[osl], reads=[b_os[osl]])
        kb.barrier()


def phase_gla(P):
    kb, ar, PS, PSB, I, S = P.kb, P.ar, P.PS, P.PSB, P.I, P.S
    ar.reset()
    wa2 = ar.alloc(1024, BF16)
    aT = ar.alloc(NV, BF16)
    nb = ar.alloc(8, F32)
    gcol = ar.alloc(16, F32)
    m2 = ar.alloc(128, F32)
    ones = ar.alloc(128, BF16)
    msk = ar.alloc(NV, F32)
    b_c = kb.buf("gconst")
    b_c2 = kb.buf("gconst2")
    b_c3 = kb.buf("gconst3")
    b_c4 = kb.buf("gconst4")
    b_c5 = kb.buf("gconst5")
    kb.dma("gpsimd", [lambda e: e.dma_start(out=wa2[0:16, :], in_=I["w_alpha2"])], b_c, writes=[b_c])
    kb.dma("sync", [lambda e: e.dma_start(out=aT[0:16, :], in_=S["ga"])], b_c2, writes=[b_c2])
    b8 = ar.alloc(128, F32)
    g16 = ar.alloc(128, F32)
    b_b8 = kb.buf("b8")
    b_g16 = kb.buf("g16")
    kb.dma("sync", [lambda e: e.dma_start(out=b8[0:8, :], in_=I["b_alpha"].rearrange("(f p) -> f p", p=128))], b_b8, writes=[b_b8])
    kb.dma("sync", [lambda e: e.dma_start(out=g16[0:16, :], in_=I["gla_norm_g"])], b_g16, writes=[b_g16])
    kb.op("tensor", lambda e: e.matmul(PS[7][:, 0:8], b8[0:8, :], P.ident_f[0:8, 0:8], start=True, stop=True),
          reads=[b_b8, P.b_ident], writes=[PSB[7]])
    kb.op("vector", lambda e: e.tensor_scalar(out=nb, in0=PS[7][:, 0:8], scalar1=-1.0, scalar2=None, op0=ALU.mult),
          reads=[PSB[7]], writes=[b_c3])
    kb.op("tensor", lambda e: e.matmul(PS[7][:, 0:16], g16[0:16, :], P.ident_f[0:16, 0:16], start=True, stop=True),
          reads=[b_g16, P.b_ident], writes=[PSB[7]])
    kb.op("vector", lambda e: e.tensor_copy(out=gcol, in_=PS[7][:, 0:16]), reads=[PSB[7]], writes=[b_c4])
    kb.dma("sync", [lambda e: e.dma_start(out=m2, in_=I["m2"])], b_c5, writes=[b_c5])
    b_ones = kb.buf("ones")
    kb.op("vector", lambda e: e.memset(ones, 1.0), writes=[b_ones])
    b_msk = kb.buf("msk")
    kb.op("gpsimd", lambda e: e.memset(msk, 1.0), writes=[b_msk])
    kb.op("gpsimd", lambda e: e.memset(msk.rearrange("p (n c) -> p n c", c=64)[:, :, 0:1], 0.0), reads=[b_msk], writes=[b_msk])

    sp = ar.alloc(NV, F32)
    Lp = ar.alloc(NOWN, F32)
    onesf = ar.alloc(NOWN, F32)
    bc = ar.alloc(1, F32)
    b_Lp, b_onesf, b_bc = kb.buf('Lp'), kb.buf('onesf'), kb.buf('bc')
    kb.op('gpsimd', lambda e: e.memset(onesf, 1.0), writes=[b_onesf])
    Eq = ar.alloc(NV, F32)
    Ek = ar.alloc(NV, F32)
    qT = ar.alloc(NOWN, BF16)
    kT = ar.alloc(NV, BF16)
    qeT = ar.alloc(NOWN, BF16)
    keT = ar.alloc(NV, BF16)
    ke_tm = ar.alloc(32 * 128, BF16).rearrange("p (t c) -> p t c", t=32)
    v_tm = ar.alloc(32 * 256, BF16).rearrange("p (t c) -> p t c", t=32)
    Tst = [ar.alloc(256, F32) for _ in range(2)]
    S32 = ar.alloc(256, F32)
    b_S32 = kb.buf('S32')
    SbA = ar.alloc(32 * 256, BF16).rearrange('p (n c) -> p n c', n=32)
    atA = ar.alloc(32 * 128, BF16).rearrange('p (n c) -> p n c', n=32)
    kvS = ar.alloc(32 * 256, BF16).rearrange('p (n c) -> p n c', n=32)
    o_sb = [ar.alloc(512, F32) for _ in range(2)]
    sq = [ar.alloc(512, BF16) for _ in range(2)]
    rs = [ar.alloc(512, F32) for _ in range(2)]
    rT = [ar.alloc(512, BF16) for _ in range(2)]
    ob = [ar.alloc(512, BF16) for _ in range(2)]
    b_sp, b_Eq, b_Ek, b_qT, b_kT, b_qeT, b_keT, b_ketm, b_vtm = [kb.buf(n) for n in
        ("sp", "Eq", "Ek", "qT", "kT", "qeT", "keT", "ketm", "vtm")]
    b_T = [kb.buf("T0"), kb.buf("T1")]
    b_SbA = [kb.buf("SbA%d" % i) for i in range(32)]
    b_atA = [kb.buf("atA%d" % i) for i in range(32)]
    b_kvS = [kb.buf("kvS%d" % i) for i in range(32)]
    b_osb = [kb.buf("osb%d" % i) for i in range(2)]
    b_sq = [kb.buf("sq%d" % i) for i in range(2)]
    b_rs = [kb.buf("rs%d" % i) for i in range(2)]
    b_rT = [kb.buf("rT%d" % i) for i in range(2)]
    b_ob = [kb.buf("ob%d" % i) for i in range(2)]
    b_po = [[kb.buf("po%d%d" % (h, i)) for i in range(2)] for h in range(2)]
    b_pa = [PSB[4]] * 4
    b_pk = [PSB[5], PSB[6]]
    b_p6 = PSB[7]
    b_p7 = PSB[7]
    ecnt = 0
    print("G arena cols used", ar.off)
    for fb in range(8):
        kb.dma("sync", [lambda e, fb=fb: e.dma_start(out=qT, in_=S["gq"][fb * 128:(fb + 1) * 128, :])], b_qT, writes=[b_qT])
        kb.dma("sync", [lambda e, fb=fb: e.dma_start(out=kT, in_=S["gk"][fb * 128:(fb + 1) * 128, :])], b_kT, writes=[b_kT])
        kb.dma("sync", [lambda e, fb=fb: e.dma_start(
            out=v_tm, in_=S["gv"][:, fb * 256:(fb + 1) * 256].rearrange("(t p) c -> p t c", p=128))], b_vtm, writes=[b_vtm])
        for c in range(8):
            pb_, bb = (PS[7], b_p7)
            kb.op("tensor", lambda e, pb_=pb_, c=c, fb=fb: e.matmul(pb_[:, :], wa2[0:16, fb * 128:(fb + 1) * 128],
                                                                    aT[0:16, c * 512:(c + 1) * 512], start=True, stop=True),
                  reads=[b_c, b_c2], writes=[bb])
            kb.op("scalar", lambda e, pb_=pb_, c=c, fb=fb: e.activation(out=sp[:, c * 512:(c + 1) * 512], in_=pb_[:, :], func=AF.Exp,
                                                                        bias=nb[:, fb:fb + 1], scale=-1.0),
                  reads=[bb, b_c3], writes=[b_sp])
        kb.op("scalar", lambda e: e.activation(out=sp, in_=sp, func=AF.Ln, bias=1.0, scale=1.0), reads=[b_sp], writes=[b_sp])
        kb.op("vector", lambda e: e.tensor_tensor_scan(out=Lp, data0=onesf, data1=sp[:, 0:NOWN], initial=0.0, op0=ALU.mult, op1=ALU.add),
              reads=[b_sp, b_onesf], writes=[b_Lp])
        kb.op("vector", lambda e: e.tensor_scalar(out=bc, in0=Lp[:, NOWN - 1:NOWN], scalar1=-1.0 / 16.0, scalar2=None, op0=ALU.mult),
              reads=[b_Lp], writes=[b_bc])
        kb.op("scalar", lambda e: e.activation(out=Ek[:, 0:NOWN], in_=Lp, func=AF.Exp, bias=bc, scale=1.0 / 16.0), reads=[b_Lp, b_bc], writes=[b_Ek])
        kb.op("vector", lambda e: e.tensor_tensor_scan(out=sp[:, NOWN:NV], data0=msk[:, 0:NOWN], data1=sp[:, NOWN:NV], initial=0.0, op0=ALU.mult, op1=ALU.add),
              reads=[b_sp, b_msk], writes=[b_sp])
        kb.op("scalar", lambda e: e.activation(out=Eq[:, NOWN:NV], in_=sp[:, NOWN:NV], func=AF.Exp, scale=-1.0 / 16.0), reads=[b_sp], writes=[b_Eq])
        kb.op("scalar", lambda e: e.activation(out=Ek[:, NOWN:NV], in_=sp[:, NOWN:NV], func=AF.Exp, scale=1.0 / 16.0), reads=[b_sp], writes=[b_Ek])
        kb.op("vector", lambda e: e.scalar_tensor_tensor(out=qeT, in0=qT, scalar=0.125, in1=Eq[:, NOWN:NV], op0=ALU.mult, op1=ALU.mult),
              reads=[b_qT, b_Eq], writes=[b_qeT])
        kb.op("vector", lambda e: e.tensor_tensor(out=keT, in0=kT, in1=Ek, op=ALU.mult), reads=[b_kT, b_Ek], writes=[b_keT])
        for q in range(4):
            pb_, bb = (PS[7], b_p7)
            pT = pb_.bitcast(BF16)

            def tfn(e, q=q, pT=pT):
                ins = None
                for j in range(8):
                    t = q * 8 + j
                    ins = e.transpose(pT[:, j * 128:(j + 1) * 128], keT[:, t * 128:(t + 1) * 128], P.ident)
                return ins
            kb.op("tensor", tfn, reads=[b_keT, P.b_ident], writes=[bb])
            dst = ke_tm[:, q * 8:(q + 1) * 8, :]
            src = pT.rearrange("p (j c) -> p j c", j=8)
            if q % 2 == 0:
                kb.op("scalar", lambda e, dst=dst, src=src: e.copy(out=dst, in_=src), reads=[bb], writes=[b_ketm])
            else:
                kb.op("vector", lambda e, dst=dst, src=src: e.tensor_copy(out=dst, in_=src), reads=[bb], writes=[b_ketm])
        def pfx(e):
            ins = None
            for t in range(16):
                ins = e.matmul(PS[5][:, 0:256], ke_tm[:, t, :], v_tm[:, t, :], start=(t == 0), stop=(t == 15))
            return ins
        kb.op("tensor", pfx, reads=[b_ketm, b_vtm], writes=[PSB[5]])
        kb.op("vector", lambda e: e.tensor_copy(out=S32, in_=PS[5][:, 0:256]), reads=[PSB[5]], writes=[b_S32])
        kb.op("scalar", lambda e: e.copy(out=SbA[:, 0, :], in_=S32), reads=[b_S32], writes=[b_SbA[0]])
        for n in range(32, 64):
            t = n // 2
            pb0 = 64 * (n % 2)
            kslot = n % 2
            pk = PS[5 + kslot][:, 0:256]
            kb.op("tensor", lambda e, pk=pk, t=t, pb0=pb0: e.matmul(pk, ke_tm[pb0:pb0 + 64, t, :], v_tm[pb0:pb0 + 64, t, :],
                                                                    start=True, stop=True),
                  reads=[b_ketm, b_vtm], writes=[b_pk[kslot]])
            kb.op("scalar", lambda e, pk=pk, n=n: e.copy(out=kvS[:, n - 32, :], in_=pk), reads=[b_pk[kslot]], writes=[b_kvS[n - 32]])
        masks = []
        for tq in range(16):
            t = 16 + tq
            for hd in range(2):
                hb = 64 * hd
                ai = tq * 2 + hd
                pa = PS[4][:, 0:128]
                kb.op("tensor", lambda e, pa=pa, hb=hb, t=t, tq=tq: e.matmul(
                    pa, keT[hb:hb + 64, t * 128:(t + 1) * 128], qeT[hb:hb + 64, tq * 128:(tq + 1) * 128], start=True, stop=True),
                    reads=[b_keT, b_qeT], writes=[PSB[4]])
                kb.op("vector", lambda e, pa=pa, ai=ai: e.tensor_tensor(out=atA[:, ai, :], in0=pa, in1=m2, op=ALU.mult),
                      reads=[PSB[4], b_c5], writes=[b_atA[ai]])
        for n in range(32, 64):
            Tn, bTn = Tst[n % 2], b_T[n % 2]
            Tp, bTp = Tst[(n - 1) % 2], b_T[(n - 1) % 2]
            kv_ = kvS[:, n - 32, :]
            if n == 32:
                kb.op("vector", lambda e, kv_=kv_, Tn=Tn: e.tensor_tensor(out=Tn, in0=S32, in1=kv_, op=ALU.add),
                      reads=[b_kvS[n - 32], b_S32], writes=[bTn])
            else:
                dcol = Eq[:, 64 * (n - 1) + 63:64 * (n - 1) + 64]
                kb.op("vector", lambda e, kv_=kv_, dcol=dcol, Tn=Tn, Tp=Tp: e.scalar_tensor_tensor(out=Tn, in0=Tp, scalar=dcol, in1=kv_,
                                                                                                  op0=ALU.mult, op1=ALU.add),
                      reads=[b_kvS[n - 32], bTp, b_Eq], writes=[bTn])
            if n < 63:
                dcol = Eq[:, 64 * n + 63:64 * n + 64]
                kb.op("vector", lambda e, n=n, dcol=dcol, Tn=Tn: e.tensor_scalar(out=SbA[:, n + 1 - 32, :], in0=Tn, scalar1=dcol, scalar2=None, op0=ALU.mult),
                      reads=[bTn, b_Eq], writes=[b_SbA[n + 1 - 32]])
        for tq in range(16):
            t = 16 + tq
            n0, n1 = 2 * t, 2 * t + 1
            g4 = tq % 4
            for hd in range(2):
                hb = 64 * hd
                ec = 128 * hd
                h = fb * 2 + hd
                ai = tq * 2 + hd
                dbl = (tq // 4) % 2
                po = PS[hd * 2 + dbl][:, g4 * 128:(g4 + 1) * 128]
                s0, s1 = n0 - 32, n1 - 32

                def ofn(e, po=po, t=t, ec=ec, hb=hb, ai=ai, s0=s0, s1=s1, tq=tq):
                    e.matmul(po[:, 0:64], v_tm[:, t, ec:ec + 128], atA[:, ai, 0:64], start=True, stop=False)
                    e.matmul(po[:, 0:64], SbA[hb:hb + 64, s0, ec:ec + 128], qeT[hb:hb + 64, tq * 128:tq * 128 + 64],
                             start=False, stop=True)
                    e.matmul(po[:, 64:128], v_tm[:, t, ec:ec + 128], atA[:, ai, 64:128], start=True, stop=False)
                    return e.matmul(po[:, 64:128], SbA[hb:hb + 64, s1, ec:ec + 128], qeT[hb:hb + 64, tq * 128 + 64:tq * 128 + 128],
                                    start=False, stop=True)
                kb.op("tensor", ofn, reads=[b_vtm, b_atA[ai], b_SbA[s0], b_SbA[s1], b_qeT], writes=[b_po[hd][dbl]])
                if g4 == 3:
                    c0 = (tq - 3) * 128
                    sl = hd
                    pob = PS[hd * 2 + dbl]
                    kb.dma("sync", [lambda e, sl=sl, h=h, c0=c0: e.dma_start(out=rT[sl], in_=S["gr"][h * 128:(h + 1) * 128, c0:c0 + 512])],
                           b_rT[sl], writes=[b_rT[sl]])
                    kb.op("scalar", lambda e, sl=sl, pob=pob: e.copy(out=o_sb[sl], in_=pob[:, :]), reads=[b_po[hd][dbl]], writes=[b_osb[sl]])
                    kb.op("scalar", lambda e, sl=sl, pob=pob: e.activation(out=sq[sl], in_=pob[:, :], func=AF.Square),
                          reads=[b_po[hd][dbl]], writes=[b_sq[sl]])
                    pss, bss = (PS[7], b_p7)
                    kb.op("tensor", lambda e, sl=sl, pss=pss: e.matmul(pss[:, :], ones, sq[sl], start=True, stop=True),
                          reads=[b_sq[sl], b_ones], writes=[bss])
                    kb.op("vector", lambda e, sl=sl, pss=pss: e.tensor_scalar(out=rs[sl], in0=pss[:, :], scalar1=1.0 / 128.0, scalar2=EPS,
                                                                              op0=ALU.mult, op1=ALU.add),
                          reads=[bss], writes=[b_rs[sl]])
                    kb.op("scalar", lambda e, sl=sl: e.activation(out=rs[sl], in_=rs[sl], func=AF.Ln), reads=[b_rs[sl]], writes=[b_rs[sl]])
                    kb.op("scalar", lambda e, sl=sl: e.activation(out=rs[sl], in_=rs[sl], func=AF.Exp, scale=-0.5), reads=[b_rs[sl]], writes=[b_rs[sl]])
                    kb.op("vector", lambda e, sl=sl, h=h: e.scalar_tensor_tensor(out=o_sb[sl], in0=o_sb[sl], scalar=gcol[:, h:h + 1], in1=rs[sl],
                                                                                 op0=ALU.mult, op1=ALU.mult),
                          reads=[b_osb[sl], b_rs[sl], b_c4], writes=[b_osb[sl]])
                    kb.op("vector", lambda e, sl=sl: e.tensor_tensor(out=ob[sl], in0=o_sb[sl], in1=rT[sl], op=ALU.mult),
                          reads=[b_osb[sl], b_rT[sl]], writes=[b_ob[sl]])
                    kb.dma("gpsimd", [lambda e, sl=sl, h=h, c0=c0: e.dma_start(out=S["oall"][h * 128:(h + 1) * 128, c0:c0 + 512], in_=ob[sl])],
                           b_ob[sl], reads=[b_ob[sl]])
    kb.barrier()


def phase_nsa(P):
    kb, ar, PS, PSB, I, S = P.kb, P.ar, P.PS, P.PSB, P.I, P.S
    ar.reset()
    A = ar.alloc
    nbuf = kb.buf
    kcmpT = A(4 * 256, BF16).rearrange("p (g c) -> p g c", g=4)
    vcmp = A(2 * 4 * 128, BF16).rearrange("p (t g d) -> p t g d", t=2, g=4)
    reach = A(16 * 64, F32).rearrange("p (q j) -> p q j", q=16)
    addc = A(16 * 64, F32).rearrange("p (q j) -> p q j", q=16)
    ekt = A(32 * 128, BF16).rearrange("p (k c) -> p k c", k=32)
    ones = A(128, BF16)
    killc = A(256, BF16)
    killcol = A(1, F32)
    zerocol = A(1, F32)
    ctab = A(16, F32)
    TEMP0 = ar.off
    ekt_f = A(32 * 128, F32)
    killc_f = A(256, F32)
    w1 = [A(32 * 128, BF16).rearrange("p (j h) -> p j h", j=32) for _ in range(2)]
    w2 = [A(128, BF16) for _ in range(2)]
    posf = [A(128, F32) for _ in range(2)]
    posT = [A(32, BF16) for _ in range(2)]
    c0 = [A(1, F32) for _ in range(2)]
    b_w1 = [nbuf("w1%d" % i) for i in range(2)]
    b_w2 = [nbuf("w2%d" % i) for i in range(2)]
    b_pos = [nbuf("pos%d" % i) for i in range(2)]
    b_posT = [nbuf("posT%d" % i) for i in range(2)]
    b_c0 = [nbuf("c0%d" % i) for i in range(2)]
    b_kcmp, b_vcmp, b_reach, b_addc, b_ektf, b_ekt, b_ones, b_killcf, b_killc, b_killcol, b_zero, b_ctab = [nbuf(n) for n in (
        "kcmp", "vcmp", "reach", "addc", "ektf", "ekt", "ones", "killcf", "killc", "killcol", "zero", "ctab")]
    for i, (wn1, wn2, pn) in enumerate((("phi_k_w1", "phi_k_w2", "cmp_pos_k"), ("phi_v_w1", "phi_v_w2", "cmp_pos_v"))):
        kb.dma("gpsimd", [lambda e, i=i, wn1=wn1: e.dma_start(out=w1[i], in_=I[wn1].rearrange("(j d) h -> d j h", d=128))],
               b_w1[i], writes=[b_w1[i]])
        kb.dma("gpsimd", [lambda e, i=i, wn2=wn2: e.dma_start(out=w2[i], in_=I[wn2])], b_w2[i], writes=[b_w2[i]])
        kb.dma("sync", [lambda e, i=i, pn=pn: e.dma_start(out=posf[i][0:32, :], in_=I[pn])], b_pos[i], writes=[b_pos[i]])
    kb.dma("sync", [lambda e: e.dma_start(out=reach, in_=I["reach01"])], b_reach, writes=[b_reach])
    kb.dma("sync", [lambda e: e.dma_start(out=addc, in_=I["addc"])], b_addc, writes=[b_addc])
    kb.dma("sync", [lambda e: e.dma_start(out=ekt_f[0:64, :], in_=I["ekt"])], b_ektf, writes=[b_ektf])
    kb.op("vector", lambda e: e.tensor_copy(out=ekt[0:64, :, :], in_=ekt_f[0:64, :].rearrange("p (k c) -> p k c", k=32)),
          reads=[b_ektf], writes=[b_ekt])
    kb.op("vector", lambda e: e.memset(ones, 1.0), writes=[b_ones])
    kb.dma("sync", [lambda e: e.dma_start(out=killc_f[0:1, :], in_=I["killc"])], b_killcf, writes=[b_killcf])
    kb.op("vector", lambda e: e.tensor_copy(out=killc[0:1, :], in_=killc_f[0:1, :]), reads=[b_killcf], writes=[b_killc])
    kb.dma("sync", [lambda e: e.dma_start(out=killcol, in_=I["killcol"])], b_killcol, writes=[b_killcol])
    kb.op("vector", lambda e: e.memset(zerocol, 0.0), writes=[b_zero])
    kb.dma("sync", [lambda e: e.dma_start(out=ctab, in_=I["rel_table"][31, :].partition_broadcast(128))], b_ctab, writes=[b_ctab])

    srcT = [A(NV, BF16) for _ in range(2)]
    b_src = [nbuf("csrc%d" % i) for i in range(2)]
    hid = [A(256, BF16) for _ in range(2)]
    b_hid = [nbuf("hid%d" % i) for i in range(2)]
    for i in range(2):
        kb.op("vector", lambda e, i=i: e.memset(hid[i], 0.0), writes=[b_hid[i]])
        kb.op("tensor", lambda e, i=i: e.matmul(PS[7][:, 0:32], posf[i][0:32, :], P.ident_f[0:32, 0:32], start=True, stop=True),
              reads=[b_pos[i], P.b_ident], writes=[PSB[7]])
        kb.op("vector", lambda e, i=i: e.tensor_copy(out=posT[i], in_=PS[7][:, 0:32]), reads=[PSB[7]], writes=[b_posT[i]])

        def c0fn(e, i=i):
            ins = None
            for j in range(32):
                ins = e.matmul(PS[7][:, 0:1], w1[i][:, j, :], posT[i][:, j:j + 1], start=(j == 0), stop=(j == 31))
            return ins
        kb.op("tensor", c0fn, reads=[b_w1[i], b_posT[i]], writes=[PSB[7]])
        kb.op("vector", lambda e, i=i: e.tensor_copy(out=c0[i], in_=PS[7][:, 0:1]), reads=[PSB[7]], writes=[b_c0[i]])
    cnt = 0
    for g in range(4):
        for i, sname in enumerate(("kc", "vc")):
            sl = cnt % 2
            cnt += 1
            kb.dma("sync", [lambda e, sl=sl, sname=sname, g=g: e.dma_start(out=srcT[sl], in_=S[sname][g * 128:(g + 1) * 128, :])],
                   b_src[sl], writes=[b_src[sl]])
            sv = srcT[sl].rearrange("p (c s) -> p c s", s=16)
            pb = 6 + sl

            def hfn(e, i=i, sv=sv, pb=pb):
                ins = None
                for j in range(32):
                    a, b = j // 16, j % 16
                    ins = e.matmul(PS[pb][:, 0:255], w1[i][:, j, :], sv[:, a:a + 255, b], start=(j == 0), stop=(j == 31))
                return ins
            kb.op("tensor", hfn, reads=[b_w1[i], b_src[sl]], writes=[PSB[pb]])
            kb.op("scalar", lambda e, i=i, pb=pb: e.activation(out=hid[i][:, 0:255], in_=PS[pb][:, 0:255], func=AF.Relu, bias=c0[i], scale=1.0),
                  reads=[PSB[pb], b_c0[i]], writes=[b_hid[i]])
            if i == 0:
                kb.op("tensor", lambda e, pb=pb: e.matmul(PS[pb][:, 0:256], w2[0], hid[0], start=True, stop=True),
                      reads=[b_w2[0], b_hid[0]], writes=[PSB[pb]])
                kb.op("vector", lambda e, g=g, pb=pb: e.tensor_copy(out=kcmpT[:, g, :], in_=PS[pb][:, 0:256]), reads=[PSB[pb]], writes=[b_kcmp])
            else:
                def vfn(e, pb=pb):
                    e.matmul(PS[pb][:, 0:128], hid[1][:, 0:128], w2[1], start=True, stop=True)
                    return e.matmul(PS[pb][:, 128:256], hid[1][:, 128:256], w2[1], start=True, stop=True)
                kb.op("tensor", vfn, reads=[b_w2[1], b_hid[1]], writes=[PSB[pb]])
                kb.op("vector", lambda e, g=g, pb=pb: e.tensor_copy(out=vcmp[:, :, g, :], in_=PS[pb][:, 0:256].rearrange("p (t d) -> p t d", t=2)),
                      reads=[PSB[pb]], writes=[b_vcmp])

    kb.barrier()
    ar.reset(TEMP0)
    kslT = [A(NV, BF16) for _ in range(2)]
    vsl = [A(32 * 128, BF16).rearrange("p (k d) -> p k d", k=32) for _ in range(2)]
    kwT = [A(2560, BF16)] * 2
    vw = [A(20 * 128, BF16).rearrange("p (k d) -> p k d", k=20)] * 2
    nqT = [A(4 * NOWN, BF16).rearrange("p (r t) -> p r t", r=4) for _ in range(2)]
    bfull = [A(4 * 376, F32).rearrange("p (r c) -> p r c", r=4) for _ in range(2)]
    bfullK = [A(4 * 376, F32).rearrange("p (r c) -> p r c", r=4) for _ in range(2)]
    b_bfullK = [nbuf("bfullK%d" % i) for i in range(2)]
    Wm = [[A(512, F32).rearrange("p (r q) -> p r q", r=4) for _ in range(5)]] * 2
    b_ksl, b_vsl, b_nq, b_bfull = [[nbuf(n + str(i)) for i in range(2)] for n in ("kslg", "vslg", "nqg", "bfullg")]
    b_kw = [nbuf("kwg")] * 2
    b_vw = [nbuf("vwg")] * 2
    b_Wm = [[nbuf("Wm%d" % i) for i in range(5)]] * 2
    lgs = A(1024, F32).rearrange("p (r c) -> p r c", r=4)
    Ec = A(1024, F32).rearrange("p (r c) -> p r c", r=4)
    rsum = A(4, F32)
    Pn = A(1024, F32).rearrange("p (r c) -> p r c", r=4)
    Pnb = A(1024, BF16).rearrange("p (r c) -> p r c", r=4)
    PnT = A(1024, BF16).rearrange("p (k q) -> p k q", k=8)
    s01 = A(256, F32)
    s23 = A(256, F32)
    Pp = A(260, F32)
    ia = A(64, F32)
    ib = A(64, F32)
    score = A(64, F32)
    sc2 = A(64, F32)
    mx = A(16, F32)
    sel = A(64, F32)
    negm4 = A(256, BF16).rearrange("p (r j) -> p r j", r=4)
    negmT = [A(512, BF16).rearrange("p (r q) -> p r q", r=4) for _ in range(2)]
    b_lgs, b_Ec, b_rsum, b_Pn, b_Pnb, b_PnT, b_s01, b_s23, b_Pp, b_ia, b_ib, b_score, b_sc2, b_mx, b_sel, b_negm4 = [
        nbuf(n) for n in ("lgs", "Ec", "rsum", "Pn", "Pnb", "PnT", "s01", "s23", "Pp", "ia", "ib", "score", "sc2", "mx", "sel", "negm4")]
    b_negmT = [nbuf("negmT%d" % i) for i in range(2)]
    kb.op("vector", lambda e: e.memset(Pp, 0.0), writes=[b_Pp])
    lgT = [A(512, F32).rearrange("p (r q) -> p r q", r=4) for _ in range(2)]
    b_lgT = [nbuf("lgT%d" % i) for i in range(2)]
    PT = [A(512, BF16) for _ in range(4)]
    b_PT = [nbuf("PT%d" % i) for i in range(4)]
    gt = [A(12 * 128, BF16).rearrange("p (k q) -> p k q", k=12) for _ in range(3)]
    b_gt = [nbuf("gt%d" % i) for i in range(3)]
    acc = [A(512, F32).rearrange("p (r q) -> p r q", r=4) for _ in range(3)]
    b_acc = [nbuf("acc%d" % i) for i in range(3)]
    rinv = [A(512, F32).rearrange("p (r q) -> p r q", r=4) for _ in range(2)]
    b_rinv = [nbuf("rinv%d" % i) for i in range(2)]
    tmp = A(512, F32).rearrange("p (r q) -> p r q", r=4)
    b_tmp = nbuf("tmp")
    obf = [A(512, BF16).rearrange("p (r q) -> p r q", r=4) for _ in range(3)]
    b_obf = [nbuf("obf%d" % i) for i in range(3)]
    print("NSA arena cols used", ar.off)
    SB = [0, 1, 4]
    smb = [A(512, F32).rearrange("p (r q) -> p r q", r=4) for _ in range(2)]
    ocb = [A(512, F32).rearrange("p (r q) -> p r q", r=4) for _ in range(2)]
    b_smb = [nbuf("smb%d" % i) for i in range(2)]
    b_ocb = [nbuf("ocb%d" % i) for i in range(2)]
    st = {"l": 0, "p": 0, "s": 0}
    deferred = []

    def load_group(g):
        s_ = g % 2
        kb.dma("sync", [lambda e: e.dma_start(out=nqT[s_], in_=S["nq"][g * 512:(g + 1) * 512, :].rearrange("(r d) t -> d r t", d=128))],
               b_nq[s_], writes=[b_nq[s_]])
        kb.dma("sync", [lambda e: e.dma_start(out=bfull[s_], in_=I["bfull"][g * 4:(g + 1) * 4].rearrange("r i c -> i r c"))],
               b_bfull[s_], writes=[b_bfull[s_]])
        kb.op("vector", lambda e: e.tensor_scalar(out=bfullK[s_], in0=bfull[s_], scalar1=killcol, scalar2=None, op0=ALU.add),
              reads=[b_bfull[s_], b_killcol], writes=[b_bfullK[s_]])
        kb.dma("sync", [lambda e: e.dma_start(out=kslT[s_], in_=S["ksl"][g * 128:(g + 1) * 128, :])], b_ksl[s_], writes=[b_ksl[s_]])
        kb.dma("sync", [lambda e: e.dma_start(out=vsl[s_], in_=S["vsl"][:, g * 128:(g + 1) * 128].rearrange("(k p) d -> p k d", p=128))],
               b_vsl[s_], writes=[b_vsl[s_]])

    def load_group_late(g):
        s_ = g % 2
        kb.dma("sync", [lambda e: e.dma_start(out=kwT[s_], in_=S["kw"][g * 128:(g + 1) * 128, :])], b_kw[s_], writes=[b_kw[s_]])
        kb.dma("sync", [lambda e: e.dma_start(out=vw[s_], in_=S["vw"][:, g * 128:(g + 1) * 128].rearrange("(k p) d -> p k d", p=128))],
               b_vw[s_], writes=[b_vw[s_]])
        for wi, wname in enumerate(("w0", "w1", "w4")):
            kb.dma("sync", [lambda e, wi=wi, wname=wname: e.dma_start(out=Wm[s_][wi], in_=I[wname][g * 4:(g + 1) * 4].rearrange("r k q -> k r q"))],
                   b_Wm[s_][wi], writes=[b_Wm[s_][wi]])
        cb4 = ctab[:, 4 * g:4 * g + 4].unsqueeze(2).to_broadcast([128, 4, 128])
        for wi in range(2):
            kb.op("vector", lambda e, wi=wi: e.tensor_tensor(out=Wm[s_][3 + wi], in0=Wm[s_][wi], in1=cb4, op=ALU.subtract),
                  reads=[b_Wm[s_][wi], b_ctab], writes=[b_Wm[s_][3 + wi]])

    def stage1(it):
        g, qi = it // 16, it % 16
        s_ = g % 2
        a_ = it % 2
        a3 = it % 3
        qv = 16 + qi
        q0 = qi * 128
        gsl = gt[a3]
        nq_, bf_ = nqT[s_], bfull[s_]
        nmT = negmT[a_]
        kb.dma("sync", [lambda e: e.dma_start(out=gsl, in_=S["ng"][g * 12:(g + 1) * 12, q0:q0 + 128].partition_broadcast(128))],
               b_gt[a3], writes=[b_gt[a3]])
        gview = gsl.rearrange("p (r i) q -> p r i q", i=3)
        for half in range(2):
            pb = 7

            def lfn(e, pb=pb, half=half):
                ins = None
                for rr in range(2):
                    r = half * 2 + rr
                    ins = e.matmul(PS[pb][:, rr * 256:(rr + 1) * 256], nq_[:, r, q0:q0 + 128], kcmpT[:, g, :], start=True, stop=True)
                return ins
            kb.op("tensor", lfn, reads=[b_nq[s_], b_kcmp], writes=[PSB[pb]])
            cc0 = 248 - 8 * qv
            bk_ = bfullK[s_]
            kb.op("vector", lambda e, pb=pb, half=half, cc0=cc0: e.tensor_tensor(
                out=lgs[:, half * 2:half * 2 + 2, 0:128], in0=PS[pb][:, :].rearrange("p (r c) -> p r c", r=2)[:, :, 0:128],
                in1=bk_[:, half * 2:half * 2 + 2, cc0:cc0 + 128], op=ALU.add),
                reads=[PSB[pb], b_bfullK[s_]], writes=[b_lgs])
            kb.op("vector", lambda e, pb=pb, half=half, cc0=cc0: e.tensor_tensor(
                out=lgs[:, half * 2:half * 2 + 2, 128:256], in0=PS[pb][:, :].rearrange("p (r c) -> p r c", r=2)[:, :, 128:256],
                in1=bf_[:, half * 2:half * 2 + 2, cc0 + 128:cc0 + 256], op=ALU.add),
                reads=[PSB[pb], b_bfull[s_]], writes=[b_lgs])
            yield
        for r in range(4):
            kb.op("scalar", lambda e, r=r: e.activation(out=Ec[:, r, :], in_=lgs[:, r, :], func=AF.Exp, accum_out=rsum[:, r:r + 1]),
                  reads=[b_lgs], writes=[b_Ec, b_rsum])
            yield
        kb.op("vector", lambda e: e.tensor_scalar(out=rsum, in0=rsum, scalar1=1e-30, scalar2=None, op0=ALU.max), reads=[b_rsum], writes=[b_rsum])
        kb.op("vector", lambda e: e.reciprocal(out=rsum, in_=rsum), reads=[b_rsum], writes=[b_rsum])
        kb.op("vector", lambda e: e.tensor_tensor(out=Pn, in0=Ec, in1=rsum.unsqueeze(2).to_broadcast([128, 4, 256]), op=ALU.mult),
              reads=[b_Ec, b_rsum], writes=[b_Pn])
        yield
        kb.op("gpsimd", lambda e: e.tensor_copy(out=Pnb, in_=Pn), reads=[b_Pn], writes=[b_Pnb])
        kb.op("vector", lambda e: e.tensor_tensor(out=s01, in0=Pn[:, 0, :], in1=Pn[:, 1, :], op=ALU.add), reads=[b_Pn], writes=[b_s01])
        yield
        kb.op("vector", lambda e: e.tensor_tensor(out=s23, in0=Pn[:, 2, :], in1=Pn[:, 3, :], op=ALU.add), reads=[b_Pn], writes=[b_s23])
        yield
        kb.op("vector", lambda e: e.tensor_tensor(out=Pp[:, 1:257], in0=s01, in1=s23, op=ALU.add), reads=[b_s01, b_s23], writes=[b_Pp])
        p6b = PS[7].bitcast(BF16)
        p7b = PS[7].bitcast(BF16)

        def t6(e):
            ins = None
            for r in range(4):
                for ct in range(2):
                    k = r * 2 + ct
                    ins = e.transpose(p6b[:, k * 128:(k + 1) * 128], Pnb[:, r, ct * 128:(ct + 1) * 128], P.ident)
            return ins
        yield
        kb.op("tensor", t6, reads=[b_Pnb, P.b_ident], writes=[PSB[7]])
        yield
        kb.op("scalar", lambda e: e.copy(out=PnT, in_=p6b.rearrange("p (k q) -> p k q", k=8)), reads=[PSB[7]], writes=[b_PnT])
        v0 = Pp[:, 0:256].rearrange("p (j m) -> p j m", m=4)
        v1 = Pp[:, 4:260].rearrange("p (j m) -> p j m", m=4)
        kb.op("vector", lambda e: e.tensor_tensor(out=ia, in0=v0[:, :, 1], in1=v0[:, :, 2], op=ALU.add), reads=[b_Pp], writes=[b_ia])
        kb.op("vector", lambda e: e.tensor_tensor(out=ib, in0=v0[:, :, 0], in1=v1[:, :, 0], op=ALU.add), reads=[b_Pp], writes=[b_ib])
        yield
        kb.op("vector", lambda e: e.tensor_tensor(out=ia, in0=ia, in1=v0[:, :, 3], op=ALU.add), reads=[b_Pp, b_ia], writes=[b_ia])

        def ocfn(e):
            ins = None
            for r in range(4):
                for ct in range(2):
                    ins = e.matmul(PS[7][:, r * 128:(r + 1) * 128], vcmp[:, ct, g, :], PnT[:, r * 2 + ct, :], start=(ct == 0), stop=(ct == 1))
            return ins
        yield
        kb.op("tensor", ocfn, reads=[b_vcmp, b_PnT], writes=[PSB[7]])
        yield
        kb.op("vector", lambda e: e.scalar_tensor_tensor(out=score, in0=ia, scalar=2.0, in1=ib, op0=ALU.mult, op1=ALU.add),
              reads=[b_ia, b_ib], writes=[b_score])
        ac = acc[a3]
        kb.op("vector", lambda e: e.tensor_tensor(out=ac, in0=PS[7][:, :].rearrange("p (r q) -> p r q", r=4), in1=gview[:, :, 0, :], op=ALU.mult),
              reads=[PSB[7], b_gt[a3]], writes=[b_acc[a3]])
        yield
        kb.op("vector", lambda e: e.tensor_tensor(out=score, in0=score, in1=reach[:, qi, :], op=ALU.mult), reads=[b_score, b_reach], writes=[b_score])
        kb.op("vector", lambda e: e.tensor_tensor(out=score, in0=score, in1=addc[:, qi, :], op=ALU.add), reads=[b_score, b_addc], writes=[b_score])
        yield
        kb.op("vector", lambda e: e.max(out=mx[:, 0:8], in_=score), reads=[b_score], writes=[b_mx])
        yield
        kb.op("vector", lambda e: e.match_replace(out=sc2, in_to_replace=mx[:, 0:8], in_values=score, imm_value=-3e4),
              reads=[b_score, b_mx], writes=[b_sc2])
        yield
        kb.op("vector", lambda e: e.max(out=mx[:, 8:16], in_=sc2), reads=[b_sc2], writes=[b_mx])
        yield
        kb.op("vector", lambda e: e.tensor_scalar(out=sel, in0=score, scalar1=mx[:, 15:16], scalar2=None, op0=ALU.is_ge),
              reads=[b_score, b_mx], writes=[b_sel])
        kb.op("vector", lambda e: e.tensor_scalar(out=sel, in0=sel, scalar1=-1.0, scalar2=32768.0, op0=ALU.add, op1=ALU.mult),
              reads=[b_sel], writes=[b_sel])
        yield
        for r in range(4):
            kb.op("vector", lambda e, r=r: e.tensor_scalar(out=negm4[:, r, :], in0=sel, scalar1=ctab[:, 4 * g + r:4 * g + r + 1], scalar2=None, op0=ALU.add),
                  reads=[b_sel, b_ctab], writes=[b_negm4])
        yield

        def t7(e):
            ins = None
            for r in range(4):
                ins = e.transpose(p7b[0:64, r * 128:(r + 1) * 128], negm4[:, r, :], P.ident)
            return ins
        yield
        kb.op("tensor", t7, reads=[b_negm4, P.b_ident], writes=[PSB[7]])
        yield
        kb.op("vector", lambda e: e.tensor_copy(out=nmT[0:64, :, :], in_=p7b[0:64, 0:512].rearrange("p (r q) -> p r q", r=4)),
              reads=[PSB[7]], writes=[b_negmT[a_]])
        yield

    def stage2(it, tick):
        g, qi = it // 16, it % 16
        s_ = g % 2
        a_ = it % 2
        a3 = it % 3
        qv = 16 + qi
        q0 = qi * 128
        gview = gt[a3].rearrange("p (r i) q -> p r i q", i=3)
        ac = acc[a3]
        nq_ = nqT[s_]
        nmT = negmT[a_]
        cb4 = ctab[:, 4 * g:4 * g + 4].unsqueeze(2).to_broadcast([128, 4, 128])
        W_ = Wm[s_]
        bW_ = b_Wm[s_]
        for br in range(2):
            po, psm = (2, 3) if (2 * it + br) % 2 == 0 else (5, 6)
            kts = list(range(0, qv + 1)) if br == 0 else list(range(qv - 4, qv + 1))
            sbs = {}

            def qk(idx):
                kt = kts[idx]
                sb = SB[st["s"] % 3]
                st["s"] += 1
                sbs[idx] = sb
                if br == 0:
                    def sfn(e):
                        e.matmul(PS[sb][:, :], kslT[s_][:, kt * 128:(kt + 1) * 128], nq_[:, :, q0:q0 + 128], start=True, stop=False)
                        return e.matmul(PS[sb][:, :], ekt[0:64, kt, :], nmT[0:64, :, :], start=False, stop=True)
                    kb.op("tensor", sfn, reads=[b_ksl[s_], b_nq[s_], b_ekt, b_negmT[a_]], writes=[PSB[sb]])
                else:
                    lt = kt - 12
                    kb.op("tensor", lambda e: e.matmul(PS[sb][:, :], kwT[s_][:, lt * 128:(lt + 1) * 128], nq_[:, :, q0:q0 + 128], start=True, stop=True),
                          reads=[b_kw[s_], b_nq[s_]], writes=[PSB[sb]])
            LA = 2
            for idx in range(min(LA, len(kts))):
                qk(idx)
            for idx, kt in enumerate(kts):
                if idx + LA < len(kts):
                    qk(idx + LA)
                off = qv - kt
                sb = sbs[idx]
                first, last = (idx == 0), (idx == len(kts) - 1)
                bcol, b_bcol = (killcol, b_killcol) if kt < 16 else (zerocol, b_zero)
                pt = PT[st["p"] % 4]
                b_pt = b_PT[st["p"] % 4]
                st["p"] += 1
                addm = None
                if br == 0 and off <= 1:
                    addm = (W_[3 + off], bW_[3 + off])
                elif br == 1:
                    if off == 0:
                        addm = (W_[0], bW_[0])
                    elif off == 1:
                        addm = (W_[1], bW_[1])
                    elif off == 4:
                        addm = (W_[2], bW_[2])
                    else:
                        addm = (cb4, b_ctab)
                if addm is not None:
                    lg = lgT[st["l"] % 2]
                    b_lg = b_lgT[st["l"] % 2]
                    st["l"] += 1
                    kb.op("vector", lambda e, lg=lg, sb=sb, addm=addm: e.tensor_tensor(out=lg, in0=PS[sb][:, :].rearrange("p (r q) -> p r q", r=4), in1=addm[0], op=ALU.add),
                          reads=[PSB[sb], addm[1]], writes=[b_lg])
                    kb.op("scalar", lambda e, pt=pt, lg=lg, bcol=bcol: e.activation(out=pt, in_=lg.rearrange("p r q -> p (r q)"), func=AF.Exp, bias=bcol, scale=1.0),
                          reads=[b_lg, b_bcol], writes=[b_pt])
                else:
                    kb.op("scalar", lambda e, pt=pt, sb=sb, bcol=bcol: e.activation(out=pt, in_=PS[sb][:, :], func=AF.Exp, bias=bcol, scale=1.0),
                          reads=[PSB[sb], b_bcol], writes=[b_pt])
                if br == 0:
                    vv, b_vv = vsl[s_][:, kt, :], b_vsl[s_]
                else:
                    vv, b_vv = vw[s_][:, kt - 12, :], b_vw[s_]

                def pvfn(e, vv=vv, pt=pt, first=first, last=last, po=po, psm=psm):
                    e.matmul(PS[po][:, :], vv, pt, start=first, stop=last)
                    return e.matmul(PS[psm][:, :], ones, pt, start=first, stop=last)
                kb.op("tensor", pvfn, reads=[b_vv, b_pt, b_ones], writes=[PSB[po], PSB[psm]])
                if idx == (3 if br == 0 else len(kts) - 1):
                    while deferred:
                        deferred.pop(0)()
                tick()
            sm, oc = smb[br], ocb[br]
            kb.op("scalar", lambda e, sm=sm, psm=psm: e.copy(out=sm, in_=PS[psm][:, :].rearrange("p (r q) -> p r q", r=4)),
                  reads=[PSB[psm]], writes=[b_smb[br]])
            kb.op("scalar", lambda e, oc=oc, po=po: e.copy(out=oc, in_=PS[po][:, :].rearrange("p (r q) -> p r q", r=4)),
                  reads=[PSB[po]], writes=[b_ocb[br]])

            def fin(br=br, sm=sm, oc=oc):
                kb.op("vector", lambda e: e.reciprocal(out=sm, in_=sm), reads=[b_smb[br]], writes=[b_smb[br]])
                kb.op("gpsimd", lambda e: e.tensor_tensor(out=oc, in0=oc, in1=gview[:, :, 1 + br, :], op=ALU.mult),
                      reads=[b_ocb[br], b_gt[a3]], writes=[b_ocb[br]])
                kb.op("gpsimd", lambda e: e.tensor_tensor(out=oc, in0=oc, in1=sm, op=ALU.mult),
                      reads=[b_ocb[br], b_smb[br]], writes=[b_ocb[br]])
                if br == 0:
                    kb.op("gpsimd", lambda e: e.tensor_tensor(out=ac, in0=ac, in1=oc, op=ALU.add), reads=[b_ocb[br], b_acc[a3]], writes=[b_acc[a3]])
                else:
                    ob_ = obf[a3]
                    kb.op("gpsimd", lambda e: e.tensor_tensor(out=ob_, in0=ac, in1=oc, op=ALU.add), reads=[b_ocb[br], b_acc[a3]], writes=[b_obf[a3]])
                    kb.dma("sync", [lambda e: e.dma_start(
                        out=S["oall"][2048 + g * 512:2048 + (g + 1) * 512, q0:q0 + 128].rearrange("(r d) q -> d r q", d=128), in_=ob_)],
                        b_obf[a3], reads=[b_obf[a3]])
            deferred.append(fin)

    load_group(0)
    for _ in stage1(0):
        pass
    for it in range(64):
        if it % 16 == 0 and it // 16 + 1 < 4:
            load_group(it // 16 + 1)
        gen = stage1(it + 1) if it + 1 < 64 else None
        state = {"gen": gen}

        def tick():
            if state["gen"] is not None:
                try:
                    next(state["gen"])
                except StopIteration:
                    state["gen"] = None
        if it % 16 == 0:
            load_group_late(it // 16)
        stage2(it, tick)
        while state["gen"] is not None:
            tick()
    while deferred:
        deferred.pop(0)()
    kb.barrier()


def phase_proj(P):
    kb, ar, PS, PSB, I, S = P.kb, P.ar, P.PS, P.PSB, P.I, P.S
    ar.reset()
    oT = ar.alloc(32 * NOWN, BF16).rearrange("p (k t) -> p k t", k=32)
    b_oT = kb.buf("oT")
    kb.dma("sync", [lambda e, i=i: e.dma_start(out=oT[:, i * 8:(i + 1) * 8, :],
                                               in_=S["oall"][i * 1024:(i + 1) * 1024, :].rearrange("(k p) t -> p k t", p=128)) for i in range(4)],
           b_oT, writes=[b_oT])
    WS = [ar.alloc(32 * 256, BF16).rearrange("p (k c) -> p k c", k=32) for _ in range(2)]
    b_ws = [kb.buf("pws%d" % i) for i in range(2)]
    SG = [ar.alloc(NOWN, BF16) for _ in range(2)]
    SN = [ar.alloc(NOWN, BF16) for _ in range(2)]
    b_sg = [kb.buf("sg%d" % i) for i in range(2)]
    b_sn = [kb.buf("sn%d" % i) for i in range(2)]
    OS = [ar.alloc(NOWN, BF16) for _ in range(2)]
    b_os = [kb.buf("pos%d" % i) for i in range(2)]
    T1 = [ar.alloc(512, F32) for _ in range(2)]
    T2 = [ar.alloc(512, F32) for _ in range(2)]
    b_t1 = [kb.buf("t1%d" % i) for i in range(2)]
    b_t2 = [kb.buf("t2%d" % i) for i in range(2)]
    wc = 0
    fc = 0
    pc = 0
    tc = 0
    for w0 in range(0, D, 256):
        ws, bw = WS[wc % 2], b_ws[wc % 2]
        wc += 1
        fns = []
        for i in range(2):
            fns.append(lambda e, ws=ws, i=i, w0=w0: e.dma_start(
                out=ws[:, i * 8:(i + 1) * 8, :], in_=I["w_gla_proj"][i * 1024:(i + 1) * 1024, w0:w0 + 256].rearrange("(k p) c -> p k c", p=128)))
        for i in range(2):
            fns.append(lambda e, ws=ws, i=i, w0=w0: e.dma_start(
                out=ws[:, 16 + i * 8:16 + (i + 1) * 8, :], in_=I["w_nsa_proj"][i * 1024:(i + 1) * 1024, w0:w0 + 256].rearrange("(k p) c -> p k c", p=128)))
        kb.dma("gpsimd", fns, bw, writes=[bw])
        for cb in (0, 128):
            f0 = w0 + cb
            fs = fc % 2
            fc += 1
            kb.dma("sync", [lambda e, fs=fs, f0=f0: e.dma_start(out=SG[fs], in_=S["mg"][f0:f0 + 128, :])], b_sg[fs], writes=[b_sg[fs]])
            kb.dma("sync", [lambda e, fs=fs, f0=f0: e.dma_start(out=SN[fs], in_=S["mn"][f0:f0 + 128, :])], b_sn[fs], writes=[b_sn[fs]])
            for pair in range(2):
                base = 4 * (pc % 2)
                pc += 1
                chs = (2 * pair, 2 * pair + 1)

                def mm(e, ws=ws, cb=cb, base=base, chs=chs):
                    ins = None
                    for kt in range(32):
                        for ci, ch in enumerate(chs):
                            bi = base + ci + (0 if kt < 16 else 2)
                            ins = e.matmul(PS[bi][:, :], ws[:, kt, cb:cb + 128], oT[:, kt, ch * 512:(ch + 1) * 512],
                                           start=(kt % 16 == 0), stop=(kt % 16 == 15))
                    return ins
                kb.op("tensor", mm, reads=[bw, b_oT], writes=[PSB[base + i] for i in range(4)])
                for ci, ch in enumerate(chs):
                    ts = tc % 2
                    tc += 1
                    csl = slice(ch * 512, (ch + 1) * 512)
                    kb.op("vector", lambda e, ts=ts, base=base, ci=ci, fs=fs, csl=csl: e.tensor_tensor(
                        out=T1[ts], in0=PS[base + ci][:, :], in1=SG[fs][:, csl], op=ALU.mult),
                        reads=[PSB[base + ci], b_sg[fs]], writes=[b_t1[ts]])
                    kb.op("vector", lambda e, ts=ts, base=base, ci=ci, fs=fs, csl=csl: e.tensor_tensor(
                        out=T2[ts], in0=PS[base + 2 + ci][:, :], in1=SN[fs][:, csl], op=ALU.mult),
                        reads=[PSB[base + 2 + ci], b_sn[fs]], writes=[b_t2[ts]])
                    kb.op("vector", lambda e, ts=ts, fs=fs, csl=csl: e.tensor_tensor(out=OS[fs][:, csl], in0=T1[ts], in1=T2[ts], op=ALU.add),
                          reads=[b_t1[ts], b_t2[ts]], writes=[b_os[fs]])
            kb.dma("scalar", [lambda e, fs=fs, f0=f0: e.dma_start(out=S["mix"][f0:f0 + 128, :], in_=OS[fs])], b_os[fs], reads=[b_os[fs]])
    kb.barrier()


def phase_out(P):
    kb, ar, PS, PSB, I, S = P.kb, P.ar, P.PS, P.PSB, P.I, P.S
    ar.reset()
    mT = ar.alloc(32 * NOWN, BF16).rearrange("p (k t) -> p k t", k=32)
    b_mT = kb.buf("mT")
    kb.dma("sync", [lambda e, i=i: e.dma_start(out=mT[:, i * 8:(i + 1) * 8, :],
                                               in_=S["mix"][i * 1024:(i + 1) * 1024, :].rearrange("(k p) t -> p k t", p=128)) for i in range(4)],
           b_mT, writes=[b_mT])
    WS = [ar.alloc(32 * 512, BF16).rearrange("p (k c) -> p k c", k=32) for _ in range(2)]
    b_ws = [kb.buf("ows%d" % i) for i in range(2)]
    XS = [ar.alloc(512, F32) for _ in range(2)]
    b_xs = [kb.buf("oxs%d" % i) for i in range(2)]
    OS = [ar.alloc(512, F32) for _ in range(2)]
    b_os = [kb.buf("oos%d" % i) for i in range(2)]
    cnt = 0
    for cbk in range(8):
        c0 = cbk * 512
        ws, bw = WS[cbk % 2], b_ws[cbk % 2]
        kb.dma("gpsimd", [lambda e, ws=ws, i=i, c0=c0: e.dma_start(
            out=ws[:, i * 8:(i + 1) * 8, :], in_=I["w_out"][i * 1024:(i + 1) * 1024, c0:c0 + 512].rearrange("(k p) c -> p k c", p=128)) for i in range(4)],
            bw, writes=[bw])
        for tt in range(16):
            bi = cnt % 8
            sl = cnt % 2
            cnt += 1

            def mm(e, ws=ws, tt=tt, bi=bi):
                ins = None
                for kt in range(32):
                    ins = e.matmul(PS[bi][:, :], mT[:, kt, tt * 128:(tt + 1) * 128], ws[:, kt, :], start=(kt == 0), stop=(kt == 31))
                return ins
            kb.op("tensor", mm, reads=[bw, b_mT], writes=[PSB[bi]])
            kb.dma("sync", [lambda e, sl=sl, tt=tt, c0=c0: e.dma_start(out=XS[sl], in_=I["x_own"][tt * 128:(tt + 1) * 128, c0:c0 + 512])],
                   b_xs[sl], writes=[b_xs[sl]])
            kb.op("vector", lambda e, sl=sl, bi=bi: e.tensor_tensor(out=OS[sl], in0=PS[bi][:, :], in1=XS[sl], op=ALU.add),
                  reads=[PSB[bi], b_xs[sl]], writes=[b_os[sl]])
            kb.dma("scalar", [lambda e, sl=sl, tt=tt, c0=c0: e.dma_start(out=S["x2"][tt * 128:(tt + 1) * 128, c0:c0 + 512], in_=OS[sl])],
                   b_os[sl], reads=[b_os[sl]])
    kb.barrier()


def phase_mlp(P):
    kb, ar, PS, PSB, I, S = P.kb, P.ar, P.PS, P.PSB, P.I, P.S
    ar.reset()
    hT = ar.alloc(32 * NOWN, BF16).rearrange("p (k t) -> p k t", k=32)
    R0 = ar.off
    rmsnorm_transpose(P, S["x2"], I["g_mlp"], hT, None, 16, R0)
    kb.barrier()
    ar.reset(R0)
    WS = [ar.alloc(32 * 512, BF16).rearrange("p (k c) -> p k c", k=32) for _ in range(2)]
    b_ws = [kb.buf("mws%d" % i) for i in range(2)]
    OS = [ar.alloc(NOWN, BF16) for _ in range(2)]
    b_os = [kb.buf("mos%d" % i) for i in range(2)]
    hid5 = S["hid"]
    oc = 0
    rc = 0
    RL = None
    for w0 in range(0, 4 * D, 512):
        ws, bw = WS[(w0 // 512) % 2], b_ws[(w0 // 512) % 2]
        kb.dma("gpsimd", [lambda e, ws=ws, i=i, w0=w0: e.dma_start(
            out=ws[:, i * 8:(i + 1) * 8, :], in_=I["w_up"][i * 1024:(i + 1) * 1024, w0:w0 + 512].rearrange("(k p) c -> p k c", p=128)) for i in range(4)],
            bw, writes=[bw])
        for cb in range(0, 512, 128):
            half = (cb // 128) % 2
            banks = [half * 4 + i for i in range(4)]

            def mm(e, ws=ws, cb=cb, banks=banks):
                ins = None
                for kt in range(32):
                    for ch, bi in enumerate(banks):
                        ins = e.matmul(PS[bi][:, :], ws[:, kt, cb:cb + 128], hT[:, kt, ch * 512:(ch + 1) * 512], start=(kt == 0), stop=(kt == 31))
                return ins
            kb.op("tensor", mm, reads=[bw], writes=[PSB[b] for b in banks])
            osl = oc % 2
            oc += 1
            for ch, bi in enumerate(banks):
                o_ap = OS[osl][:, ch * 512:(ch + 1) * 512]
                kb.op("scalar", lambda e, o_ap=o_ap, bi=bi: e.activation(out=o_ap, in_=PS[bi][:, :], func=AF.Relu),
                      reads=[PSB[bi]], writes=[b_os[osl]])
                kb.op("vector", lambda e, o_ap=o_ap: e.tensor_tensor(out=o_ap, in0=o_ap, in1=o_ap, op=ALU.mult),
                      reads=[b_os[osl]], writes=[b_os[osl]])
            kff = (w0 + cb) // 128
            kb.dma("sync", [lambda e, osl=osl, kff=kff: e.dma_start(
                out=hid5[:, :, kff, :].rearrange("t p q -> p t q"), in_=OS[osl].rearrange("p (t q) -> p t q", t=16))],
                b_os[osl], reads=[b_os[osl]])
    kb.barrier()
    ar.reset()
    WD = [ar.alloc(64 * 512, BF16).rearrange("p (k c) -> p k c", k=64) for _ in range(2)]
    b_wd = [kb.buf("wd%d" % i) for i in range(2)]
    HS = [ar.alloc(64 * 128, BF16).rearrange("p (k q) -> p k q", k=64) for _ in range(2)]
    b_hs = [kb.buf("hs%d" % i) for i in range(2)]
    ACC = ar.alloc(16 * 512, F32).rearrange("p (t c) -> p t c", t=16)
    b_acc = [kb.buf("dacc%d" % i) for i in range(16)]
    XS = [ar.alloc(512, F32) for _ in range(2)]
    b_xs = [kb.buf("dxs%d" % i) for i in range(2)]
    OS2 = [ar.alloc(512, F32) for _ in range(2)]
    b_os2 = [kb.buf("dos%d" % i) for i in range(2)]
    print("down arena cols used", ar.off)
    units = [(cbk, kh) for cbk in range(8) for kh in range(2)]

    def load_w(u):
        cbk, kh = units[u]
        c0 = cbk * 512
        sl = u % 2
        kb.dma("gpsimd", [lambda e, i=i: e.dma_start(
            out=WD[sl][:, i * 8:(i + 1) * 8, :],
            in_=I["w_down"][kh * 8192 + i * 1024:kh * 8192 + (i + 1) * 1024, c0:c0 + 512].rearrange("(k p) c -> p k c", p=128)) for i in range(8)],
            b_wd[sl], writes=[b_wd[sl]])
    jobs = [(u, tt) for u in range(16) for tt in range(16)]

    def load_h(j):
        u, tt = jobs[j]
        kh = units[u][1]
        sl = j % 2
        kb.dma("sync", [lambda e: e.dma_start(out=HS[sl], in_=hid5[tt][:, kh * 64:(kh + 1) * 64, :])], b_hs[sl], writes=[b_hs[sl]])
        if kh == 0:
            cbk = units[u][0]
            kb.dma("sync", [lambda e: e.dma_start(out=XS[sl], in_=S["x2"][tt * 128:(tt + 1) * 128, cbk * 512:(cbk + 1) * 512])],
                   b_xs[sl], writes=[b_xs[sl]])
    load_w(0)
    load_h(0)
    for j, (u, tt) in enumerate(jobs):
        cbk, kh = units[u]
        c0 = cbk * 512
        if tt == 0 and u + 1 < 16:
            load_w(u + 1)
        if j + 1 < len(jobs):
            load_h(j + 1)
        bi = j % 8
        sl = j % 2
        wsl = u % 2

        def mm(e, sl=sl, bi=bi, wsl=wsl):
            ins = None
            for kt in range(64):
                ins = e.matmul(PS[bi][:, :], HS[sl][:, kt, :], WD[wsl][:, kt, :], start=(kt == 0), stop=(kt == 63))
            return ins
        kb.op("tensor", mm, reads=[b_wd[wsl], b_hs[sl]], writes=[PSB[bi]])
        if kh == 0:
            kb.op("vector", lambda e, sl=sl, bi=bi, tt=tt: e.tensor_tensor(out=ACC[:, tt, :], in0=PS[bi][:, :], in1=XS[sl], op=ALU.add),
                  reads=[PSB[bi], b_xs[sl]], writes=[b_acc[tt]])
        else:
            kb.op("vector", lambda e, sl=sl, bi=bi, tt=tt: e.tensor_tensor(out=OS2[sl], in0=PS[bi][:, :], in1=ACC[:, tt, :], op=ALU.add),
                  reads=[PSB[bi], b_acc[tt]], writes=[b_os2[sl]])
            kb.dma("scalar", [lambda e, sl=sl, tt=tt, c0=c0: e.dma_start(out=S["x3"][tt * 128:(tt + 1) * 128, c0:c0 + 512], in_=OS2[sl])],
                   b_os2[sl], reads=[b_os2[sl]])
    kb.barrier()


def phase_final(P):
    kb, ar, PS, PSB, I, S = P.kb, P.ar, P.PS, P.PSB, P.I, P.S
    ar.reset()
    xs = [ar.alloc(D, F32) for _ in range(3)]
    b_xs = [kb.buf("fxs%d" % i) for i in range(3)]
    ys = [ar.alloc(D, F32) for _ in range(3)]
    b_ys = [kb.buf("fys%d" % i) for i in range(3)]
    gb = ar.alloc(D, F32)
    b_gb = kb.buf("fgb")
    st = ar.alloc(64, F32)
    b_st = [kb.buf("fst%d" % i) for i in range(3)]
    kb.dma("sync", [lambda e: e.dma_start(out=gb, in_=I["g_final"].partition_broadcast(128))], b_gb, writes=[b_gb])
    for t in range(16):
        s = t % 3
        xt, yt = xs[s], ys[s]
        ss = st[:, 2 * s:2 * s + 1]
        rs = st[:, 2 * s + 1:2 * s + 2]
        kb.dma("sync", [lambda e, xt=xt, t=t: e.dma_start(out=xt, in_=S["x3"][t * 128:(t + 1) * 128, :])], b_xs[s], writes=[b_xs[s]])
        kb.op("scalar", lambda e, xt=xt, yt=yt, ss=ss: e.activation(out=yt, in_=xt, func=AF.Square, accum_out=ss),
              reads=[b_xs[s]], writes=[b_ys[s], b_st[s]])
        kb.op("vector", lambda e, ss=ss, rs=rs: e.tensor_scalar(out=rs, in0=ss, scalar1=1.0 / D, scalar2=EPS, op0=ALU.mult, op1=ALU.add),
              reads=[b_st[s]], writes=[b_st[s]])
        kb.op("scalar", lambda e, rs=rs: e.sqrt(out=rs, in_=rs), reads=[b_st[s]], writes=[b_st[s]])
        kb.op("vector", lambda e, rs=rs: e.reciprocal(out=rs, in_=rs), reads=[b_st[s]], writes=[b_st[s]])
        kb.op("vector", lambda e, xt=xt, yt=yt, rs=rs: e.scalar_tensor_tensor(out=yt, in0=xt, scalar=rs, in1=gb, op0=ALU.mult, op1=ALU.mult),
              reads=[b_xs[s], b_st[s], b_gb], writes=[b_ys[s]])
        kb.dma("gpsimd", [lambda e, yt=yt, t=t: e.dma_start(out=P.out[t * 128:(t + 1) * 128, :], in_=yt)], b_ys[s], reads=[b_ys[s]])
    kb.barrier()


import math


def _rel_bucket_np(dist):
    n = np.maximum(dist, 0)
    nf = np.maximum(n, 16).astype(np.float32)
    large = 16 + (np.log(nf / np.float32(16)) / np.float32(math.log(128 / 16)) * np.float32(16)).astype(np.int32)
    large = np.minimum(large, 31)
    return np.where(n < 16, n, large).astype(np.int64)


def nsa_consts(rel_table, half):
    tab_ext = np.concatenate([np.asarray(rel_table, np.float32), np.full((1, 16), NEG, np.float32)], 0)
    i = np.arange(128)[:, None]
    cc = np.arange(376)[None, :]
    dist = i - 16 * (cc - 248) - 31
    idx = np.where(dist >= 0, _rel_bucket_np(dist), 32)
    bfull = np.ascontiguousarray(tab_ext[idx].transpose(2, 0, 1))
    key = np.arange(128)[:, None]
    q = np.arange(128)[None, :]
    d0 = q - key
    w0 = np.ascontiguousarray(tab_ext[np.where(d0 >= 0, _rel_bucket_np(d0), 32)].transpose(2, 0, 1))
    w1 = np.ascontiguousarray(tab_ext[_rel_bucket_np(128 + q - key)].transpose(2, 0, 1))
    w4 = np.ascontiguousarray(tab_ext[np.where(key > q, 31, 32)].transpose(2, 0, 1))
    ekt = np.zeros((64, 32, 128), np.float32)
    for kt in range(32):
        ekt[2 * kt, kt, 0:64] = 1.0
        ekt[2 * kt + 1, kt, 64:128] = 1.0
    ii = np.arange(128)[:, None, None]
    qi = np.arange(16)[None, :, None]
    jv = np.arange(64)[None, None, :]
    t = half * 2048 + 128 * qi + ii
    j = jv - 32 * (1 - half)
    cur = t // 64
    exists = j >= 0
    forced = exists & ((j == 0) | ((j <= cur) & (j > cur - 2)))
    reachable = exists & (j * 64 <= t)
    reach01 = (reachable & ~forced).astype(np.float32)
    addc = np.where(forced, 1e4, np.where(reachable, 0.0, -1e4)).astype(np.float32)
    killc = np.zeros((1, 256), np.float32)
    killcol = np.zeros((128, 1), np.float32)
    if half == 0:
        killc[0, :128] = NEG
        killcol[:] = NEG
    return {"bfull": bfull, "w0": w0, "w1": w1, "w4": w4, "ekt": ekt.reshape(64, 32 * 128), "reach01": np.ascontiguousarray(reach01),
            "addc": np.ascontiguousarray(addc), "killc": killc, "killcol": killcol}


def _m2_const():
    s = np.arange(128)[:, None]
    c = np.arange(128)[None, :]
    return ((s // 64 == c // 64) & (s <= c)).astype(np.float32)


def _core_inputs(inputs, x, b, half, shared):
    m = dict(shared)
    xo = x[b, half * NOWN:(half + 1) * NOWN]
    m["x_own"] = np.ascontiguousarray(xo)
    m["x_pre"] = np.ascontiguousarray(x[b, 0:NOWN]) if half == 1 else np.zeros((NOWN, D), np.float32)
    m.update(shared["_nsa"][half])
    del m["_nsa"]
    return m


def make_in_maps(inputs, cores):
    f = lambda k: np.ascontiguousarray(np.asarray(inputs[k], np.float32))
    x = f("x")
    shared = {
        "g_mix": f("g_mix_norm")[0], "w_in": f("w_in")[0], "ident": np.eye(128, dtype=np.float32),
        "w_alpha2": f("w_alpha2")[0], "b_alpha": f("b_alpha")[0], "gla_norm_g": f("gla_norm_g")[0], "m2": _m2_const(),
        "phi_k_w1": f("phi_k_w1")[0], "phi_v_w1": f("phi_v_w1")[0], "phi_k_w2": f("phi_k_w2")[0], "phi_v_w2": f("phi_v_w2")[0],
        "cmp_pos_k": f("cmp_pos_k")[0], "cmp_pos_v": f("cmp_pos_v")[0], "rel_table": f("rel_bias_table"),
        "w_gla_proj": f("w_gla_proj")[0], "w_nsa_proj": f("w_nsa_proj")[0], "w_out": f("w_out")[0], "g_mlp": f("g_mlp_norm")[0],
        "w_up": f("w_up")[0], "w_down": f("w_down")[0], "g_final": f("g_final_norm"),
    }
    shared["_nsa"] = [nsa_consts(shared["rel_table"], h) for h in range(2)]
    return [_core_inputs(inputs, x, c // 2, c % 2, shared) for c in cores]


def kernel(**inputs):
    nc = build()
    cores = list(range(8))
    in_maps = make_in_maps(inputs, cores)
    res = run_bass_kernel_spmd(nc, in_maps, core_ids=cores)
    out = np.empty((4, 4096, D), np.float32)
    for c in cores:
        out[c // 2, (c % 2) * NOWN:(c % 2 + 1) * NOWN] = np.asarray(res.results[c]["out"], np.float32)
    return out
```
